# Optimizing a Trainium2 kernel written in Bass

```python
import math
import jax, jax.numpy as jnp
from jax import lax
import numpy as np

D_MODEL = 1024
BATCH = 8
SEQ = 4096
DEPTH = 4
DEC_BATCH = 16
DEC_SEQ = 64
PAST_LEN = 2048

CHUNK = 64
N_META = 16
D_RNN = D_MODEL
N_RNN_BLOCKS = 8
RNN_BLOCK = D_RNN // N_RNN_BLOCKS
CONV_W = 4
LRU_C = 8.0
N_HEADS = 8
HEAD_DIM = D_MODEL // (2 * N_HEADS)
V_DIM = 2 * HEAD_DIM
QK_W = N_HEADS * 2 * HEAD_DIM
D_ATT = N_HEADS * V_DIM
IN_COLS = 2 * D_RNN + 2 * QK_W + 2 * D_ATT + 2 * D_MODEL
Q_BLOCK = 128
ROPE_THETA = 10000.0
EPS = 1e-6

kernel_name = "hawk_diffattn_gated_stream_step"

F32 = jnp.float32


def rms_norm(x, w):
    xf = x.astype(F32)
    y = xf * lax.rsqrt(jnp.mean(xf * xf, axis=-1, keepdims=True) + EPS)
    return (y * w.astype(F32)).astype(x.dtype)


def rope(x, pos):
    half = HEAD_DIM // 2
    inv = 1.0 / (ROPE_THETA ** (jnp.arange(half, dtype=F32) / half))
    ang = pos.astype(F32)[:, None] * inv[None, :]
    cos = jnp.cos(ang)[None, :, None, None, :]
    sin = jnp.sin(ang)[None, :, None, None, :]
    xf = x.astype(F32)
    x1, x2 = xf[..., :half], xf[..., half:]
    return jnp.concatenate([x1 * cos - x2 * sin, x2 * cos + x1 * sin], axis=-1).astype(x.dtype)


def in_projection(h, norm_w, w_in):
    z = rms_norm(h, norm_w) @ w_in
    parts = []
    off = 0
    for width in (D_RNN, D_RNN, QK_W, QK_W, D_ATT, D_ATT, D_MODEL, D_MODEL):
        parts.append(z[..., off:off + width])
        off += width
    return parts


def causal_conv(x, prefix, conv_w, conv_b):
    T = x.shape[1]
    xp = jnp.concatenate([prefix.astype(x.dtype), x], axis=1)
    y = conv_b + xp[:, 0:T] * conv_w[0]
    for k in range(1, CONV_W):
        y = y + xp[:, k:k + T] * conv_w[k]
    return y, xp[:, -(CONV_W - 1):]


def block_diag(x, w, b):
    B, T, _ = x.shape
    xb = x.reshape(B, T, N_RNN_BLOCKS, RNN_BLOCK)
    return jnp.einsum('btni,nij->btnj', xb, w).reshape(B, T, D_RNN) + b


def rglru_branch(xr, zr, prefix, h0, conv_w, conv_b, w_rg, b_rg, w_ig, b_ig, lru_lambda):
    xc, new_prefix = causal_conv(xr, prefix, conv_w, conv_b)
    r = jax.nn.sigmoid(block_diag(xc, w_rg, b_rg).astype(F32))
    i = jax.nn.sigmoid(block_diag(xc, w_ig, b_ig).astype(F32))
    log_a = -LRU_C * r * jax.nn.softplus(-lru_lambda.astype(F32))
    a = jnp.exp(log_a)
    mult = jnp.sqrt(-jnp.expm1(2.0 * log_a))
    b = mult * i * xc.astype(F32)
    b = b.at[:, 0].add(a[:, 0] * h0.astype(F32))

    def combine(lhs, rhs):
        a1, b1 = lhs
        a2, b2 = rhs
        return a1 * a2, a2 * b1 + b2

    _, hs = lax.associative_scan(combine, (a, b), axis=1)
    out = hs.astype(xr.dtype) * jax.nn.silu(zr)
    return out, new_prefix, hs[:, -1].astype(xr.dtype)


def diff_core(q, k, v, lam, mask):
    s = jnp.einsum('bqhcd,bkhcd->bhcqk', q, k).astype(F32) * (HEAD_DIM ** -0.5)
    if mask is not None:
        s = jnp.where(mask, s, -jnp.inf)
    p = jax.nn.softmax(s, axis=-1)
    attn = p[:, :, 0] - lam * p[:, :, 1]
    return jnp.einsum('bhqk,bkhe->bqhe', attn.astype(v.dtype), v)


def prompt_attention(q, k, v, lam):
    B, T = q.shape[:2]
    S = T - N_META
    pos = jnp.arange(T)
    chunk_id = jnp.where(pos < N_META, 0, (pos - N_META) // CHUNK + 1)
    o_meta = diff_core(q[:, :N_META], k[:, :N_META], v[:, :N_META], lam, None)
    nb = S // Q_BLOCK
    qf = q[:, N_META:].reshape(B, nb, Q_BLOCK, N_HEADS, 2, HEAD_DIM).swapaxes(0, 1)
    qc = chunk_id[N_META:].reshape(nb, Q_BLOCK)

    def one_block(args):
        qb, cb = args
        mask = chunk_id[None, :] <= cb[:, None]
        return diff_core(qb, k, v, lam, mask)

    o_f = lax.map(one_block, (qf, qc))
    o_f = o_f.swapaxes(0, 1).reshape(B, S, N_HEADS, V_DIM)
    return jnp.concatenate([o_meta, o_f], axis=1)


def diff_attn_finish(o, za, subln_w, lam_init):
    o = rms_norm(o, subln_w) * (1.0 - lam_init)
    B, T = o.shape[:2]
    return o.reshape(B, T, D_ATT) * jax.nn.silu(za)


def merge_out(o_rnn, o_att, g_rnn, g_att, w_proj_rnn, w_proj_att, w_out):
    m = jax.nn.sigmoid(g_rnn) * (o_rnn @ w_proj_rnn) + jax.nn.sigmoid(g_att) * (o_att @ w_proj_att)
    return m @ w_out


def setup_inputs(seed: int = 0) -> dict:
    key = jax.random.key(seed)
    ks = jax.random.split(key, 32)

    def nrm(k, shape, scale):
        return jax.random.normal(k, shape, F32) * scale

    u = jax.random.uniform(ks[12], (DEPTH, D_RNN), F32, 0.9, 0.999)
    return {
        "x_prompt": nrm(ks[0], (BATCH, SEQ, D_MODEL), 1.0),
        "x_sample": nrm(ks[1], (DEC_BATCH, DEC_SEQ, D_MODEL), 1.0),
        "cache_k": nrm(ks[2], (DEPTH, DEC_BATCH, PAST_LEN, N_HEADS, 2, HEAD_DIM), 1.0),
        "cache_v": nrm(ks[3], (DEPTH, DEC_BATCH, PAST_LEN, N_HEADS, V_DIM), 1.0),
        "state_conv": nrm(ks[4], (DEPTH, DEC_BATCH, CONV_W - 1, D_RNN), 1.0),
        "state_rnn": nrm(ks[5], (DEPTH, DEC_BATCH, D_RNN), 0.5),
        "meta_tokens": nrm(ks[6], (N_META, D_MODEL), 1.0),
        "norm_w": 1.0 + nrm(ks[7], (DEPTH, D_MODEL), 0.01),
        "w_in": nrm(ks[8], (DEPTH, D_MODEL, IN_COLS), D_MODEL ** -0.5),
        "conv_w": nrm(ks[9], (DEPTH, CONV_W, D_RNN), CONV_W ** -0.5),
        "conv_b": nrm(ks[10], (DEPTH, D_RNN), 0.01),
        "w_rg": nrm(ks[11], (DEPTH, N_RNN_BLOCKS, RNN_BLOCK, RNN_BLOCK), RNN_BLOCK ** -0.5),
        "b_rg": nrm(ks[13], (DEPTH, D_RNN), 0.01),
        "w_ig": nrm(ks[14], (DEPTH, N_RNN_BLOCKS, RNN_BLOCK, RNN_BLOCK), RNN_BLOCK ** -0.5),
        "b_ig": nrm(ks[15], (DEPTH, D_RNN), 0.01),
        "lru_lambda": jnp.log(u) - jnp.log1p(-u),
        "lambda_q1": nrm(ks[16], (DEPTH, HEAD_DIM), 0.1),
        "lambda_k1": nrm(ks[17], (DEPTH, HEAD_DIM), 0.1),
        "lambda_q2": nrm(ks[18], (DEPTH, HEAD_DIM), 0.1),
        "lambda_k2": nrm(ks[19], (DEPTH, HEAD_DIM), 0.1),
        "subln_w": 1.0 + nrm(ks[20], (DEPTH, V_DIM), 0.01),
        "w_proj_rnn": nrm(ks[21], (DEPTH, D_RNN, D_MODEL), D_RNN ** -0.5),
        "w_proj_att": nrm(ks[22], (DEPTH, D_ATT, D_MODEL), D_ATT ** -0.5),
        "w_out": nrm(ks[23], (DEPTH, D_MODEL, D_MODEL), D_MODEL ** -0.5),
        "final_norm_w": 1.0 + nrm(ks[24], (D_MODEL,), 0.01),
    }


def reference(x_prompt, x_sample, cache_k, cache_v, state_conv, state_rnn, meta_tokens,
              norm_w, w_in, conv_w, conv_b, w_rg, b_rg, w_ig, b_ig, lru_lambda,
              lambda_q1, lambda_k1, lambda_q2, lambda_k2, subln_w,
              w_proj_rnn, w_proj_att, w_out, final_norm_w):
    B = x_prompt.shape[0]
    DB, S = x_sample.shape[0], x_sample.shape[1]
    dt = x_prompt.dtype
    hp = jnp.concatenate(
        [jnp.broadcast_to(meta_tokens[None].astype(dt), (B, N_META, D_MODEL)), x_prompt], axis=1)
    hs = x_sample
    Tp = hp.shape[1]
    past = cache_k.shape[2]
    pos_p = jnp.arange(Tp)
    pos_s = N_META + past + jnp.arange(S)

    kp_l, vp_l, cp_l, rp_l = [], [], [], []
    ks_l, vs_l, cs_l, rs_l = [], [], [], []
    for l in range(DEPTH):
        lam_init = 0.8 - 0.6 * math.exp(-0.3 * l)
        lam = (jnp.exp(jnp.sum(lambda_q1[l].astype(F32) * lambda_k1[l].astype(F32)))
               - jnp.exp(jnp.sum(lambda_q2[l].astype(F32) * lambda_k2[l].astype(F32))) + lam_init)
        rnn_args = (conv_w[l], conv_b[l], w_rg[l], b_rg[l], w_ig[l], b_ig[l], lru_lambda[l])

        xr, zr, q, k, v, za, g_r, g_a = in_projection(hp, norm_w[l], w_in[l])
        q = rope(q.reshape(B, Tp, N_HEADS, 2, HEAD_DIM), pos_p)
        k = rope(k.reshape(B, Tp, N_HEADS, 2, HEAD_DIM), pos_p)
        v = v.reshape(B, Tp, N_HEADS, V_DIM)
        o_r, cp, rp = rglru_branch(xr, zr, jnp.zeros((B, CONV_W - 1, D_RNN), dt),
                                   jnp.zeros((B, D_RNN), dt), *rnn_args)
        o_a = diff_attn_finish(prompt_attention(q, k, v, lam), za, subln_w[l], lam_init)
        hp = hp + merge_out(o_r, o_a, g_r, g_a, w_proj_rnn[l], w_proj_att[l], w_out[l])
        kp_l.append(k)
        vp_l.append(v)
        cp_l.append(cp)
        rp_l.append(rp)

        xr, zr, q, k, v, za, g_r, g_a = in_projection(hs, norm_w[l], w_in[l])
        q = rope(q.reshape(DB, S, N_HEADS, 2, HEAD_DIM), pos_s)
        k = rope(k.reshape(DB, S, N_HEADS, 2, HEAD_DIM), pos_s)
        v = v.reshape(DB, S, N_HEADS, V_DIM)
        o_r, cs, rs = rglru_branch(xr, zr, state_conv[l], state_rnn[l], *rnn_args)
        k_all = jnp.concatenate([cache_k[l].astype(k.dtype), k], axis=1)
        v_all = jnp.concatenate([cache_v[l].astype(v.dtype), v], axis=1)
        o_a = diff_attn_finish(diff_core(q, k_all, v_all, lam, None), za, subln_w[l], lam_init)
        hs = hs + merge_out(o_r, o_a, g_r, g_a, w_proj_rnn[l], w_proj_att[l], w_out[l])
        ks_l.append(k)
        vs_l.append(v)
        cs_l.append(cs)
        rs_l.append(rs)

    y_prompt = rms_norm(hp[:, N_META:], final_norm_w)
    y_sample = rms_norm(hs, final_norm_w)
    return (y_prompt, y_sample,
            jnp.stack(kp_l), jnp.stack(vp_l), jnp.stack(cp_l), jnp.stack(rp_l),
            jnp.stack(ks_l), jnp.stack(vs_l), jnp.stack(cs_l), jnp.stack(rs_l))
```

```python
import contextlib
import math
import numpy as np
import concourse.bass as bass
import concourse.mybir as mybir
from concourse.bass_utils import run_bass_kernel_spmd

F32 = mybir.dt.float32
BF16 = mybir.dt.bfloat16
AF = mybir.ActivationFunctionType
ALU = mybir.AluOpType
AX = mybir.AxisListType

D = 1024
DEPTH = 4
SEQ = 4096
NMETA = 16
TP = SEQ + NMETA
DSEQ = 64
PAST = 2048
NH = 8
EPS = 1e-6
TT = 256
NW = 3
NTILE = SEQ // TT
KCOLS_P = 33 * 128
KCOLS_S = 17 * 128
SEM_ROT = 30000
DBG = False
NOCACHE = False


class _Chan:
    def __init__(self, nc, name):
        self.nc, self.name, self.sems, self.cur = nc, name, [], 0

    def bump(self, units):
        if not self.sems or self.cur + units > SEM_ROT:
            self.sems.append(self.nc.alloc_semaphore(f"{self.name}_{len(self.sems)}"))
            self.cur = 0
        self.cur += units
        return (self.sems[-1], self.cur)


class Sched:
    ENG = ("pe", "act", "dve", "pool", "sp")

    def __init__(self, nc):
        self.nc, self.ops, self.state = nc, [], {}

    limit = None
    bases = set()

    def _split(self, b):
        if isinstance(b, tuple) and isinstance(b[0], str) and b[0] in self.bases:
            return b[0], b
        return b, None

    def add(self, eng, fn, reads=(), writes=(), dma=None, grp=False):
        if self.limit is not None and len(self.ops) >= self.limit:
            return -1
        deps, raw = set(), set()
        st = self.state
        for b in reads:
            base, part = self._split(b)
            e = st.setdefault(base, [None, [], {}])
            ws = [e[0]]
            if part is None:
                ws += [pv[0] for pv in e[2].values()]
            elif part in e[2]:
                ws.append(e[2][part][0])
            for w in ws:
                if w is not None:
                    deps.add(w)
                    raw.add(w)
        for b in writes:
            base, part = self._split(b)
            e = st.setdefault(base, [None, [], {}])
            if e[0] is not None:
                deps.add(e[0])
            deps.update(e[1])
            if part is None:
                for pv in e[2].values():
                    deps.add(pv[0])
                    deps.update(pv[1])
            elif part in e[2]:
                deps.add(e[2][part][0])
                deps.update(e[2][part][1])
        i = len(self.ops)
        deps.discard(None)
        if dma is not None:
            dma = (eng, dma)
        self.ops.append(dict(eng=eng, fn=fn, deps=deps, raw=raw, dma=dma, mark=False, grp=grp))
        for b in reads:
            base, part = self._split(b)
            e = st[base]
            if part is None:
                e[1].append(i)
            else:
                e[2].setdefault(part, [None, []])[1].append(i)
        for b in writes:
            base, part = self._split(b)
            e = st[base]
            if part is None:
                e[0], e[1], e[2] = i, [], {}
            else:
                e[2][part] = [i, []]
        return i

    def finalize(self):
        nc, ops = self.nc, self.ops
        for op in ops:
            need = []
            for d in op["deps"]:
                p = ops[d]
                if p["dma"] is not None or p["eng"] != op["eng"]:
                    need.append(d)
                elif p["eng"] != "pe" and d in op["raw"]:
                    need.append(d)
            op["need"] = need
            for d in need:
                ops[d]["mark"] = True
        chans = {e: _Chan(nc, "c_" + e) for e in self.ENG}
        dchan = {}
        for op in ops:
            if op["dma"] is not None:
                key = op["dma"]
                if key not in dchan:
                    dchan[key] = _Chan(nc, f"dma{len(dchan)}")
                op["inc"] = dchan[key].bump(16)
            elif op["mark"]:
                op["inc"] = chans[op["eng"]].bump(1)
            else:
                op["inc"] = None
        gfinal = {}
        for op in ops:
            if op["grp"]:
                gfinal[op["dma"]] = op["inc"]
        seen = {e: {} for e in self.ENG}
        for op in ops:
            w = {}
            for d in op["need"]:
                sem, val = gfinal[ops[d]["dma"]] if ops[d]["grp"] else ops[d]["inc"]
                k = id(sem)
                if seen[op["eng"]].get(k, 0) >= val:
                    continue
                if k not in w or w[k][1] < val:
                    w[k] = (sem, val)
            for k, sv in w.items():
                seen[op["eng"]][k] = sv[1]
            op["waits"] = list(w.values())
        self.dchan = dchan
        self.n_sems = sum(len(c.sems) for c in chans.values()) + sum(len(c.sems) for c in dchan.values())

    def emit(self, block):
        ops = self.ops

        def run(engname, e):
            for op in ops:
                if op["eng"] != engname:
                    continue
                for sem, val in op["waits"]:
                    e.wait_ge(sem, val)
                ins = op["fn"](e)
                if op["inc"] is not None:
                    ins.then_inc(op["inc"][0], 16 if op["dma"] is not None else 1)

        @block.tensor
        def _(e):
            run("pe", e)

        @block.scalar
        def _(e):
            run("act", e)

        @block.vector
        def _(e):
            run("dve", e)

        @block.gpsimd
        def _(e):
            run("pool", e)

        @block.sync
        def _(e):
            run("sp", e)
            for c in self.dchan.values():
                e.wait_ge(c.sems[-1], c.cur)


class Rot:
    def __init__(self, es, nc, name, shape, dt, n, psum=False):
        mk = nc.psum_tensor if psum else nc.sbuf_tensor
        self.t = [es.enter_context(mk(f"{name}{i}", shape, dt)) for i in range(n)]
        self.ids = [f"{name}{i}" for i in range(n)]
        Sched.bases.update(self.ids)
        self.k = 0

    def next(self):
        i = self.k % len(self.t)
        self.k += 1
        return self.t[i], self.ids[i]


def build_program(depth=DEPTH, ntile=NTILE, small=True):
    nc = bass.Bass("TRN2", target_bir_lowering=False)

    def din(name, shape):
        return nc.dram_tensor(name, shape, F32, kind="ExternalInput").ap()

    def dout(name, shape):
        return nc.dram_tensor(name, shape, F32, kind="ExternalOutput").ap()

    def dscr(name, shape, dt):
        return nc.dram_tensor(name, shape, dt).ap()

    x_p = din("x_p", [SEQ, D])
    x_s = din("x_s", [2, DSEQ, D])
    cache_k = din("cache_k", [DEPTH, 2, PAST, D])
    cache_v = din("cache_v", [DEPTH, 2, PAST, D])
    st_conv = din("st_conv", [DEPTH, 2, 3, D])
    st_rnn = din("st_rnn", [DEPTH, 2, D])
    meta = din("meta", [NMETA, D])
    norm_w = din("norm_w", [DEPTH, D])
    w_in = din("w_in", [DEPTH, D, 8 * D])
    conv_w = din("conv_w", [DEPTH, 4, D])
    conv_b = din("conv_b", [DEPTH, D])
    w_rg = din("w_rg", [DEPTH, 8, 128, 128])
    b_rg = din("b_rg", [DEPTH, D])
    w_ig = din("w_ig", [DEPTH, 8, 128, 128])
    b_ig = din("b_ig", [DEPTH, D])
    lru_l = din("lru_l", [DEPTH, D])
    lam_in = [din(n, [DEPTH, 64]) for n in ("lq1", "lk1", "lq2", "lk2")]
    subln = din("subln", [DEPTH, 128])
    w_pr = din("w_pr", [DEPTH, D, D])
    w_pa = din("w_pa", [DEPTH, D, D])
    w_out = din("w_out", [DEPTH, D, D])
    fnw = din("fnw", [D])
    rope = din("rope", [TP + DSEQ, 2, 32])

    y_p = dout("y_p", [SEQ, D])
    y_s = dout("y_s", [2, DSEQ, D])
    nk_p = dout("nk_p", [DEPTH, TP, D])
    nv_p = dout("nv_p", [DEPTH, TP, D])
    nc_p = dout("nc_p", [DEPTH, 3, D])
    nr_p = dout("nr_p", [DEPTH, D])
    nk_s = dout("nk_s", [DEPTH, 2, DSEQ, D])
    nv_s = dout("nv_s", [DEPTH, 2, DSEQ, D])
    nc_s = dout("nc_s", [DEPTH, 2, 3, D])
    nr_s = dout("nr_s", [DEPTH, 2, D])

    wsc_in = dscr("wsc_in", [DEPTH, D, 8 * D], BF16)
    wsc_p = dscr("wsc_p", [DEPTH, 3, D, D], BF16)
    hb_p = dscr("hb_p", [TP, D], F32)
    hb_s = dscr("hb_s", [2, DSEQ, D], F32)
    kT_p = dscr("kT_p", [NH, 128, KCOLS_P], BF16)
    v_p = dscr("v_p", [NH, 128, 33, 130], BF16)
    kT_s = dscr("kT_s", [2, NH, 128, KCOLS_S], BF16)
    v_s = dscr("v_s", [2, NH, 128, 17, 130], BF16)

    S = Sched(nc)
    TW = TT
    with contextlib.ExitStack() as es:
        def T(name, shape, dt):
            return es.enter_context(nc.sbuf_tensor(name, shape, dt))

        identf = T("identf", [128, 128], F32)
        ident = T("ident", [128, 128], BF16)
        zeros = T("zeros", [128, 1040], BF16)
        nwt = T("nwt", [128, DEPTH, 8], F32)
        cwt = T("cwt", [128, DEPTH, 4, 8], F32)
        cbt = T("cbt", [128, DEPTH, 8], F32)
        nbrg = T("nbrg", [128, DEPTH, 8], F32)
        nbig = T("nbig", [128, DEPTH, 8], F32)
        cch = T("cch", [128, DEPTH, 8], F32)
        wg = T("wg", [128, 2, 8, 128], BF16)
        subw = T("subw", [128, DEPTH, 128], F32)
        sub_t = T("sub_t", [128, 128], F32)
        lamt = T("lamt", [128, 4, 64], F32)
        lams = T("lams", [128, 2], F32)
        neglam = T("neglam", [128, DEPTH], F32)
        fnwt = T("fnwt", [128, D], F32)
        hcar = {k: T("hcar_" + k, [128, 8], F32) for k in ("p", "s0", "s1")}
        xhalo = T("xhalo_p", [128, 8, 3], F32)

        S.add("pool", lambda e: e.memset(identf[:], 1.0), writes=["identf"])
        S.add("pool", lambda e: e.affine_select(out=identf[:], in_=identf[:], pattern=[[-1, 128]],
                                                compare_op=ALU.is_equal, fill=0.0, base=0, channel_multiplier=1),
              reads=["identf"], writes=["identf"])
        S.add("dve", lambda e: e.tensor_copy(out=ident[:], in_=identf[:]), reads=["identf"], writes=["ident"])
        S.add("pool", lambda e: e.memset(zeros[:], 0.0), writes=["zeros"])

        def sdma(out, in_, w, r=()):
            S.add("sp", lambda e: e.dma_start(out=out, in_=in_, allow_slow_non_contiguous=True),
                  reads=list(r), writes=[w], dma=w)

        sdma(nwt[:], norm_w.rearrange("l (j p) -> p l j", p=128), "nwt")
        sdma(cwt[:], conv_w.rearrange("l k (j p) -> p l k j", p=128), "cwt")
        sdma(cbt[:], conv_b.rearrange("l (j p) -> p l j", p=128), "cbt")
        sdma(nbrg[:], b_rg.rearrange("l (j p) -> p l j", p=128), "nbrg")
        sdma(nbig[:], b_ig.rearrange("l (j p) -> p l j", p=128), "nbig")
        sdma(cch[:], lru_l.rearrange("l (j p) -> p l j", p=128), "cch")
        sdma(fnwt[:], fnw.partition_broadcast(128), "fnwt")
        S.add("dve", lambda e: e.tensor_scalar(out=nbrg[:], in0=nbrg[:], scalar1=-1.0, scalar2=None, op0=ALU.mult),
              reads=["nbrg"], writes=["nbrg"])
        S.add("dve", lambda e: e.tensor_scalar(out=nbig[:], in0=nbig[:], scalar1=-1.0, scalar2=None, op0=ALU.mult),
              reads=["nbig"], writes=["nbig"])
        S.add("act", lambda e: e.activation(out=cch[:], in_=cch[:], func=AF.Exp, scale=-1.0), reads=["cch"], writes=["cch"])
        S.add("act", lambda e: e.activation(out=cch[:], in_=cch[:], func=AF.Ln, bias=1.0), reads=["cch"], writes=["cch"])
        S.add("dve", lambda e: e.tensor_scalar(out=cch[:], in0=cch[:], scalar1=-8.0, scalar2=None, op0=ALU.mult),
              reads=["cch"], writes=["cch"])
        for l in range(DEPTH):
            lam_init = 0.8 - 0.6 * math.exp(-0.3 * l)
            sdma(sub_t[:], subln[l].partition_broadcast(128), "sub_t")
            S.add("dve", lambda e, l=l, li=lam_init: e.tensor_scalar(
                out=subw[:, l, :], in0=sub_t[:],
                scalar1=1.0 - li, scalar2=None, op0=ALU.mult), reads=["sub_t"], writes=[("subw", l)])
            for i4 in range(4):
                sdma(lamt[:, i4, :], lam_in[i4][l].partition_broadcast(128), ("lamt", i4))
            for pr in range(2):
                S.add("dve", lambda e, pr=pr: e.tensor_tensor(out=lamt[:, 2 * pr, :], in0=lamt[:, 2 * pr, :],
                                                            in1=lamt[:, 2 * pr + 1, :], op=ALU.mult),
                      reads=[("lamt", 2 * pr), ("lamt", 2 * pr + 1)], writes=[("lamt", 2 * pr)])
                S.add("dve", lambda e, pr=pr: e.tensor_reduce(out=lams[:, pr:pr + 1], in_=lamt[:, 2 * pr, :],
                                                            axis=AX.X, op=ALU.add),
                      reads=[("lamt", 2 * pr)], writes=[("lams", pr)])
            S.add("act", lambda e: e.activation(out=lams[:], in_=lams[:], func=AF.Exp),
                  reads=[("lams", 0), ("lams", 1)], writes=[("lams", 0), ("lams", 1)])
            S.add("dve", lambda e, l=l: e.tensor_tensor(out=neglam[:, l:l + 1], in0=lams[:, 1:2], in1=lams[:, 0:1],
                                                      op=ALU.subtract),
                  reads=[("lams", 0), ("lams", 1)], writes=[("neglam", l)])
            S.add("dve", lambda e, l=l, li=lam_init: e.tensor_scalar(out=neglam[:, l:l + 1], in0=neglam[:, l:l + 1],
                                                                   scalar1=-li, scalar2=None, op0=ALU.add),
                  reads=[("neglam", l)], writes=[("neglam", l)])

        def load_gate_w(l):
            for gi, wsrc in enumerate((w_rg, w_ig)):
                S.add("pool", lambda e, l=l, gi=gi, wsrc=wsrc: e.dma_start(
                    out=wg[:, gi, :, :], in_=wsrc[l].rearrange("n i j -> i n j")),
                    writes=[("wg", gi)], dma=("wg", gi))

        def convert_weights(l):
            for q in range(8):
                S.add("pool", lambda e, l=l, q=q: e.dma_start(out=wsc_in[l][:, q * D:(q + 1) * D],
                                                            in_=w_in[l][:, q * D:(q + 1) * D]),
                      writes=[("wsc", l, q)], dma=("wsc", l), grp=True)
            for i3, wsrc in enumerate((w_pr, w_pa, w_out)):
                S.add("pool", lambda e, l=l, i3=i3, wsrc=wsrc: e.dma_start(out=wsc_p[l, i3], in_=wsrc[l]),
                      writes=[("wscp", l, i3)], dma=("wsc", l), grp=True)

        convert_weights(0)
        S.add("sp", lambda e: e.dma_start(out=kT_p.rearrange("h p t -> p h t")[:, :, 0:128],
                                          in_=zeros[:, 0:1024].rearrange("p (h x) -> p h x", h=8)),
              reads=["zeros"], writes=["kTp"], dma="z0")
        S.add("sp", lambda e: e.dma_start(out=v_p.rearrange("h p k x -> p h k x")[:, :, 0, :],
                                          in_=zeros[:, 0:1040].rearrange("p (h x) -> p h x", h=8)),
              reads=["zeros"], writes=["vp"], dma="z1")
        for s_ in range(2):
            S.add("sp", lambda e, s_=s_: e.dma_start(out=kT_s[s_].rearrange("h p t -> p h t")[:, :, 2048:2176],
                                                   in_=zeros[:, 0:1024].rearrange("p (h x) -> p h x", h=8)),
                  reads=["zeros"], writes=[("kTs", s_)], dma=("z2", s_))
            S.add("sp", lambda e, s_=s_: e.dma_start(out=v_s[s_].rearrange("h p k x -> p h k x")[:, :, 16, :],
                                                   in_=zeros[:, 0:1040].rearrange("p (h x) -> p h x", h=8)),
                  reads=["zeros"], writes=[("vs", s_)], dma=("z3", s_))

        wpool = Rot(es, nc, "wch", [128, 8, 512], BF16, NW)
        hres = Rot(es, nc, "hres", [128, D], F32, 3)
        xnb = Rot(es, nc, "xnb", [128, D], BF16, 2)
        st1 = Rot(es, nc, "st1", [128, 2], F32, 4)
        xnT = Rot(es, nc, "xnT", [128, 8, TW], BF16, 1)
        rot = Rot(es, nc, "rot", [128, D], F32, 2)
        rotb = Rot(es, nc, "rotb", [128, 2048], BF16, 2)
        rtmp = Rot(es, nc, "rtmp", [128, 4, 256], F32, 2)
        ropet = Rot(es, nc, "ropet", [128, 2, 32], F32, 3)
        qT = Rot(es, nc, "qT", [128, 8, TW], BF16, 1)
        kTst = Rot(es, nc, "kTst", [128, 8, 256], BF16, 2)
        vst = Rot(es, nc, "vst", [128, 8, 2, 130], BF16, 2)
        vf = Rot(es, nc, "vf", [128, D], F32, 3)
        gatea = Rot(es, nc, "gatea", [128, D], BF16, 2)
        etm = Rot(es, nc, "etm", [128, 512], F32, 2)
        XW = TW + 9
        xr = Rot(es, nc, "xr", [128, 8, XW], F32, 1)
        szr = Rot(es, nc, "szr", [128, 8, TW], BF16, 1)
        sgr = Rot(es, nc, "sgr", [128, 8, TW], BF16, 1)
        sga = Rot(es, nc, "sga", [128, 8, TW], BF16, 1)
        rn = {k: Rot(es, nc, "rn_" + k, [128, TW], F32, 2) for k in ("xc", "r", "i", "a", "m", "hs")}
        xcb = Rot(es, nc, "xcb", [128, TW], BF16, 2)
        orT = Rot(es, nc, "orT", [128, 8, TW], BF16, 1)
        oaT = Rot(es, nc, "oaT", [128, 8, TW], BF16, 1)
        mT = Rot(es, nc, "mT", [128, 8, TW], BF16, 1)
        kbuf = Rot(es, nc, "kbuf", [128, 17 * 128], BF16, 2)
        vbuf = Rot(es, nc, "vbuf", [128, 17, 130], BF16, 2)
        ptb = Rot(es, nc, "ptb", [128, 2, TW], BF16, 3)
        obuf = Rot(es, nc, "obuf", [128, 8, 128], F32, 2)
        oab = Rot(es, nc, "oab", [128, D], BF16, 1)
        arec = Rot(es, nc, "arec", [128, 4], F32, 4)
        atmp = Rot(es, nc, "atmp", [128, 128], F32, 2)
        st8 = Rot(es, nc, "st8", [128, 8], F32, 2)
        mtmp = Rot(es, nc, "mtmp", [128, 2, TW], F32, 2)
        yst = vf
        cst = vf
        cstb = xnb
        psc = Rot(es, nc, "psc", [128, 2, 512], F32, 2, psum=True)
        pacc_t = es.enter_context(nc.psum_tensor("pacc", [128, 2, 512], F32))
        pg = Rot(es, nc, "pg", [128, 512], F32, 2, psum=True)
        print("sbuf bytes remaining", nc.sbuf_bytes_remaining)

        for v_t, v_id in zip(vst.t, vst.ids):
            S.add("pool", lambda e, v_t=v_t: e.memset(v_t[:, :, :, 128:129], 1.0), writes=[v_id])
            S.add("pool", lambda e, v_t=v_t: e.memset(v_t[:, :, :, 129:130], 0.0), writes=[v_id])

        def load_w(src, src_id):
            wt, wid = wpool.next()
            S.add("sp", lambda e: e.dma_start(out=wt[:], in_=src.rearrange("(kc p) n -> p kc n", p=128)),
                  reads=[src_id], writes=[wid], dma=wid)
            return wt, wid

        def pgT(p):
            return p[:].bitcast(BF16).rearrange("p (j x) -> p j x", j=8)

        def rmsnorm_stats(src_ap, src_ids, n, scale_div, width_ap_out, junk_id):
            st, sid = st1.next()
            S.add("act", lambda e: e.activation(out=width_ap_out, in_=src_ap, func=AF.Square, accum_out=st[0:n, 0:1]),
                  reads=src_ids, writes=[junk_id, sid])
            S.add("act", lambda e: e.activation(out=st[0:n, 1:2], in_=st[0:n, 0:1], func=AF.Ln, scale=1.0 / scale_div, bias=EPS),
                  reads=[sid], writes=[sid])
            S.add("act", lambda e: e.activation(out=st[0:n, 1:2], in_=st[0:n, 1:2], func=AF.Exp, scale=-0.5),
                  reads=[sid], writes=[sid])
            return st, sid

        def sigmoid_from_psum(p, pid, n_part, ncol, out_ap, out_id, mul_by_x=False, extra=None, extra_id=None):
            et, eid = etm.next()
            S.add("act", lambda e: e.activation(out=et[0:n_part, 0:ncol], in_=p, func=AF.Exp, scale=-1.0),
                  reads=[pid], writes=[eid])
            S.add("dve", lambda e: e.tensor_scalar(out=et[0:n_part, 0:ncol], in0=et[0:n_part, 0:ncol], scalar1=1.0,
                                                   scalar2=None, op0=ALU.add), reads=[eid], writes=[eid])
            if not mul_by_x:
                S.add("dve", lambda e: e.reciprocal(out=out_ap, in_=et[0:n_part, 0:ncol]), reads=[eid], writes=[out_id])
                return
            S.add("dve", lambda e: e.reciprocal(out=et[0:n_part, 0:ncol], in_=et[0:n_part, 0:ncol]), reads=[eid], writes=[eid])
            if extra is None:
                S.add("dve", lambda e: e.tensor_tensor(out=out_ap, in0=p, in1=et[0:n_part, 0:ncol], op=ALU.mult),
                      reads=[pid, eid], writes=[out_id])
            else:
                S.add("dve", lambda e: e.tensor_tensor(out=et[0:n_part, 0:ncol], in0=p, in1=et[0:n_part, 0:ncol], op=ALU.mult),
                      reads=[pid, eid], writes=[eid])
                S.add("pool", lambda e: e.tensor_tensor(
                    out=out_ap.rearrange("p (h x) -> p h x", x=128),
                    in0=et[0:n_part, 0:ncol].rearrange("p (h x) -> p h x", x=128),
                    in1=extra.unsqueeze(1).to_broadcast([n_part, ncol // 128, 128]), op=ALU.mult),
                      reads=[eid, extra_id], writes=[out_id])

        def convert_cache(l):
            for s in range(2):
                for g2 in range(8):
                    ks_t, ks_id = kTst.next()
                    vs_t, vs_id = vst.next()
                    for kk in range(2):
                        kb = 2 * g2 + kk
                        c_t, c_id = cst.next()
                        S.add("sp", lambda e, c_t=c_t, kb=kb, s=s: e.dma_start(out=c_t[:], in_=cache_k[l, s, kb * 128:(kb + 1) * 128, :]),
                              writes=[c_id], dma=c_id)
                        cb_t, cb_id = cstb.next()
                        S.add("dve", lambda e, c_t=c_t, cb_t=cb_t: e.tensor_copy(out=cb_t[:], in_=c_t[:]),
                              reads=[c_id], writes=[cb_id])
                        p, pid = pg.next()
                        for h in range(NH):
                            S.add("pe", lambda e, p=p, cb_t=cb_t, h=h: e.transpose(out=pgT(p)[:, h, :], in_=cb_t[:, h * 128:(h + 1) * 128],
                                                                                 identity=ident[:]),
                                  reads=[cb_id, "ident"], writes=[pid])
                        S.add("act", lambda e, p=p, ks_t=ks_t, kk=kk: e.copy(out=ks_t[:, :, kk * 128:(kk + 1) * 128], in_=pgT(p)),
                              reads=[pid], writes=[ks_id])
                        c2_t, c2_id = cst.next()
                        S.add("sp", lambda e, c2_t=c2_t, kb=kb, s=s: e.dma_start(out=c2_t[:], in_=cache_v[l, s, kb * 128:(kb + 1) * 128, :]),
                              writes=[c2_id], dma=c2_id)
                        S.add("pool", lambda e, c2_t=c2_t, vs_t=vs_t, kk=kk: e.tensor_copy(
                            out=vs_t[:, :, kk, 0:128], in_=c2_t[:].rearrange("p (h x) -> p h x", h=8)),
                            reads=[c2_id], writes=[vs_id])
                    S.add("pool", lambda e, ks_t=ks_t, g2=g2, s=s: e.dma_start(
                        out=kT_s[s].rearrange("h p t -> p h t")[:, :, g2 * 256:(g2 + 1) * 256], in_=ks_t[:]),
                        reads=[ks_id], writes=[("kTs", s)], dma=ks_id)
                    S.add("pool", lambda e, vs_t=vs_t, g2=g2, s=s: e.dma_start(
                        out=v_s[s].rearrange("h p k x -> p h k x")[:, :, 2 * g2:2 * g2 + 2, :], in_=vs_t[:]),
                        reads=[vs_id], writes=[("vs", s)], dma=vs_id)

        def do_tile(l, segs):
            last_layer = (l == depth - 1)
            ncols = sum(sg["n"] for sg in segs)
            blocks = []
            for si, sg in enumerate(segs):
                for bo in range(0, sg["n"], 128):
                    bn = min(128, sg["n"] - bo)
                    blocks.append(dict(si=si, sg=sg, bo=bo, bn=bn, col=sg["col0"] + bo))
            xt_t, xt_id = xnT.next()
            if DBG: print('MARK', segs[0]['kind'], segs[0]['t0'], '# ---- step 1:', len(S.ops))
            for b in blocks:
                sg, bn = b["sg"], b["bn"]
                h_t, h_id = hres.next()
                b["h"], b["hid"] = h_t, h_id
                r0 = sg["t0"] + b["bo"]
                if sg["kind"] in ("m", "p"):
                    hid_src = ("hp", r0 // 128 if sg["kind"] == "p" else "m")
                    if l == 0:
                        src = meta[0:bn, :] if sg["kind"] == "m" else x_p[r0 - NMETA:r0 - NMETA + bn, :]
                    else:
                        src = hb_p[r0:r0 + bn, :]
                else:
                    s = int(sg["kind"][1])
                    hid_src = ("hs", s)
                    src = x_s[s, r0:r0 + bn, :] if l == 0 else hb_s[s, r0:r0 + bn, :]
                b["hsrc"] = hid_src
                S.add("sp", lambda e, h_t=h_t, src=src, bn=bn: e.dma_start(out=h_t[0:bn, :], in_=src),
                      reads=[hid_src], writes=[h_id], dma=h_id)
                xb_t, xb_id = xnb.next()
                st, sid = rmsnorm_stats(h_t[0:bn, :], [h_id], bn, float(D), xb_t[0:bn, :], xb_id)
                S.add("dve", lambda e, xb_t=xb_t, h_t=h_t, st=st, bn=bn: e.tensor_scalar(
                    out=xb_t[0:bn, :], in0=h_t[0:bn, :], scalar1=st[0:bn, 1:2], scalar2=None, op0=ALU.mult),
                    reads=[h_id, sid], writes=[xb_id])
                p, pid = pg.next()
                for j in range(8):
                    S.add("pe", lambda e, p=p, xb_t=xb_t, j=j, bn=bn: e.transpose(
                        out=pgT(p)[:, j, 0:bn], in_=xb_t[0:bn, j * 128:(j + 1) * 128], identity=ident[0:bn, 0:bn]),
                        reads=[xb_id, "ident"], writes=[pid])
                S.add("dve", lambda e, p=p, b=b, bn=bn: e.tensor_tensor(
                    out=xt_t[:, :, b["col"]:b["col"] + bn], in0=pgT(p)[:, :, 0:bn],
                    in1=nwt[:, l, :].unsqueeze(2).to_broadcast([128, 8, bn]), op=ALU.mult),
                    reads=[pid, "nwt"], writes=[(xt_id, b["col"])])
            xt_ids = [(xt_id, b["col"]) for b in blocks]

            if DBG: print('MARK', segs[0]['kind'], segs[0]['t0'], '# ---- step 2:', len(S.ops))
            qT_t, qT_id = qT.next()
            for b in blocks:
                sg, bn = b["sg"], b["bn"]
                b["vf"], b["vfid"] = vf.next()
                b["ga"], b["gaid"] = gatea.next()
                b["ro"], b["roid"] = rot.next()
                b["rb"], b["rbid"] = rotb.next()
                r0 = sg["t0"] + b["bo"]
                rrow = r0 if sg["kind"] in ("m", "p") else TP + r0
                b["rp"], b["rpid"] = ropet.next()
                S.add("sp", lambda e, rp_t=b["rp"], rrow=rrow, bn=bn: e.dma_start(out=rp_t[0:bn], in_=rope[rrow:rrow + bn]),
                      writes=[b["rpid"]], dma=b["rpid"])
            for sg in segs:
                sg["vst"], sg["vstid"] = vst.next()
                sg["kst"], sg["kstid"] = kTst.next()
            for g in range(8):
                wt, wid = load_w(wsc_in[l][:, 2 * D + g * 512: 2 * D + (g + 1) * 512], ("wsc", l, 2 + g // 2))
                for b in blocks:
                    bn = b["bn"]
                    p, pid = pg.next()
                    for kc in range(8):
                        S.add("pe", lambda e, p=p, wt=wt, kc=kc, b=b, bn=bn: e.matmul(
                            p[0:bn, :], lhsT=xt_t[:, kc, b["col"]:b["col"] + bn], rhs=wt[:, kc, :],
                            start=(kc == 0), stop=(kc == 7)), reads=[(xt_id, b["col"]), wid], writes=[pid])
                    if g < 4:
                        rp_t = b["rp"]
                        src = p[0:bn, :].rearrange("p (a c x) -> p a c x", a=8, c=2)
                        x1, x2 = src[:, :, 0, :], src[:, :, 1, :]
                        cosb = rp_t[0:bn, 0:1, :].to_broadcast([bn, 8, 32])
                        sinb = rp_t[0:bn, 1:2, :].to_broadcast([bn, 8, 32])
                        tm_t, tm_id = rtmp.next()
                        tt = [tm_t[0:bn, i4, :].rearrange("p (a x) -> p a x", a=8) for i4 in range(4)]
                        for i4, (xa, tb) in enumerate(((x1, cosb), (x2, sinb), (x2, cosb), (x1, sinb))):
                            S.add("dve", lambda e, o=tt[i4], xa=xa, tb=tb: e.tensor_tensor(out=o, in0=xa, in1=tb, op=ALU.mult),
                                  reads=[pid, b["rpid"]], writes=[(tm_id, i4)])
                        if g < 2:
                            dst = b["rb"][0:bn, g * 512:(g + 1) * 512].rearrange("p (a c x) -> p a c x", a=8, c=2)
                            did = (b["rbid"], g)
                        else:
                            dst = b["ro"][0:bn, (g - 2) * 512:(g - 1) * 512].rearrange("p (a c x) -> p a c x", a=8, c=2)
                            did = (b["roid"], g)
                        S.add("pool", lambda e, dst=dst, tt=tt: e.tensor_tensor(out=dst[:, :, 0, :], in0=tt[0], in1=tt[1], op=ALU.subtract),
                              reads=[(tm_id, 0), (tm_id, 1)], writes=[did])
                        S.add("pool", lambda e, dst=dst, tt=tt: e.tensor_tensor(out=dst[:, :, 1, :], in0=tt[2], in1=tt[3], op=ALU.add),
                              reads=[(tm_id, 2), (tm_id, 3)], writes=[did])
                        if g >= 2:
                            S.add("pool", lambda e, b=b, bn=bn, g=g: e.tensor_copy(
                                out=b["rb"][0:bn, 1024 + (g - 2) * 512:1024 + (g - 1) * 512], in_=b["ro"][0:bn, (g - 2) * 512:(g - 1) * 512]),
                                reads=[did], writes=[(b["rbid"], g)])
                    elif g < 6:
                        S.add("act", lambda e, p=p, b=b, bn=bn, g=g: e.copy(out=b["vf"][0:bn, (g - 4) * 512:(g - 3) * 512], in_=p[0:bn, :]),
                              reads=[pid], writes=[(b["vfid"], g)])
                        sg = b["sg"]
                        kk = b["bo"] // 128
                        S.add("pool", lambda e, b=b, sg=sg, kk=kk, bn=bn, g=g: e.tensor_copy(
                            out=sg["vst"][0:bn, 4 * (g - 4):4 * (g - 3), kk, 0:128],
                            in_=b["vf"][0:bn, (g - 4) * 512:(g - 3) * 512].rearrange("p (h x) -> p h x", h=4)),
                            reads=[(b["vfid"], g)], writes=[sg["vstid"]])
                    else:
                        c0 = (g - 6) * 512
                        sigmoid_from_psum(p[0:bn, :], pid, bn, 512, b["ga"][0:bn, c0:c0 + 512], (b["gaid"], g),
                                          mul_by_x=True, extra=subw[0:bn, l, :], extra_id=("subw", l))
            for b in blocks:
                sg, bn = b["sg"], b["bn"]
                kind = sg["kind"]
                r0 = sg["t0"] + b["bo"]
                rb_t, rb_id = b["rb"], b["rbid"]
                rbids = [(rb_id, g) for g in range(4)]
                if kind in ("m", "p"):
                    kdst, vdst = nk_p[l, r0:r0 + bn, :], nv_p[l, r0:r0 + bn, :]
                else:
                    s = int(kind[1])
                    kdst, vdst = nk_s[l, s, r0:r0 + bn, :], nv_s[l, s, r0:r0 + bn, :]
                S.add("pool", lambda e, kdst=kdst, b=b, bn=bn: e.dma_start(out=kdst, in_=b["ro"][0:bn, :]),
                      reads=[(b["roid"], 2), (b["roid"], 3)], writes=[("out_k", l, kind, r0)], dma=b["roid"])
                S.add("pool", lambda e, vdst=vdst, b=b, bn=bn: e.dma_start(out=vdst, in_=b["vf"][0:bn, :]),
                      reads=[(b["vfid"], 4), (b["vfid"], 5)], writes=[("out_v", l, kind, r0)], dma=b["vfid"])
                for which in range(2):
                    p, pid = pg.next()
                    for h in range(NH):
                        S.add("pe", lambda e, p=p, rb_t=rb_t, h=h, which=which, bn=bn: e.transpose(
                            out=pgT(p)[:, h, 0:bn], in_=rb_t[0:bn, which * 1024 + h * 128: which * 1024 + (h + 1) * 128],
                            identity=ident[0:bn, 0:bn]), reads=rbids + ["ident"], writes=[pid])
                    if which == 0:
                        S.add("act", lambda e, p=p, b=b, bn=bn: e.copy(out=qT_t[:, :, b["col"]:b["col"] + bn], in_=pgT(p)[:, :, 0:bn]),
                              reads=[pid], writes=[(qT_id, b["col"])])
                    else:
                        S.add("act", lambda e, p=p, sg=sg, b=b, bn=bn: e.copy(out=sg["kst"][:, :, b["bo"]:b["bo"] + bn], in_=pgT(p)[:, :, 0:bn]),
                              reads=[pid], writes=[sg["kstid"]])
            if DBG: print('MARK', segs[0]['kind'], segs[0]['t0'], '# store K^T / ', len(S.ops))
            for sg in segs:
                kind, n = sg["kind"], sg["n"]
                if kind == "m":
                    kcol, kb0, kTd, vd, kid, vid_ = 0, 0, kT_p, v_p, "kTp", "vp"
                elif kind == "p":
                    kb0 = 1 + (sg["t0"] - NMETA) // 128
                    kcol, kTd, vd, kid, vid_ = kb0 * 128, kT_p, v_p, "kTp", "vp"
                else:
                    s = int(kind[1])
                    kb0, kcol, kTd, vd, kid, vid_ = 16, 2048, kT_s[s], v_s[s], ("kTs", s), ("vs", s)
                sg["kb0"] = kb0
                nkb = (n + 127) // 128
                pn = min(n, 128)
                S.add("pool", lambda e, sg=sg, kTd=kTd, kcol=kcol, n=n: e.dma_start(
                    out=kTd.rearrange("h p t -> p h t")[:, :, kcol:kcol + n], in_=sg["kst"][:, :, 0:n]),
                    reads=[sg["kstid"]], writes=[kid], dma=sg["kstid"])
                S.add("pool", lambda e, sg=sg, vd=vd, kb0=kb0, nkb=nkb, pn=pn: e.dma_start(
                    out=vd.rearrange("h p k x -> p h k x")[0:pn, :, kb0:kb0 + nkb, :], in_=sg["vst"][0:pn, :, 0:nkb, :]),
                    reads=[sg["vstid"]], writes=[vid_], dma=sg["vstid"])

            if DBG: print('MARK', segs[0]['kind'], segs[0]['t0'], '# ---- step 3:', len(S.ops))
            xr_t, xr_id = xr.next()
            szr_t, szr_id = szr.next()
            sgr_t, sgr_id = sgr.next()
            sga_t, sga_id = sga.next()
            for si, sg in enumerate(segs):
                sg["xb"] = sg["col0"] + 3 * si
            for gg in range(8):
                colbase = gg * 512 if gg < 4 else 6 * D + (gg - 4) * 512
                wt, wid = load_w(wsc_in[l][:, colbase:colbase + 512], ("wsc", l, colbase // D))
                for c4 in range(4):
                    j = (gg % 2) * 4 + c4
                    p, pid = pg.next()
                    for kc in range(8):
                        S.add("pe", lambda e, p=p, wt=wt, kc=kc, c4=c4: e.matmul(
                            p[:, 0:ncols], lhsT=wt[:, kc, c4 * 128:(c4 + 1) * 128], rhs=xt_t[:, kc, 0:ncols],
                            start=(kc == 0), stop=(kc == 7)), reads=xt_ids + [wid], writes=[pid])
                    if gg < 2:
                        for sg in segs:
                            S.add("act", lambda e, p=p, sg=sg, j=j: e.copy(
                                out=xr_t[:, j, sg["xb"] + 3:sg["xb"] + 3 + sg["n"]], in_=p[:, sg["col0"]:sg["col0"] + sg["n"]]),
                                reads=[pid], writes=[(xr_id, j)])
                    elif gg < 4:
                        sigmoid_from_psum(p[:, 0:ncols], pid, 128, ncols, szr_t[:, j, 0:ncols], (szr_id, j), mul_by_x=True)
                    elif gg < 6:
                        sigmoid_from_psum(p[:, 0:ncols], pid, 128, ncols, sgr_t[:, j, 0:ncols], (sgr_id, j))
                    else:
                        sigmoid_from_psum(p[:, 0:ncols], pid, 128, ncols, sga_t[:, j, 0:ncols], (sga_id, j))

            if DBG: print('MARK', segs[0]['kind'], segs[0]['t0'], '# ---- step 5 ', len(S.ops))
            oaT_t, oaT_id = oaT.next()
            for sg in segs:
                kind, n, col0 = sg["kind"], sg["n"], sg["col0"]
                if kind == "m":
                    kbs = [(0, 16, 0, False)]
                    kTd, vd, kid, vid_ = kT_p, v_p, "kTp", "vp"
                elif kind == "p":
                    f0 = sg["t0"] - NMETA
                    kbs = [(0, 16, 0, False)] + [(kb, 128, 0, False) for kb in range(1, 1 + f0 // 128)]
                    for m in range(n // 128):
                        kbs.append((1 + f0 // 128 + m, 128, 128 * m, True))
                    kTd, vd, kid, vid_ = kT_p, v_p, "kTp", "vp"
                else:
                    s = int(kind[1])
                    kbs = [(kb, 128, 0, False) for kb in range(16)] + [(16, 64, 0, False)]
                    kTd, vd, kid, vid_ = kT_s[s], v_s[s], ("kTs", s), ("vs", s)
                nkb_all = kbs[-1][0] + 1
                nqb = (n + 127) // 128
                qbn = [min(128, n - 128 * q) for q in range(nqb)]
                ob = []
                for q in range(nqb):
                    o_t, o_id = obuf.next()
                    ob.append((o_t, o_id))
                for h in range(NH):
                    for q in range(nqb):
                        S.add("pe", lambda e, q=q, qn=qbn[q]: e.matmul(pacc_t[0:qn, q, :], lhsT=zeros[:, 0:qn], rhs=zeros[:, 0:512],
                                                                     start=True, stop=True), reads=["zeros"], writes=[("pacc", q)])
                    for lo in (0, 17):
                        part = [x for x in kbs if lo <= x[0] < lo + 17]
                        if not part:
                            continue
                        npk = part[-1][0] - lo + 1
                        kb_t, kb_id = kbuf.next()
                        vb_t, vb_id = vbuf.next()
                        S.add("sp", lambda e, kb_t=kb_t, kTd=kTd, h=h, lo=lo, npk=npk: e.dma_start(
                            out=kb_t[:, 0:npk * 128], in_=kTd[h, :, lo * 128:(lo + npk) * 128]), reads=[kid], writes=[kb_id], dma=kb_id)
                        S.add("sp", lambda e, vb_t=vb_t, vd=vd, h=h, lo=lo, npk=npk: e.dma_start(
                            out=vb_t[:, 0:npk, :], in_=vd[h, :, lo:lo + npk, :]), reads=[vid_], writes=[vb_id], dma=vb_id)
                        for (kb, kk, qlo, diag) in part:
                            nqc = n - qlo
                            kl = kb - lo
                            ps_t, ps_id = psc.next()
                            for c in range(2):
                                S.add("pe", lambda e, ps_t=ps_t, kb_t=kb_t, c=c, kl=kl, kk=kk, qlo=qlo, nqc=nqc, h=h, col0=col0, n=n: e.matmul(
                                    ps_t[0:kk, c, 0:nqc], lhsT=kb_t[64 * c:64 * c + 64, kl * 128:kl * 128 + kk],
                                    rhs=qT_t[64 * c:64 * c + 64, h, col0 + qlo:col0 + n], start=True, stop=True),
                                    reads=[kb_id] + [(qT_id, bb["col"]) for bb in blocks if bb["sg"] is sg], writes=[ps_id])
                            pt_t, pt_id = ptb.next()
                            S.add("act", lambda e, pt_t=pt_t, ps_t=ps_t, kk=kk, nqc=nqc: e.activation(
                                out=pt_t[0:kk, :, 0:nqc], in_=ps_t[0:kk, :, 0:nqc], func=AF.Exp, scale=0.125),
                                reads=[ps_id], writes=[pt_id])
                            if diag:
                                S.add("pool", lambda e, pt_t=pt_t: e.memset(pt_t[64:128, :, 0:64], 0.0),
                                      reads=[pt_id], writes=[pt_id])
                            for q in range(qlo // 128, nqb):
                                for c in range(2):
                                    S.add("pe", lambda e, pt_t=pt_t, vb_t=vb_t, q=q, c=c, kk=kk, kl=kl, qlo=qlo, qn=qbn[q]: e.matmul(
                                        pacc_t[0:qn, q, c * 129:(c + 1) * 129], lhsT=pt_t[0:kk, c, 128 * q - qlo:128 * q - qlo + qn],
                                        rhs=vb_t[0:kk, kl, 0:129], start=False, stop=True, skip_group_check=True),
                                        reads=[pt_id, vb_id], writes=[("pacc", q)])
                    for q in range(nqb):
                        nq = qbn[q]
                        o_t, o_id = ob[q]
                        ar_t, ar_id = arec.next()
                        acc = pacc_t[0:nq, q, 0:258].rearrange("p (c x) -> p c x", c=2)
                        S.add("dve", lambda e, ar_t=ar_t, acc=acc, nq=nq: e.reciprocal(out=ar_t[0:nq, 0:2], in_=acc[:, :, 128]),
                              reads=[("pacc", q)], writes=[ar_id])
                        S.add("dve", lambda e, ar_t=ar_t, nq=nq: e.tensor_tensor(out=ar_t[0:nq, 2:3], in0=ar_t[0:nq, 1:2],
                                                                              in1=neglam[0:nq, l:l + 1], op=ALU.mult),
                              reads=[ar_id, ("neglam", l)], writes=[ar_id])
                        at_t, at_id = atmp.next()
                        S.add("dve", lambda e, at_t=at_t, acc=acc, ar_t=ar_t, nq=nq: e.tensor_scalar(
                            out=at_t[0:nq, :], in0=acc[:, 1, 0:128], scalar1=ar_t[0:nq, 2:3], scalar2=None, op0=ALU.mult),
                            reads=[("pacc", q), ar_id], writes=[at_id])
                        S.add("dve", lambda e, o_t=o_t, at_t=at_t, acc=acc, ar_t=ar_t, nq=nq, h=h: e.scalar_tensor_tensor(
                            out=o_t[0:nq, h, :], in0=acc[:, 0, 0:128], scalar=ar_t[0:nq, 0:1], in1=at_t[0:nq, :],
                            op0=ALU.mult, op1=ALU.add), reads=[("pacc", q), ar_id, at_id], writes=[(o_id, h)])
                sblocks = [bb for bb in blocks if bb["sg"] is sg]
                for q in range(nqb):
                    nq = qbn[q]
                    o_t, o_id = ob[q]
                    bb = sblocks[q]
                    oids = [(o_id, h) for h in range(NH)]
                    s8_t, s8_id = st8.next()
                    osq_t, osq_id = rot.next()
                    osq = osq_t[:].rearrange("p (h x) -> p h x", h=8)
                    S.add("pool", lambda e, o_t=o_t, nq=nq, osq=osq: e.tensor_tensor(out=osq[0:nq], in0=o_t[0:nq], in1=o_t[0:nq], op=ALU.mult),
                          reads=oids, writes=[osq_id])
                    S.add("dve", lambda e, s8_t=s8_t, nq=nq, osq=osq: e.tensor_reduce(out=s8_t[0:nq, :], in_=osq[0:nq], axis=AX.X, op=ALU.add),
                          reads=[osq_id], writes=[s8_id])
                    S.add("act", lambda e, s8_t=s8_t, nq=nq: e.activation(out=s8_t[0:nq, :], in_=s8_t[0:nq, :], func=AF.Ln,
                                                                        scale=1.0 / 128, bias=EPS), reads=[s8_id], writes=[s8_id])
                    S.add("act", lambda e, s8_t=s8_t, nq=nq: e.activation(out=s8_t[0:nq, :], in_=s8_t[0:nq, :], func=AF.Exp, scale=-0.5),
                          reads=[s8_id], writes=[s8_id])
                    S.add("dve", lambda e, o_t=o_t, s8_t=s8_t, nq=nq: e.tensor_tensor(
                        out=o_t[0:nq], in0=o_t[0:nq], in1=s8_t[0:nq, :].unsqueeze(2).to_broadcast([nq, 8, 128]), op=ALU.mult),
                        reads=oids + [s8_id], writes=oids)
                    oa_t, oa_id = oab.next()
                    S.add("pool", lambda e, oa_t=oa_t, o_t=o_t, bb=bb, nq=nq: e.tensor_tensor(
                        out=oa_t[0:nq, :], in0=o_t[0:nq].rearrange("p h x -> p (h x)"), in1=bb["ga"][0:nq, :], op=ALU.mult),
                        reads=oids + [(bb["gaid"], 6), (bb["gaid"], 7)], writes=[oa_id])
                    p, pid = pg.next()
                    for j in range(8):
                        S.add("pe", lambda e, p=p, oa_t=oa_t, j=j, nq=nq: e.transpose(
                            out=pgT(p)[:, j, 0:nq], in_=oa_t[0:nq, j * 128:(j + 1) * 128], identity=ident[0:nq, 0:nq]),
                            reads=[oa_id, "ident"], writes=[pid])
                    S.add("act", lambda e, p=p, bb=bb, nq=nq: e.copy(out=oaT_t[:, :, bb["col"]:bb["col"] + nq], in_=pgT(p)[:, :, 0:nq]),
                          reads=[pid], writes=[(oaT_id, bb["col"])])
            oaT_ids = [(oaT_id, bb["col"]) for bb in blocks]

            if DBG: print('MARK', segs[0]['kind'], segs[0]['t0'], '# ---- step 4:', len(S.ops))
            orT_t, orT_id = orT.next()
            for sg in segs:
                kind, n, col0, xb = sg["kind"], sg["n"], sg["col0"], sg["xb"]
                ck = "p" if kind in ("m", "p") else kind
                hc = hcar[ck]
                if kind == "m":
                    S.add("pool", lambda e, xb=xb: e.memset(xr_t[:, :, xb:xb + 3], 0.0), writes=[(xr_id, "halo", xb)])
                    S.add("pool", lambda e, hc=hc: e.memset(hc[:], 0.0), writes=[("hcar", ck)])
                elif kind == "p":
                    S.add("pool", lambda e, xb=xb: e.tensor_copy(out=xr_t[:, :, xb:xb + 3], in_=xhalo[:]),
                          reads=["xhalo"], writes=[(xr_id, "halo", xb)])
                else:
                    s = int(kind[1])
                    for k3 in range(3):
                        S.add("sp", lambda e, xb=xb, s=s, k3=k3: e.dma_start(out=xr_t[:, :, xb + k3],
                                                                           in_=st_conv[l, s, k3].rearrange("(j p) -> p j", p=128),
                                                                           allow_slow_non_contiguous=True),
                              writes=[(xr_id, "halo", xb)], dma=(xr_id, "halo", xb))
                    S.add("sp", lambda e, s=s, hc=hc: e.dma_start(out=hc[:], in_=st_rnn[l, s].rearrange("(j p) -> p j", p=128),
                                                                allow_slow_non_contiguous=True),
                          writes=[("hcar", ck)], dma=("hcar", ck))
                for j in range(8):
                    xc_t, xc_id = rn["xc"].next()
                    xin = [(xr_id, j), (xr_id, "halo", xb)]
                    S.add("dve", lambda e, xc_t=xc_t, j=j, xb=xb, n=n: e.tensor_scalar(
                        out=xc_t[:, 0:n], in0=xr_t[:, j, xb:xb + n], scalar1=cwt[:, l, 0, j:j + 1], scalar2=cbt[:, l, j:j + 1],
                        op0=ALU.mult, op1=ALU.add), reads=xin + ["cwt", "cbt"], writes=[xc_id])
                    for k in range(1, 4):
                        S.add("dve", lambda e, xc_t=xc_t, j=j, xb=xb, n=n, k=k: e.scalar_tensor_tensor(
                            out=xc_t[:, 0:n], in0=xr_t[:, j, xb + k:xb + k + n], scalar=cwt[:, l, k, j:j + 1], in1=xc_t[:, 0:n],
                            op0=ALU.mult, op1=ALU.add), reads=xin + ["cwt", xc_id], writes=[xc_id])
                    xcb_t, xcb_id = xcb.next()
                    S.add("pool", lambda e, xcb_t=xcb_t, xc_t=xc_t, n=n: e.tensor_copy(out=xcb_t[:, 0:n], in_=xc_t[:, 0:n]),
                          reads=[xc_id], writes=[xcb_id])
                    gates = []
                    for gi, nb_ in ((0, nbrg), (1, nbig)):
                        p, pid = pg.next()
                        S.add("pe", lambda e, p=p, xcb_t=xcb_t, gi=gi, j=j, n=n: e.matmul(
                            p[:, 0:n], lhsT=wg[:, gi, j, :], rhs=xcb_t[:, 0:n], start=True, stop=True),
                            reads=[xcb_id, ("wg", gi)], writes=[pid])
                        g_t, g_id = rn["r" if gi == 0 else "i"].next()
                        S.add("act", lambda e, g_t=g_t, p=p, nb_=nb_, j=j, n=n: e.activation(
                            out=g_t[:, 0:n], in_=p[:, 0:n], func=AF.Exp, scale=-1.0, bias=nb_[:, l, j:j + 1]),
                            reads=[pid, "nbrg", "nbig"], writes=[g_id])
                        S.add("dve", lambda e, g_t=g_t, n=n: e.tensor_scalar(out=g_t[:, 0:n], in0=g_t[:, 0:n], scalar1=1.0, scalar2=None,
                                                                           op0=ALU.add), reads=[g_id], writes=[g_id])
                        S.add("dve", lambda e, g_t=g_t, n=n: e.reciprocal(out=g_t[:, 0:n], in_=g_t[:, 0:n]), reads=[g_id], writes=[g_id])
                        gates.append((g_t, g_id))
                    (r_t, r_id), (i_t, i_id) = gates
                    a_t, a_id = rn["a"].next()
                    m_t, m_id = rn["m"].next()
                    S.add("act", lambda e, a_t=a_t, r_t=r_t, j=j, n=n: e.activation(out=a_t[:, 0:n], in_=r_t[:, 0:n], func=AF.Exp,
                                                                                 scale=cch[:, l, j:j + 1]), reads=[r_id, "cch"], writes=[a_id])
                    S.add("act", lambda e, a_t=a_t, m_t=m_t, n=n: e.activation(out=m_t[:, 0:n], in_=a_t[:, 0:n], func=AF.Square),
                          reads=[a_id], writes=[m_id])
                    S.add("act", lambda e, m_t=m_t, n=n: e.activation(out=m_t[:, 0:n], in_=m_t[:, 0:n], func=AF.Ln, scale=-1.0, bias=1.0),
                          reads=[m_id], writes=[m_id])
                    S.add("act", lambda e, m_t=m_t, n=n: e.activation(out=m_t[:, 0:n], in_=m_t[:, 0:n], func=AF.Exp, scale=0.5),
                          reads=[m_id], writes=[m_id])
                    S.add("dve", lambda e, m_t=m_t, i_t=i_t, n=n: e.tensor_tensor(out=m_t[:, 0:n], in0=m_t[:, 0:n], in1=i_t[:, 0:n], op=ALU.mult),
                          reads=[m_id, i_id], writes=[m_id])
                    S.add("dve", lambda e, m_t=m_t, xc_t=xc_t, n=n: e.tensor_tensor(out=m_t[:, 0:n], in0=m_t[:, 0:n], in1=xc_t[:, 0:n], op=ALU.mult),
                          reads=[m_id, xc_id], writes=[m_id])
                    hs_t, hs_id = rn["hs"].next()
                    S.add("dve", lambda e, hs_t=hs_t, a_t=a_t, m_t=m_t, j=j, n=n, hc=hc: e.tensor_tensor_scan(
                        out=hs_t[:, 0:n], data0=a_t[:, 0:n], data1=m_t[:, 0:n], initial=hc[:, j:j + 1], op0=ALU.mult, op1=ALU.add),
                        reads=[a_id, m_id, ("hcar", ck)], writes=[hs_id])
                    S.add("pool", lambda e, hs_t=hs_t, j=j, n=n, hc=hc: e.tensor_copy(out=hc[:, j:j + 1], in_=hs_t[:, n - 1:n]),
                          reads=[hs_id], writes=[("hcar", ck)])
                    S.add("dve", lambda e, hs_t=hs_t, j=j, n=n, col0=col0: e.tensor_tensor(
                        out=orT_t[:, j, col0:col0 + n], in0=hs_t[:, 0:n], in1=szr_t[:, j, col0:col0 + n], op=ALU.mult),
                        reads=[hs_id, (szr_id, j)], writes=[(orT_id, j)])
                xall = [(xr_id, j) for j in range(8)] + [(xr_id, "halo", xb)]
                if kind in ("m", "p") and not sg["last"]:
                    S.add("pool", lambda e, xb=xb, n=n: e.tensor_copy(out=xhalo[:], in_=xr_t[:, :, xb + n:xb + n + 3]),
                          reads=xall, writes=["xhalo"])
                if sg["last"]:
                    if kind == "p":
                        cdst, rdst = nc_p[l], nr_p[l]
                    else:
                        s = int(kind[1])
                        cdst, rdst = nc_s[l, s], nr_s[l, s]
                    for k3 in range(3):
                        S.add("pool", lambda e, cdst=cdst, xb=xb, n=n, k3=k3: e.dma_start(
                            out=cdst[k3].rearrange("(j p) -> p j", p=128), in_=xr_t[:, :, xb + n + k3], allow_slow_non_contiguous=True),
                            reads=xall, writes=[("out_c", l, kind, k3)], dma=(xr_id, "o"))
                    S.add("pool", lambda e, rdst=rdst, hc=hc: e.dma_start(
                        out=rdst.rearrange("(j p) -> p j", p=128), in_=hc[:], allow_slow_non_contiguous=True),
                        reads=[("hcar", ck)], writes=[("out_r", l, kind)], dma=("hcar", ck, "o"))
            orT_ids = [(orT_id, j) for j in range(8)]

            if DBG: print('MARK', segs[0]['kind'], segs[0]['t0'], '# ---- step 6:', len(S.ops))
            mT_t, mT_id = mT.next()
            for hf in range(2):
                wr_t, wr_id = load_w(wsc_p[l, 0][:, hf * 512:(hf + 1) * 512], ("wscp", l, 0))
                wa_t, wa_id = load_w(wsc_p[l, 1][:, hf * 512:(hf + 1) * 512], ("wscp", l, 1))
                for c4 in range(4):
                    j = hf * 4 + c4
                    pr_, prid = pg.next()
                    for kc in range(8):
                        S.add("pe", lambda e, pr_=pr_, wr_t=wr_t, kc=kc, c4=c4: e.matmul(
                            pr_[:, 0:ncols], lhsT=wr_t[:, kc, c4 * 128:(c4 + 1) * 128], rhs=orT_t[:, kc, 0:ncols],
                            start=(kc == 0), stop=(kc == 7)), reads=orT_ids + [wr_id], writes=[prid])
                    pa_, paid = pg.next()
                    for kc in range(8):
                        S.add("pe", lambda e, pa_=pa_, wa_t=wa_t, kc=kc, c4=c4: e.matmul(
                            pa_[:, 0:ncols], lhsT=wa_t[:, kc, c4 * 128:(c4 + 1) * 128], rhs=oaT_t[:, kc, 0:ncols],
                            start=(kc == 0), stop=(kc == 7)), reads=oaT_ids + [wa_id], writes=[paid])
                    mt_t, mt_id = mtmp.next()
                    S.add("dve", lambda e, mt_t=mt_t, pr_=pr_, j=j: e.tensor_tensor(out=mt_t[:, 0, 0:ncols], in0=pr_[:, 0:ncols],
                                                                                 in1=sgr_t[:, j, 0:ncols], op=ALU.mult),
                          reads=[prid, (sgr_id, j)], writes=[(mt_id, 0)])
                    S.add("dve", lambda e, mt_t=mt_t, pa_=pa_, j=j: e.tensor_tensor(out=mt_t[:, 1, 0:ncols], in0=pa_[:, 0:ncols],
                                                                                 in1=sga_t[:, j, 0:ncols], op=ALU.mult),
                          reads=[paid, (sga_id, j)], writes=[(mt_id, 1)])
                    S.add("pool", lambda e, mt_t=mt_t, j=j: e.tensor_tensor(out=mT_t[:, j, 0:ncols], in0=mt_t[:, 0, 0:ncols],
                                                                          in1=mt_t[:, 1, 0:ncols], op=ALU.add),
                          reads=[(mt_id, 0), (mt_id, 1)], writes=[(mT_id, j)])
            mT_ids = [(mT_id, j) for j in range(8)]
            for hf in range(2):
                wo_t, wo_id = load_w(wsc_p[l, 2][:, hf * 512:(hf + 1) * 512], ("wscp", l, 2))
                for b in blocks:
                    bn = b["bn"]
                    p, pid = pg.next()
                    for kc in range(8):
                        S.add("pe", lambda e, p=p, wo_t=wo_t, kc=kc, b=b, bn=bn: e.matmul(
                            p[0:bn, :], lhsT=mT_t[:, kc, b["col"]:b["col"] + bn], rhs=wo_t[:, kc, :],
                            start=(kc == 0), stop=(kc == 7)), reads=mT_ids + [wo_id], writes=[pid])
                    S.add("dve", lambda e, p=p, b=b, bn=bn, hf=hf: e.tensor_tensor(
                        out=b["h"][0:bn, hf * 512:(hf + 1) * 512], in0=b["h"][0:bn, hf * 512:(hf + 1) * 512], in1=p[0:bn, :], op=ALU.add),
                        reads=[pid, b["hid"]], writes=[b["hid"]])
            for b in blocks:
                sg, bn = b["sg"], b["bn"]
                kind = sg["kind"]
                r0 = sg["t0"] + b["bo"]
                if not last_layer:
                    dst = hb_p[r0:r0 + bn, :] if kind in ("m", "p") else hb_s[int(kind[1]), r0:r0 + bn, :]
                    S.add("pool", lambda e, dst=dst, b=b, bn=bn: e.dma_start(out=dst, in_=b["h"][0:bn, :]),
                          reads=[b["hid"]], writes=[b["hsrc"]], dma=(b["hid"], "o"))
                elif kind != "m":
                    jk_t, jk_id = xnb.next()
                    st, sid = rmsnorm_stats(b["h"][0:bn, :], [b["hid"]], bn, float(D), jk_t[0:bn, :], jk_id)
                    y_t, y_id = yst.next()
                    S.add("dve", lambda e, y_t=y_t, b=b, st=st, bn=bn: e.scalar_tensor_tensor(
                        out=y_t[0:bn, :], in0=b["h"][0:bn, :], scalar=st[0:bn, 1:2], in1=fnwt[0:bn, :], op0=ALU.mult, op1=ALU.mult),
                        reads=[b["hid"], sid, "fnwt"], writes=[y_id])
                    dst = y_p[r0 - NMETA:r0 - NMETA + bn, :] if kind == "p" else y_s[int(kind[1]), r0:r0 + bn, :]
                    S.add("pool", lambda e, dst=dst, y_t=y_t, bn=bn: e.dma_start(out=dst, in_=y_t[0:bn, :]),
                          reads=[y_id], writes=[("out_y", kind, r0)], dma=y_id)

        for l in range(depth):
            if l + 1 < depth:
                convert_weights(l + 1)
            load_gate_w(l)
            if not NOCACHE:
                convert_cache(l)
            do_tile(l, [dict(kind="s0", col0=0, n=64, t0=0, last=True)])
            do_tile(l, [dict(kind="s1", col0=0, n=64, t0=0, last=True)])
            do_tile(l, [dict(kind="m", col0=0, n=16, t0=0, last=False)])
            for ti in range(ntile):
                do_tile(l, [dict(kind="p", col0=0, n=TT, t0=NMETA + ti * TT, last=(ti == ntile - 1))])

        S.finalize()
        print("ops", len(S.ops), "sems", S.n_sems)
        with nc.allow_low_precision(reason="bf16 matmul operands by design"), nc.Block() as block:
            S.emit(block)
    return nc


_ROPE = None


def _rope_table():
    global _ROPE
    if _ROPE is None:
        half = 32
        inv = 1.0 / (10000.0 ** (np.arange(half, dtype=np.float32) / np.float32(half)))
        pos = np.concatenate([np.arange(TP), NMETA + PAST + np.arange(DSEQ)]).astype(np.float32)
        ang = pos[:, None] * inv[None, :].astype(np.float32)
        _ROPE = np.ascontiguousarray(np.stack([np.cos(ang), np.sin(ang)], axis=1).astype(np.float32))
    return _ROPE


def kernel(x_prompt, x_sample, cache_k, cache_v, state_conv, state_rnn, meta_tokens,
           norm_w, w_in, conv_w, conv_b, w_rg, b_rg, w_ig, b_ig, lru_lambda,
           lambda_q1, lambda_k1, lambda_q2, lambda_k2, subln_w,
           w_proj_rnn, w_proj_att, w_out, final_norm_w):
    A = lambda a: np.ascontiguousarray(np.asarray(a, dtype=np.float32))
    nc = build_program()
    shared = dict(meta=A(meta_tokens), norm_w=A(norm_w), w_in=A(w_in), conv_w=A(conv_w), conv_b=A(conv_b),
                  w_rg=A(w_rg), b_rg=A(b_rg), w_ig=A(w_ig), b_ig=A(b_ig), lru_l=A(lru_lambda),
                  lq1=A(lambda_q1), lk1=A(lambda_k1), lq2=A(lambda_q2), lk2=A(lambda_k2), subln=A(subln_w),
                  w_pr=A(w_proj_rnn), w_pa=A(w_proj_att), w_out=A(w_out), fnw=A(final_norm_w), rope=_rope_table())
    x_prompt, x_sample = np.asarray(x_prompt), np.asarray(x_sample)
    cache_k, cache_v = np.asarray(cache_k), np.asarray(cache_v)
    state_conv, state_rnn = np.asarray(state_conv), np.asarray(state_rnn)
    in_maps = []
    for c in range(8):
        m = dict(shared)
        m["x_p"] = A(x_prompt[c])
        m["x_s"] = A(x_sample[2 * c:2 * c + 2])
        m["cache_k"] = A(cache_k[:, 2 * c:2 * c + 2].reshape(DEPTH, 2, PAST, D))
        m["cache_v"] = A(cache_v[:, 2 * c:2 * c + 2].reshape(DEPTH, 2, PAST, D))
        m["st_conv"] = A(state_conv[:, 2 * c:2 * c + 2])
        m["st_rnn"] = A(state_rnn[:, 2 * c:2 * c + 2])
        in_maps.append(m)
    res = run_bass_kernel_spmd(nc, in_maps, core_ids=list(range(8)))
    R = res.results
    y_p = np.stack([R[c]["y_p"] for c in range(8)])
    y_s = np.concatenate([R[c]["y_s"] for c in range(8)], axis=0)
    nk_p = np.stack([R[c]["nk_p"] for c in range(8)], axis=1).reshape(DEPTH, 8, TP, NH, 2, 64)
    nv_p = np.stack([R[c]["nv_p"] for c in range(8)], axis=1).reshape(DEPTH, 8, TP, NH, 128)
    nc_p = np.stack([R[c]["nc_p"] for c in range(8)], axis=1)
    nr_p = np.stack([R[c]["nr_p"] for c in range(8)], axis=1)
    nk_s = np.concatenate([R[c]["nk_s"] for c in range(8)], axis=1).reshape(DEPTH, 16, DSEQ, NH, 2, 64)
    nv_s = np.concatenate([R[c]["nv_s"] for c in range(8)], axis=1).reshape(DEPTH, 16, DSEQ, NH, 128)
    nc_s = np.concatenate([R[c]["nc_s"] for c in range(8)], axis=1)
    nr_s = np.concatenate([R[c]["nr_s"] for c in range(8)], axis=1)
    return (y_p, y_s, nk_p, nv_p, nc_p, nr_p, nk_s, nv_s, nc_s, nr_s)
```

```python
import contextlib
import math
import numpy as np
import concourse.bass as bass
import concourse.mybir as mybir
from concourse.bass_utils import run_bass_kernel_spmd

F32 = mybir.dt.float32
BF16 = mybir.dt.bfloat16
AF = mybir.ActivationFunctionType
ALU = mybir.AluOpType
AX = mybir.AxisListType

D = 1024
DEPTH = 4
SEQ = 4096
NMETA = 16
TP = SEQ + NMETA
DSEQ = 64
PAST = 2048
NH = 8
EPS = 1e-6
TT = 256
NW = 3
NTILE = SEQ // TT
KCOLS_P = 33 * 128
KCOLS_S = 17 * 128
SEM_ROT = 30000
DBG = False
NOCACHE = False


class _Chan:
    def __init__(self, nc, name):
        self.nc, self.name, self.sems, self.cur = nc, name, [], 0

    def bump(self, units):
        if not self.sems or self.cur + units > SEM_ROT:
            self.sems.append(self.nc.alloc_semaphore(f"{self.name}_{len(self.sems)}"))
            self.cur = 0
        self.cur += units
        return (self.sems[-1], self.cur)


class Sched:
    ENG = ("pe", "act", "dve", "pool", "sp")

    def __init__(self, nc):
        self.nc, self.ops, self.state = nc, [], {}

    limit = None
    bases = set()

    def _split(self, b):
        if isinstance(b, tuple) and isinstance(b[0], str) and b[0] in self.bases:
            return b[0], b
        return b, None

    def add(self, eng, fn, reads=(), writes=(), dma=None, grp=False):
        if self.limit is not None and len(self.ops) >= self.limit:
            return -1
        deps, raw = set(), set()
        st = self.state
        for b in reads:
            base, part = self._split(b)
            e = st.setdefault(base, [None, [], {}])
            ws = [e[0]]
            if part is None:
                ws += [pv[0] for pv in e[2].values()]
            elif part in e[2]:
                ws.append(e[2][part][0])
            for w in ws:
                if w is not None:
                    deps.add(w)
                    raw.add(w)
        for b in writes:
            base, part = self._split(b)
            e = st.setdefault(base, [None, [], {}])
            if e[0] is not None:
                deps.add(e[0])
            deps.update(e[1])
            if part is None:
                for pv in e[2].values():
                    deps.add(pv[0])
                    deps.update(pv[1])
            elif part in e[2]:
                deps.add(e[2][part][0])
                deps.update(e[2][part][1])
        i = len(self.ops)
        deps.discard(None)
        if dma is not None:
            dma = (eng, dma)
        self.ops.append(dict(eng=eng, fn=fn, deps=deps, raw=raw, dma=dma, mark=False, grp=grp))
        for b in reads:
            base, part = self._split(b)
            e = st[base]
            if part is None:
                e[1].append(i)
            else:
                e[2].setdefault(part, [None, []])[1].append(i)
        for b in writes:
            base, part = self._split(b)
            e = st[base]
            if part is None:
                e[0], e[1], e[2] = i, [], {}
            else:
                e[2][part] = [i, []]
        return i

    def finalize(self):
        nc, ops = self.nc, self.ops
        for op in ops:
            need = []
            for d in op["deps"]:
                p = ops[d]
                if p["dma"] is not None or p["eng"] != op["eng"]:
                    need.append(d)
                elif p["eng"] != "pe" and d in op["raw"]:
                    need.append(d)
            op["need"] = need
            for d in need:
                ops[d]["mark"] = True
        chans = {e: _Chan(nc, "c_" + e) for e in self.ENG}
        dchan = {}
        for op in ops:
            if op["dma"] is not None:
                key = op["dma"]
                if key not in dchan:
                    dchan[key] = _Chan(nc, f"dma{len(dchan)}")
                op["inc"] = dchan[key].bump(16)
            elif op["mark"]:
                op["inc"] = chans[op["eng"]].bump(1)
            else:
                op["inc"] = None
        gfinal = {}
        for op in ops:
            if op["grp"]:
                gfinal[op["dma"]] = op["inc"]
        seen = {e: {} for e in self.ENG}
        for op in ops:
            w = {}
            for d in op["need"]:
                sem, val = gfinal[ops[d]["dma"]] if ops[d]["grp"] else ops[d]["inc"]
                k = id(sem)
                if seen[op["eng"]].get(k, 0) >= val:
                    continue
                if k not in w or w[k][1] < val:
                    w[k] = (sem, val)
            for k, sv in w.items():
                seen[op["eng"]][k] = sv[1]
            op["waits"] = list(w.values())
        self.dchan = dchan
        self.n_sems = sum(len(c.sems) for c in chans.values()) + sum(len(c.sems) for c in dchan.values())

    def emit(self, block):
        ops = self.ops

        def run(engname, e):
            for op in ops:
                if op["eng"] != engname:
                    continue
                for sem, val in op["waits"]:
                    e.wait_ge(sem, val)
                ins = op["fn"](e)
                if op["inc"] is not None:
                    ins.then_inc(op["inc"][0], 16 if op["dma"] is not None else 1)

        @block.tensor
        def _(e):
            run("pe", e)

        @block.scalar
        def _(e):
            run("act", e)

        @block.vector
        def _(e):
            run("dve", e)

        @block.gpsimd
        def _(e):
            run("pool", e)

        @block.sync
        def _(e):
            run("sp", e)
            for c in self.dchan.values():
                e.wait_ge(c.sems[-1], c.cur)


class Rot:
    def __init__(self, es, nc, name, shape, dt, n, psum=False):
        mk = nc.psum_tensor if psum else nc.sbuf_tensor
        self.t = [es.enter_context(mk(f"{name}{i}", shape, dt)) for i in range(n)]
        self.ids = [f"{name}{i}" for i in range(n)]
        Sched.bases.update(self.ids)
        self.k = 0

    def next(self):
        i = self.k % len(self.t)
        self.k += 1
        return self.t[i], self.ids[i]


def build_program(depth=DEPTH, ntile=NTILE, small=True):
    nc = bass.Bass("TRN2", target_bir_lowering=False)

    def din(name, shape):
        return nc.dram_tensor(name, shape, F32, kind="ExternalInput").ap()

    def dout(name, shape):
        return nc.dram_tensor(name, shape, F32, kind="ExternalOutput").ap()

    def dscr(name, shape, dt):
        return nc.dram_tensor(name, shape, dt).ap()

    x_p = din("x_p", [SEQ, D])
    x_s = din("x_s", [2, DSEQ, D])
    cache_k = din("cache_k", [DEPTH, 2, PAST, D])
    cache_v = din("cache_v", [DEPTH, 2, PAST, D])
    st_conv = din("st_conv", [DEPTH, 2, 3, D])
    st_rnn = din("st_rnn", [DEPTH, 2, D])
    meta = din("meta", [NMETA, D])
    norm_w = din("norm_w", [DEPTH, D])
    w_in = din("w_in", [DEPTH, D, 8 * D])
    conv_w = din("conv_w", [DEPTH, 4, D])
    conv_b = din("conv_b", [DEPTH, D])
    w_rg = din("w_rg", [DEPTH, 8, 128, 128])
    b_rg = din("b_rg", [DEPTH, D])
    w_ig = din("w_ig", [DEPTH, 8, 128, 128])
    b_ig = din("b_ig", [DEPTH, D])
    lru_l = din("lru_l", [DEPTH, D])
    lam_in = [din(n, [DEPTH, 64]) for n in ("lq1", "lk1", "lq2", "lk2")]
    subln = din("subln", [DEPTH, 128])
    w_pr = din("w_pr", [DEPTH, D, D])
    w_pa = din("w_pa", [DEPTH, D, D])
    w_out = din("w_out", [DEPTH, D, D])
    fnw = din("fnw", [D])
    rope = din("rope", [TP + DSEQ, 2, 32])

    y_p = dout("y_p", [SEQ, D])
    y_s = dout("y_s", [2, DSEQ, D])
    nk_p = dout("nk_p", [DEPTH, TP, D])
    nv_p = dout("nv_p", [DEPTH, TP, D])
    nc_p = dout("nc_p", [DEPTH, 3, D])
    nr_p = dout("nr_p", [DEPTH, D])
    nk_s = dout("nk_s", [DEPTH, 2, DSEQ, D])
    nv_s = dout("nv_s", [DEPTH, 2, DSEQ, D])
    nc_s = dout("nc_s", [DEPTH, 2, 3, D])
    nr_s = dout("nr_s", [DEPTH, 2, D])

    wsc_in = dscr("wsc_in", [DEPTH, D, 8 * D], BF16)
    wsc_p = dscr("wsc_p", [DEPTH, 3, D, D], BF16)
    hb_p = dscr("hb_p", [TP, D], F32)
    hb_s = dscr("hb_s", [2, DSEQ, D], F32)
    kT_p = dscr("kT_p", [NH, 128, KCOLS_P], BF16)
    v_p = dscr("v_p", [NH, 128, 33, 130], BF16)
    kT_s = dscr("kT_s", [2, NH, 128, KCOLS_S], BF16)
    v_s = dscr("v_s", [2, NH, 128, 17, 130], BF16)

    S = Sched(nc)
    TW = TT
    with contextlib.ExitStack() as es:
        def T(name, shape, dt):
            return es.enter_context(nc.sbuf_tensor(name, shape, dt))

        identf = T("identf", [128, 128], F32)
        ident = T("ident", [128, 128], BF16)
        zeros = T("zeros", [128, 1040], BF16)
        nwt = T("nwt", [128, DEPTH, 8], F32)
        cwt = T("cwt", [128, DEPTH, 4, 8], F32)
        cbt = T("cbt", [128, DEPTH, 8], F32)
        nbrg = T("nbrg", [128, DEPTH, 8], F32)
        nbig = T("nbig", [128, DEPTH, 8], F32)
        cch = T("cch", [128, DEPTH, 8], F32)
        wg = T("wg", [128, 2, 8, 128], BF16)
        subw = T("subw", [128, DEPTH, 128], F32)
        sub_t = T("sub_t", [128, 128], F32)
        lamt = T("lamt", [128, 4, 64], F32)
        lams = T("lams", [128, 2], F32)
        neglam = T("neglam", [128, DEPTH], F32)
        fnwt = T("fnwt", [128, D], F32)
        hcar = {k: T("hcar_" + k, [128, 8], F32) for k in ("p", "s0", "s1")}
        xhalo = T("xhalo_p", [128, 8, 3], F32)

        S.add("pool", lambda e: e.memset(identf[:], 1.0), writes=["identf"])
        S.add("pool", lambda e: e.affine_select(out=identf[:], in_=identf[:], pattern=[[-1, 128]],
                                                compare_op=ALU.is_equal, fill=0.0, base=0, channel_multiplier=1),
              reads=["identf"], writes=["identf"])
        S.add("dve", lambda e: e.tensor_copy(out=ident[:], in_=identf[:]), reads=["identf"], writes=["ident"])
        S.add("pool", lambda e: e.memset(zeros[:], 0.0), writes=["zeros"])

        def sdma(out, in_, w, r=()):
            S.add("sp", lambda e: e.dma_start(out=out, in_=in_, allow_slow_non_contiguous=True),
                  reads=list(r), writes=[w], dma=w)

        sdma(nwt[:], norm_w.rearrange("l (j p) -> p l j", p=128), "nwt")
        sdma(cwt[:], conv_w.rearrange("l k (j p) -> p l k j", p=128), "cwt")
        sdma(cbt[:], conv_b.rearrange("l (j p) -> p l j", p=128), "cbt")
        sdma(nbrg[:], b_rg.rearrange("l (j p) -> p l j", p=128), "nbrg")
        sdma(nbig[:], b_ig.rearrange("l (j p) -> p l j", p=128), "nbig")
        sdma(cch[:], lru_l.rearrange("l (j p) -> p l j", p=128), "cch")
        sdma(fnwt[:], fnw.partition_broadcast(128), "fnwt")
        S.add("dve", lambda e: e.tensor_scalar(out=nbrg[:], in0=nbrg[:], scalar1=-1.0, scalar2=None, op0=ALU.mult),
              reads=["nbrg"], writes=["nbrg"])
        S.add("dve", lambda e: e.tensor_scalar(out=nbig[:], in0=nbig[:], scalar1=-1.0, scalar2=None, op0=ALU.mult),
              reads=["nbig"], writes=["nbig"])
        S.add("act", lambda e: e.activation(out=cch[:], in_=cch[:], func=AF.Exp, scale=-1.0), reads=["cch"], writes=["cch"])
        S.add("act", lambda e: e.activation(out=cch[:], in_=cch[:], func=AF.Ln, bias=1.0), reads=["cch"], writes=["cch"])
        S.add("dve", lambda e: e.tensor_scalar(out=cch[:], in0=cch[:], scalar1=-8.0, scalar2=None, op0=ALU.mult),
              reads=["cch"], writes=["cch"])
        for l in range(DEPTH):
            lam_init = 0.8 - 0.6 * math.exp(-0.3 * l)
            sdma(sub_t[:], subln[l].partition_broadcast(128), "sub_t")
            S.add("dve", lambda e, l=l, li=lam_init: e.tensor_scalar(
                out=subw[:, l, :], in0=sub_t[:],
                scalar1=1.0 - li, scalar2=None, op0=ALU.mult), reads=["sub_t"], writes=[("subw", l)])
            for i4 in range(4):
                sdma(lamt[:, i4, :], lam_in[i4][l].partition_broadcast(128), ("lamt", i4))
            for pr in range(2):
                S.add("dve", lambda e, pr=pr: e.tensor_tensor(out=lamt[:, 2 * pr, :], in0=lamt[:, 2 * pr, :],
                                                            in1=lamt[:, 2 * pr + 1, :], op=ALU.mult),
                      reads=[("lamt", 2 * pr), ("lamt", 2 * pr + 1)], writes=[("lamt", 2 * pr)])
                S.add("dve", lambda e, pr=pr: e.tensor_reduce(out=lams[:, pr:pr + 1], in_=lamt[:, 2 * pr, :],
                                                            axis=AX.X, op=ALU.add),
                      reads=[("lamt", 2 * pr)], writes=[("lams", pr)])
            S.add("act", lambda e: e.activation(out=lams[:], in_=lams[:], func=AF.Exp),
                  reads=[("lams", 0), ("lams", 1)], writes=[("lams", 0), ("lams", 1)])
            S.add("dve", lambda e, l=l: e.tensor_tensor(out=neglam[:, l:l + 1], in0=lams[:, 1:2], in1=lams[:, 0:1],
                                                      op=ALU.subtract),
                  reads=[("lams", 0), ("lams", 1)], writes=[("neglam", l)])
            S.add("dve", lambda e, l=l, li=lam_init: e.tensor_scalar(out=neglam[:, l:l + 1], in0=neglam[:, l:l + 1],
                                                                   scalar1=-li, scalar2=None, op0=ALU.add),
                  reads=[("neglam", l)], writes=[("neglam", l)])

        def load_gate_w(l):
            for gi, wsrc in enumerate((w_rg, w_ig)):
                S.add("pool", lambda e, l=l, gi=gi, wsrc=wsrc: e.dma_start(
                    out=wg[:, gi, :, :], in_=wsrc[l].rearrange("n i j -> i n j")),
                    writes=[("wg", gi)], dma=("wg", gi))

        def convert_weights(l):
            for q in range(8):
                S.add("pool", lambda e, l=l, q=q: e.dma_start(out=wsc_in[l][:, q * D:(q + 1) * D],
                                                            in_=w_in[l][:, q * D:(q + 1) * D]),
                      writes=[("wsc", l, q)], dma=("wsc", l), grp=True)
            for i3, wsrc in enumerate((w_pr, w_pa, w_out)):
                S.add("pool", lambda e, l=l, i3=i3, wsrc=wsrc: e.dma_start(out=wsc_p[l, i3], in_=wsrc[l]),
                      writes=[("wscp", l, i3)], dma=("wsc", l), grp=True)

        convert_weights(0)
        S.add("sp", lambda e: e.dma_start(out=kT_p.rearrange("h p t -> p h t")[:, :, 0:128],
                                          in_=zeros[:, 0:1024].rearrange("p (h x) -> p h x", h=8)),
              reads=["zeros"], writes=["kTp"], dma="z0")
        S.add("sp", lambda e: e.dma_start(out=v_p.rearrange("h p k x -> p h k x")[:, :, 0, :],
                                          in_=zeros[:, 0:1040].rearrange("p (h x) -> p h x", h=8)),
              reads=["zeros"], writes=["vp"], dma="z1")
        for s_ in range(2):
            S.add("sp", lambda e, s_=s_: e.dma_start(out=kT_s[s_].rearrange("h p t -> p h t")[:, :, 2048:2176],
                                                   in_=zeros[:, 0:1024].rearrange("p (h x) -> p h x", h=8)),
                  reads=["zeros"], writes=[("kTs", s_)], dma=("z2", s_))
            S.add("sp", lambda e, s_=s_: e.dma_start(out=v_s[s_].rearrange("h p k x -> p h k x")[:, :, 16, :],
                                                   in_=zeros[:, 0:1040].rearrange("p (h x) -> p h x", h=8)),
                  reads=["zeros"], writes=[("vs", s_)], dma=("z3", s_))

        wpool = Rot(es, nc, "wch", [128, 8, 512], BF16, NW)
        hres = Rot(es, nc, "hres", [128, D], F32, 3)
        xnb = Rot(es, nc, "xnb", [128, D], BF16, 2)
        st1 = Rot(es, nc, "st1", [128, 2], F32, 4)
        xnT = Rot(es, nc, "xnT", [128, 8, TW], BF16, 1)
        rot = Rot(es, nc, "rot", [128, D], F32, 2)
        rotb = Rot(es, nc, "rotb", [128, 2048], BF16, 2)
        rtmp = Rot(es, nc, "rtmp", [128, 4, 256], F32, 2)
        ropet = Rot(es, nc, "ropet", [128, 2, 32], F32, 3)
        qT = Rot(es, nc, "qT", [128, 8, TW], BF16, 1)
        kTst = Rot(es, nc, "kTst", [128, 8, 256], BF16, 2)
        vst = Rot(es, nc, "vst", [128, 8, 2, 130], BF16, 2)
        vf = Rot(es, nc, "vf", [128, D], F32, 3)
        gatea = Rot(es, nc, "gatea", [128, D], BF16, 2)
        etm = Rot(es, nc, "etm", [128, 512], F32, 2)
        XW = TW + 9
        xr = Rot(es, nc, "xr", [128, 8, XW], F32, 1)
        szr = Rot(es, nc, "szr", [128, 8, TW], BF16, 1)
        sgr = Rot(es, nc, "sgr", [128, 8, TW], BF16, 1)
        sga = Rot(es, nc, "sga", [128, 8, TW], BF16, 1)
        rn = {k: Rot(es, nc, "rn_" + k, [128, TW], F32, 2) for k in ("xc", "r", "i", "a", "m", "hs")}
        xcb = Rot(es, nc, "xcb", [128, TW], BF16, 2)
        orT = Rot(es, nc, "orT", [128, 8, TW], BF16, 1)
        oaT = Rot(es, nc, "oaT", [128, 8, TW], BF16, 1)
        mT = Rot(es, nc, "mT", [128, 8, TW], BF16, 1)
        kbuf = Rot(es, nc, "kbuf", [128, 17 * 128], BF16, 2)
        vbuf = Rot(es, nc, "vbuf", [128, 17, 130], BF16, 2)
        ptb = Rot(es, nc, "ptb", [128, 2, TW], BF16, 3)
        obuf = Rot(es, nc, "obuf", [128, 8, 128], F32, 2)
        oab = Rot(es, nc, "oab", [128, D], BF16, 1)
        arec = Rot(es, nc, "arec", [128, 4], F32, 4)
        atmp = Rot(es, nc, "atmp", [128, 128], F32, 2)
        st8 = Rot(es, nc, "st8", [128, 8], F32, 2)
        mtmp = Rot(es, nc, "mtmp", [128, 2, TW], F32, 2)
        yst = vf
        cst = vf
        cstb = xnb
        psc = Rot(es, nc, "psc", [128, 2, 512], F32, 2, psum=True)
        pacc_t = es.enter_context(nc.psum_tensor("pacc", [128, 2, 512], F32))
        pg = Rot(es, nc, "pg", [128, 512], F32, 2, psum=True)
        print("sbuf bytes remaining", nc.sbuf_bytes_remaining)

        for v_t, v_id in zip(vst.t, vst.ids):
            S.add("pool", lambda e, v_t=v_t: e.memset(v_t[:, :, :, 128:129], 1.0), writes=[v_id])
            S.add("pool", lambda e, v_t=v_t: e.memset(v_t[:, :, :, 129:130], 0.0), writes=[v_id])

        def load_w(src, src_id):
            wt, wid = wpool.next()
            S.add("sp", lambda e: e.dma_start(out=wt[:], in_=src.rearrange("(kc p) n -> p kc n", p=128)),
                  reads=[src_id], writes=[wid], dma=wid)
            return wt, wid

        def pgT(p):
            return p[:].bitcast(BF16).rearrange("p (j x) -> p j x", j=8)

        def rmsnorm_stats(src_ap, src_ids, n, scale_div, width_ap_out, junk_id):
            st, sid = st1.next()
            S.add("act", lambda e: e.activation(out=width_ap_out, in_=src_ap, func=AF.Square, accum_out=st[0:n, 0:1]),
                  reads=src_ids, writes=[junk_id, sid])
            S.add("act", lambda e: e.activation(out=st[0:n, 1:2], in_=st[0:n, 0:1], func=AF.Ln, scale=1.0 / scale_div, bias=EPS),
                  reads=[sid], writes=[sid])
            S.add("act", lambda e: e.activation(out=st[0:n, 1:2], in_=st[0:n, 1:2], func=AF.Exp, scale=-0.5),
                  reads=[sid], writes=[sid])
            return st, sid

        def sigmoid_from_psum(p, pid, n_part, ncol, out_ap, out_id, mul_by_x=False, extra=None, extra_id=None):
            et, eid = etm.next()
            S.add("act", lambda e: e.activation(out=et[0:n_part, 0:ncol], in_=p, func=AF.Exp, scale=-1.0),
                  reads=[pid], writes=[eid])
            S.add("dve", lambda e: e.tensor_scalar(out=et[0:n_part, 0:ncol], in0=et[0:n_part, 0:ncol], scalar1=1.0,
                                                   scalar2=None, op0=ALU.add), reads=[eid], writes=[eid])
            if not mul_by_x:
                S.add("dve", lambda e: e.reciprocal(out=out_ap, in_=et[0:n_part, 0:ncol]), reads=[eid], writes=[out_id])
                return
            S.add("dve", lambda e: e.reciprocal(out=et[0:n_part, 0:ncol], in_=et[0:n_part, 0:ncol]), reads=[eid], writes=[eid])
            if extra is None:
                S.add("dve", lambda e: e.tensor_tensor(out=out_ap, in0=p, in1=et[0:n_part, 0:ncol], op=ALU.mult),
                      reads=[pid, eid], writes=[out_id])
            else:
                S.add("dve", lambda e: e.tensor_tensor(out=et[0:n_part, 0:ncol], in0=p, in1=et[0:n_part, 0:ncol], op=ALU.mult),
                      reads=[pid, eid], writes=[eid])
                S.add("pool", lambda e: e.tensor_tensor(
                    out=out_ap.rearrange("p (h x) -> p h x", x=128),
                    in0=et[0:n_part, 0:ncol].rearrange("p (h x) -> p h x", x=128),
                    in1=extra.unsqueeze(1).to_broadcast([n_part, ncol // 128, 128]), op=ALU.mult),
                      reads=[eid, extra_id], writes=[out_id])

        def convert_cache(l):
            for s in range(2):
                for g2 in range(8):
                    ks_t, ks_id = kTst.next()
                    vs_t, vs_id = vst.next()
                    for kk in range(2):
                        kb = 2 * g2 + kk
                        c_t, c_id = cst.next()
                        S.add("sp", lambda e, c_t=c_t, kb=kb, s=s: e.dma_start(out=c_t[:], in_=cache_k[l, s, kb * 128:(kb + 1) * 128, :]),
                              writes=[c_id], dma=c_id)
                        cb_t, cb_id = cstb.next()
                        S.add("dve", lambda e, c_t=c_t, cb_t=cb_t: e.tensor_copy(out=cb_t[:], in_=c_t[:]),
                              reads=[c_id], writes=[cb_id])
                        p, pid = pg.next()
                        for h in range(NH):
                            S.add("pe", lambda e, p=p, cb_t=cb_t, h=h: e.transpose(out=pgT(p)[:, h, :], in_=cb_t[:, h * 128:(h + 1) * 128],
                                                                                 identity=ident[:]),
                                  reads=[cb_id, "ident"], writes=[pid])
                        S.add("act", lambda e, p=p, ks_t=ks_t, kk=kk: e.copy(out=ks_t[:, :, kk * 128:(kk + 1) * 128], in_=pgT(p)),
                              reads=[pid], writes=[ks_id])
                        c2_t, c2_id = cst.next()
                        S.add("sp", lambda e, c2_t=c2_t, kb=kb, s=s: e.dma_start(out=c2_t[:], in_=cache_v[l, s, kb * 128:(kb + 1) * 128, :]),
                              writes=[c2_id], dma=c2_id)
                        S.add("pool", lambda e, c2_t=c2_t, vs_t=vs_t, kk=kk: e.tensor_copy(
                            out=vs_t[:, :, kk, 0:128], in_=c2_t[:].rearrange("p (h x) -> p h x", h=8)),
                            reads=[c2_id], writes=[vs_id])
                    S.add("pool", lambda e, ks_t=ks_t, g2=g2, s=s: e.dma_start(
                        out=kT_s[s].rearrange("h p t -> p h t")[:, :, g2 * 256:(g2 + 1) * 256], in_=ks_t[:]),
                        reads=[ks_id], writes=[("kTs", s)], dma=ks_id)
                    S.add("pool", lambda e, vs_t=vs_t, g2=g2, s=s: e.dma_start(
                        out=v_s[s].rearrange("h p k x -> p h k x")[:, :, 2 * g2:2 * g2 + 2, :], in_=vs_t[:]),
                        reads=[vs_id], writes=[("vs", s)], dma=vs_id)

        def do_tile(l, segs):
            last_layer = (l == depth - 1)
            ncols = sum(sg["n"] for sg in segs)
            blocks = []
            for si, sg in enumerate(segs):
                for bo in range(0, sg["n"], 128):
                    bn = min(128, sg["n"] - bo)
                    blocks.append(dict(si=si, sg=sg, bo=bo, bn=bn, col=sg["col0"] + bo))
            xt_t, xt_id = xnT.next()
            if DBG: print('MARK', segs[0]['kind'], segs[0]['t0'], '# ---- step 1:', len(S.ops))
            for b in blocks:
                sg, bn = b["sg"], b["bn"]
                h_t, h_id = hres.next()
                b["h"], b["hid"] = h_t, h_id
                r0 = sg["t0"] + b["bo"]
                if sg["kind"] in ("m", "p"):
                    hid_src = ("hp", r0 // 128 if sg["kind"] == "p" else "m")
                    if l == 0:
                        src = meta[0:bn, :] if sg["kind"] == "m" else x_p[r0 - NMETA:r0 - NMETA + bn, :]
                    else:
                        src = hb_p[r0:r0 + bn, :]
                else:
                    s = int(sg["kind"][1])
                    hid_src = ("hs", s)
                    src = x_s[s, r0:r0 + bn, :] if l == 0 else hb_s[s, r0:r0 + bn, :]
                b["hsrc"] = hid_src
                S.add("sp", lambda e, h_t=h_t, src=src, bn=bn: e.dma_start(out=h_t[0:bn, :], in_=src),
                      reads=[hid_src], writes=[h_id], dma=h_id)
                xb_t, xb_id = xnb.next()
                st, sid = rmsnorm_stats(h_t[0:bn, :], [h_id], bn, float(D), xb_t[0:bn, :], xb_id)
                S.add("dve", lambda e, xb_t=xb_t, h_t=h_t, st=st, bn=bn: e.tensor_scalar(
                    out=xb_t[0:bn, :], in0=h_t[0:bn, :], scalar1=st[0:bn, 1:2], scalar2=None, op0=ALU.mult),
                    reads=[h_id, sid], writes=[xb_id])
                p, pid = pg.next()
                for j in range(8):
                    S.add("pe", lambda e, p=p, xb_t=xb_t, j=j, bn=bn: e.transpose(
                        out=pgT(p)[:, j, 0:bn], in_=xb_t[0:bn, j * 128:(j + 1) * 128], identity=ident[0:bn, 0:bn]),
                        reads=[xb_id, "ident"], writes=[pid])
                S.add("dve", lambda e, p=p, b=b, bn=bn: e.tensor_tensor(
                    out=xt_t[:, :, b["col"]:b["col"] + bn], in0=pgT(p)[:, :, 0:bn],
                    in1=nwt[:, l, :].unsqueeze(2).to_broadcast([128, 8, bn]), op=ALU.mult),
                    reads=[pid, "nwt"], writes=[(xt_id, b["col"])])
            xt_ids = [(xt_id, b["col"]) for b in blocks]

            if DBG: print('MARK', segs[0]['kind'], segs[0]['t0'], '# ---- step 2:', len(S.ops))
            qT_t, qT_id = qT.next()
            for b in blocks:
                sg, bn = b["sg"], b["bn"]
                b["vf"], b["vfid"] = vf.next()
                b["ga"], b["gaid"] = gatea.next()
                b["ro"], b["roid"] = rot.next()
                b["rb"], b["rbid"] = rotb.next()
                r0 = sg["t0"] + b["bo"]
                rrow = r0 if sg["kind"] in ("m", "p") else TP + r0
                b["rp"], b["rpid"] = ropet.next()
                S.add("sp", lambda e, rp_t=b["rp"], rrow=rrow, bn=bn: e.dma_start(out=rp_t[0:bn], in_=rope[rrow:rrow + bn]),
                      writes=[b["rpid"]], dma=b["rpid"])
            for sg in segs:
                sg["vst"], sg["vstid"] = vst.next()
                sg["kst"], sg["kstid"] = kTst.next()
            for g in range(8):
                wt, wid = load_w(wsc_in[l][:, 2 * D + g * 512: 2 * D + (g + 1) * 512], ("wsc", l, 2 + g // 2))
                for b in blocks:
                    bn = b["bn"]
                    p, pid = pg.next()
                    for kc in range(8):
                        S.add("pe", lambda e, p=p, wt=wt, kc=kc, b=b, bn=bn: e.matmul(
                            p[0:bn, :], lhsT=xt_t[:, kc, b["col"]:b["col"] + bn], rhs=wt[:, kc, :],
                            start=(kc == 0), stop=(kc == 7)), reads=[(xt_id, b["col"]), wid], writes=[pid])
                    if g < 4:
                        rp_t = b["rp"]
                        src = p[0:bn, :].rearrange("p (a c x) -> p a c x", a=8, c=2)
                        x1, x2 = src[:, :, 0, :], src[:, :, 1, :]
                        cosb = rp_t[0:bn, 0:1, :].to_broadcast([bn, 8, 32])
                        sinb = rp_t[0:bn, 1:2, :].to_broadcast([bn, 8, 32])
                        tm_t, tm_id = rtmp.next()
                        tt = [tm_t[0:bn, i4, :].rearrange("p (a x) -> p a x", a=8) for i4 in range(4)]
                        for i4, (xa, tb) in enumerate(((x1, cosb), (x2, sinb), (x2, cosb), (x1, sinb))):
                            S.add("dve", lambda e, o=tt[i4], xa=xa, tb=tb: e.tensor_tensor(out=o, in0=xa, in1=tb, op=ALU.mult),
                                  reads=[pid, b["rpid"]], writes=[(tm_id, i4)])
                        if g < 2:
                            dst = b["rb"][0:bn, g * 512:(g + 1) * 512].rearrange("p (a c x) -> p a c x", a=8, c=2)
                            did = (b["rbid"], g)
                        else:
                            dst = b["ro"][0:bn, (g - 2) * 512:(g - 1) * 512].rearrange("p (a c x) -> p a c x", a=8, c=2)
                            did = (b["roid"], g)
                        S.add("pool", lambda e, dst=dst, tt=tt: e.tensor_tensor(out=dst[:, :, 0, :], in0=tt[0], in1=tt[1], op=ALU.subtract),
                              reads=[(tm_id, 0), (tm_id, 1)], writes=[did])
                        S.add("pool", lambda e, dst=dst, tt=tt: e.tensor_tensor(out=dst[:, :, 1, :], in0=tt[2], in1=tt[3], op=ALU.add),
                              reads=[(tm_id, 2), (tm_id, 3)], writes=[did])
                        if g >= 2:
                            S.add("pool", lambda e, b=b, bn=bn, g=g: e.tensor_copy(
                                out=b["rb"][0:bn, 1024 + (g - 2) * 512:1024 + (g - 1) * 512], in_=b["ro"][0:bn, (g - 2) * 512:(g - 1) * 512]),
                                reads=[did], writes=[(b["rbid"], g)])
                    elif g < 6:
                        S.add("act", lambda e, p=p, b=b, bn=bn, g=g: e.copy(out=b["vf"][0:bn, (g - 4) * 512:(g - 3) * 512], in_=p[0:bn, :]),
                              reads=[pid], writes=[(b["vfid"], g)])
                        sg = b["sg"]
                        kk = b["bo"] // 128
                        S.add("pool", lambda e, b=b, sg=sg, kk=kk, bn=bn, g=g: e.tensor_copy(
                            out=sg["vst"][0:bn, 4 * (g - 4):4 * (g - 3), kk, 0:128],
                            in_=b["vf"][0:bn, (g - 4) * 512:(g - 3) * 512].rearrange("p (h x) -> p h x", h=4)),
                            reads=[(b["vfid"], g)], writes=[sg["vstid"]])
                    else:
                        c0 = (g - 6) * 512
                        sigmoid_from_psum(p[0:bn, :], pid, bn, 512, b["ga"][0:bn, c0:c0 + 512], (b["gaid"], g),
                                          mul_by_x=True, extra=subw[0:bn, l, :], extra_id=("subw", l))
            for b in blocks:
                sg, bn = b["sg"], b["bn"]
                kind = sg["kind"]
                r0 = sg["t0"] + b["bo"]
                rb_t, rb_id = b["rb"], b["rbid"]
                rbids = [(rb_id, g) for g in range(4)]
                if kind in ("m", "p"):
                    kdst, vdst = nk_p[l, r0:r0 + bn, :], nv_p[l, r0:r0 + bn, :]
                else:
                    s = int(kind[1])
                    kdst, vdst = nk_s[l, s, r0:r0 + bn, :], nv_s[l, s, r0:r0 + bn, :]
                S.add("pool", lambda e, kdst=kdst, b=b, bn=bn: e.dma_start(out=kdst, in_=b["ro"][0:bn, :]),
                      reads=[(b["roid"], 2), (b["roid"], 3)], writes=[("out_k", l, kind, r0)], dma=b["roid"])
                S.add("pool", lambda e, vdst=vdst, b=b, bn=bn: e.dma_start(out=vdst, in_=b["vf"][0:bn, :]),
                      reads=[(b["vfid"], 4), (b["vfid"], 5)], writes=[("out_v", l, kind, r0)], dma=b["vfid"])
                for which in range(2):
                    p, pid = pg.next()
                    for h in range(NH):
                        S.add("pe", lambda e, p=p, rb_t=rb_t, h=h, which=which, bn=bn: e.transpose(
                            out=pgT(p)[:, h, 0:bn], in_=rb_t[0:bn, which * 1024 + h * 128: which * 1024 + (h + 1) * 128],
                            identity=ident[0:bn, 0:bn]), reads=rbids + ["ident"], writes=[pid])
                    if which == 0:
                        S.add("act", lambda e, p=p, b=b, bn=bn: e.copy(out=qT_t[:, :, b["col"]:b["col"] + bn], in_=pgT(p)[:, :, 0:bn]),
                              reads=[pid], writes=[(qT_id, b["col"])])
                    else:
                        S.add("act", lambda e, p=p, sg=sg, b=b, bn=bn: e.copy(out=sg["kst"][:, :, b["bo"]:b["bo"] + bn], in_=pgT(p)[:, :, 0:bn]),
                              reads=[pid], writes=[sg["kstid"]])
            if DBG: print('MARK', segs[0]['kind'], segs[0]['t0'], '# store K^T / ', len(S.ops))
            for sg in segs:
                kind, n = sg["kind"], sg["n"]
                if kind == "m":
                    kcol, kb0, kTd, vd, kid, vid_ = 0, 0, kT_p, v_p, "kTp", "vp"
                elif kind == "p":
                    kb0 = 1 + (sg["t0"] - NMETA) // 128
                    kcol, kTd, vd, kid, vid_ = kb0 * 128, kT_p, v_p, "kTp", "vp"
                else:
                    s = int(kind[1])
                    kb0, kcol, kTd, vd, kid, vid_ = 16, 2048, kT_s[s], v_s[s], ("kTs", s), ("vs", s)
                sg["kb0"] = kb0
                nkb = (n + 127) // 128
                pn = min(n, 128)
                S.add("pool", lambda e, sg=sg, kTd=kTd, kcol=kcol, n=n: e.dma_start(
                    out=kTd.rearrange("h p t -> p h t")[:, :, kcol:kcol + n], in_=sg["kst"][:, :, 0:n]),
                    reads=[sg["kstid"]], writes=[kid], dma=sg["kstid"])
                S.add("pool", lambda e, sg=sg, vd=vd, kb0=kb0, nkb=nkb, pn=pn: e.dma_start(
                    out=vd.rearrange("h p k x -> p h k x")[0:pn, :, kb0:kb0 + nkb, :], in_=sg["vst"][0:pn, :, 0:nkb, :]),
                    reads=[sg["vstid"]], writes=[vid_], dma=sg["vstid"])

            if DBG: print('MARK', segs[0]['kind'], segs[0]['t0'], '# ---- step 3:', len(S.ops))
            xr_t, xr_id = xr.next()
            szr_t, szr_id = szr.next()
            sgr_t, sgr_id = sgr.next()
            sga_t, sga_id = sga.next()
            for si, sg in enumerate(segs):
                sg["xb"] = sg["col0"] + 3 * si
            for gg in range(8):
                colbase = gg * 512 if gg < 4 else 6 * D + (gg - 4) * 512
                wt, wid = load_w(wsc_in[l][:, colbase:colbase + 512], ("wsc", l, colbase // D))
                for c4 in range(4):
                    j = (gg % 2) * 4 + c4
                    p, pid = pg.next()
                    for kc in range(8):
                        S.add("pe", lambda e, p=p, wt=wt, kc=kc, c4=c4: e.matmul(
                            p[:, 0:ncols], lhsT=wt[:, kc, c4 * 128:(c4 + 1) * 128], rhs=xt_t[:, kc, 0:ncols],
                            start=(kc == 0), stop=(kc == 7)), reads=xt_ids + [wid], writes=[pid])
                    if gg < 2:
                        for sg in segs:
                            S.add("act", lambda e, p=p, sg=sg, j=j: e.copy(
                                out=xr_t[:, j, sg["xb"] + 3:sg["xb"] + 3 + sg["n"]], in_=p[:, sg["col0"]:sg["col0"] + sg["n"]]),
                                reads=[pid], writes=[(xr_id, j)])
                    elif gg < 4:
                        sigmoid_from_psum(p[:, 0:ncols], pid, 128, ncols, szr_t[:, j, 0:ncols], (szr_id, j), mul_by_x=True)
                    elif gg < 6:
                        sigmoid_from_psum(p[:, 0:ncols], pid, 128, ncols, sgr_t[:, j, 0:ncols], (sgr_id, j))
                    else:
                        sigmoid_from_psum(p[:, 0:ncols], pid, 128, ncols, sga_t[:, j, 0:ncols], (sga_id, j))

            if DBG: print('MARK', segs[0]['kind'], segs[0]['t0'], '# ---- step 5 ', len(S.ops))
            oaT_t, oaT_id = oaT.next()
            assert len(segs) == 1
            sg = segs[0]
            if True:
                kind, n, col0 = sg["kind"], sg["n"], sg["col0"]
                if kind == "m":
                    kbs = [(0, 16, 0, False)]
                    kTd, vd, kid, vid_ = kT_p, v_p, "kTp", "vp"
                elif kind == "p":
                    f0 = sg["t0"] - NMETA
                    kbs = [(0, 16, 0, False)] + [(kb, 128, 0, False) for kb in range(1, 1 + f0 // 128)]
                    for m in range(n // 128):
                        kbs.append((1 + f0 // 128 + m, 128, 128 * m, True))
                    kTd, vd, kid, vid_ = kT_p, v_p, "kTp", "vp"
                else:
                    s = int(kind[1])
                    kbs = [(kb, 128, 0, False) for kb in range(16)] + [(16, 64, 0, False)]
                    kTd, vd, kid, vid_ = kT_s[s], v_s[s], ("kTs", s), ("vs", s)
                nkb_all = kbs[-1][0] + 1
                nqb = (n + 127) // 128
                qbn = [min(128, n - 128 * q) for q in range(nqb)]
                ob = []
                for q in range(nqb):
                    o_t, o_id = obuf.next()
                    ob.append((o_t, o_id))
                def attn_head(h):
                    for q in range(nqb):
                        S.add("pe", lambda e, q=q, qn=qbn[q]: e.matmul(pacc_t[0:qn, q, :], lhsT=zeros[:, 0:qn], rhs=zeros[:, 0:512],
                                                                     start=True, stop=True), reads=["zeros"], writes=[("pacc", q)])
                    for lo in (0, 17):
                        part = [x for x in kbs if lo <= x[0] < lo + 17]
                        if not part:
                            continue
                        npk = part[-1][0] - lo + 1
                        kb_t, kb_id = kbuf.next()
                        vb_t, vb_id = vbuf.next()
                        S.add("sp", lambda e, kb_t=kb_t, kTd=kTd, h=h, lo=lo, npk=npk: e.dma_start(
                            out=kb_t[:, 0:npk * 128], in_=kTd[h, :, lo * 128:(lo + npk) * 128]), reads=[kid], writes=[kb_id], dma=kb_id)
                        S.add("sp", lambda e, vb_t=vb_t, vd=vd, h=h, lo=lo, npk=npk: e.dma_start(
                            out=vb_t[:, 0:npk, :], in_=vd[h, :, lo:lo + npk, :]), reads=[vid_], writes=[vb_id], dma=vb_id)
                        for (kb, kk, qlo, diag) in part:
                            nqc = n - qlo
                            kl = kb - lo
                            ps_t, ps_id = psc.next()
                            for c in range(2):
                                S.add("pe", lambda e, ps_t=ps_t, kb_t=kb_t, c=c, kl=kl, kk=kk, qlo=qlo, nqc=nqc, h=h, col0=col0, n=n: e.matmul(
                                    ps_t[0:kk, c, 0:nqc], lhsT=kb_t[64 * c:64 * c + 64, kl * 128:kl * 128 + kk],
                                    rhs=qT_t[64 * c:64 * c + 64, h, col0 + qlo:col0 + n], start=True, stop=True),
                                    reads=[kb_id] + [(qT_id, bb["col"]) for bb in blocks if bb["sg"] is sg], writes=[ps_id])
                            pt_t, pt_id = ptb.next()
                            S.add("act", lambda e, pt_t=pt_t, ps_t=ps_t, kk=kk, nqc=nqc: e.activation(
                                out=pt_t[0:kk, :, 0:nqc], in_=ps_t[0:kk, :, 0:nqc], func=AF.Exp, scale=0.125),
                                reads=[ps_id], writes=[pt_id])
                            if diag:
                                S.add("pool", lambda e, pt_t=pt_t: e.memset(pt_t[64:128, :, 0:64], 0.0),
                                      reads=[pt_id], writes=[pt_id])
                            for q in range(qlo // 128, nqb):
                                for c in range(2):
                                    S.add("pe", lambda e, pt_t=pt_t, vb_t=vb_t, q=q, c=c, kk=kk, kl=kl, qlo=qlo, qn=qbn[q]: e.matmul(
                                        pacc_t[0:qn, q, c * 129:(c + 1) * 129], lhsT=pt_t[0:kk, c, 128 * q - qlo:128 * q - qlo + qn],
                                        rhs=vb_t[0:kk, kl, 0:129], start=False, stop=True, skip_group_check=True),
                                        reads=[pt_id, vb_id], writes=[("pacc", q)])
                    for q in range(nqb):
                        nq = qbn[q]
                        o_t, o_id = ob[q]
                        ar_t, ar_id = arec.next()
                        acc = pacc_t[0:nq, q, 0:258].rearrange("p (c x) -> p c x", c=2)
                        S.add("dve", lambda e, ar_t=ar_t, acc=acc, nq=nq: e.reciprocal(out=ar_t[0:nq, 0:2], in_=acc[:, :, 128]),
                              reads=[("pacc", q)], writes=[ar_id])
                        S.add("dve", lambda e, ar_t=ar_t, nq=nq: e.tensor_tensor(out=ar_t[0:nq, 2:3], in0=ar_t[0:nq, 1:2],
                                                                              in1=neglam[0:nq, l:l + 1], op=ALU.mult),
                              reads=[ar_id, ("neglam", l)], writes=[ar_id])
                        at_t, at_id = atmp.next()
                        S.add("dve", lambda e, at_t=at_t, acc=acc, ar_t=ar_t, nq=nq: e.tensor_scalar(
                            out=at_t[0:nq, :], in0=acc[:, 1, 0:128], scalar1=ar_t[0:nq, 2:3], scalar2=None, op0=ALU.mult),
                            reads=[("pacc", q), ar_id], writes=[at_id])
                        S.add("dve", lambda e, o_t=o_t, at_t=at_t, acc=acc, ar_t=ar_t, nq=nq, h=h: e.scalar_tensor_tensor(
                            out=o_t[0:nq, h, :], in0=acc[:, 0, 0:128], scalar=ar_t[0:nq, 0:1], in1=at_t[0:nq, :],
                            op0=ALU.mult, op1=ALU.add), reads=[("pacc", q), ar_id, at_id], writes=[(o_id, h)])
                def attn_post():
                    sblocks = [bb for bb in blocks if bb["sg"] is sg]
                    for q in range(nqb):
                        nq = qbn[q]
                        o_t, o_id = ob[q]
                        bb = sblocks[q]
                        oids = [(o_id, h) for h in range(NH)]
                        s8_t, s8_id = st8.next()
                        osq_t, osq_id = rot.next()
                        osq = osq_t[:].rearrange("p (h x) -> p h x", h=8)
                        S.add("pool", lambda e, o_t=o_t, nq=nq, osq=osq: e.tensor_tensor(out=osq[0:nq], in0=o_t[0:nq], in1=o_t[0:nq], op=ALU.mult),
                              reads=oids, writes=[osq_id])
                        S.add("dve", lambda e, s8_t=s8_t, nq=nq, osq=osq: e.tensor_reduce(out=s8_t[0:nq, :], in_=osq[0:nq], axis=AX.X, op=ALU.add),
                              reads=[osq_id], writes=[s8_id])
                        S.add("act", lambda e, s8_t=s8_t, nq=nq: e.activation(out=s8_t[0:nq, :], in_=s8_t[0:nq, :], func=AF.Ln,
                                                                            scale=1.0 / 128, bias=EPS), reads=[s8_id], writes=[s8_id])
                        S.add("act", lambda e, s8_t=s8_t, nq=nq: e.activation(out=s8_t[0:nq, :], in_=s8_t[0:nq, :], func=AF.Exp, scale=-0.5),
                              reads=[s8_id], writes=[s8_id])
                        S.add("dve", lambda e, o_t=o_t, s8_t=s8_t, nq=nq: e.tensor_tensor(
                            out=o_t[0:nq], in0=o_t[0:nq], in1=s8_t[0:nq, :].unsqueeze(2).to_broadcast([nq, 8, 128]), op=ALU.mult),
                            reads=oids + [s8_id], writes=oids)
                        oa_t, oa_id = oab.next()
                        S.add("pool", lambda e, oa_t=oa_t, o_t=o_t, bb=bb, nq=nq: e.tensor_tensor(
                            out=oa_t[0:nq, :], in0=o_t[0:nq].rearrange("p h x -> p (h x)"), in1=bb["ga"][0:nq, :], op=ALU.mult),
                            reads=oids + [(bb["gaid"], 6), (bb["gaid"], 7)], writes=[oa_id])
                        p, pid = pg.next()
                        for j in range(8):
                            S.add("pe", lambda e, p=p, oa_t=oa_t, j=j, nq=nq: e.transpose(
                                out=pgT(p)[:, j, 0:nq], in_=oa_t[0:nq, j * 128:(j + 1) * 128], identity=ident[0:nq, 0:nq]),
                                reads=[oa_id, "ident"], writes=[pid])
                        S.add("act", lambda e, p=p, bb=bb, nq=nq: e.copy(out=oaT_t[:, :, bb["col"]:bb["col"] + nq], in_=pgT(p)[:, :, 0:nq]),
                              reads=[pid], writes=[(oaT_id, bb["col"])])

            orT_t, orT_id = orT.next()
            if True:
                kind, n, col0, xb = sg["kind"], sg["n"], sg["col0"], sg["xb"]
                ck = "p" if kind in ("m", "p") else kind
                hc = hcar[ck]
                if kind == "m":
                    S.add("pool", lambda e, xb=xb: e.memset(xr_t[:, :, xb:xb + 3], 0.0), writes=[(xr_id, "halo", xb)])
                    S.add("pool", lambda e, hc=hc: e.memset(hc[:], 0.0), writes=[("hcar", ck)])
                elif kind == "p":
                    S.add("pool", lambda e, xb=xb: e.tensor_copy(out=xr_t[:, :, xb:xb + 3], in_=xhalo[:]),
                          reads=["xhalo"], writes=[(xr_id, "halo", xb)])
                else:
                    s = int(kind[1])
                    for k3 in range(3):
                        S.add("sp", lambda e, xb=xb, s=s, k3=k3: e.dma_start(out=xr_t[:, :, xb + k3],
                                                                           in_=st_conv[l, s, k3].rearrange("(j p) -> p j", p=128),
                                                                           allow_slow_non_contiguous=True),
                              writes=[(xr_id, "halo", xb)], dma=(xr_id, "halo", xb))
                    S.add("sp", lambda e, s=s, hc=hc: e.dma_start(out=hc[:], in_=st_rnn[l, s].rearrange("(j p) -> p j", p=128),
                                                                allow_slow_non_contiguous=True),
                          writes=[("hcar", ck)], dma=("hcar", ck))
                def rnn_chunk(j):
                    xc_t, xc_id = rn["xc"].next()
                    xin = [(xr_id, j), (xr_id, "halo", xb)]
                    S.add("dve", lambda e, xc_t=xc_t, j=j, xb=xb, n=n: e.tensor_scalar(
                        out=xc_t[:, 0:n], in0=xr_t[:, j, xb:xb + n], scalar1=cwt[:, l, 0, j:j + 1], scalar2=cbt[:, l, j:j + 1],
                        op0=ALU.mult, op1=ALU.add), reads=xin + ["cwt", "cbt"], writes=[xc_id])
                    for k in range(1, 4):
                        S.add("dve", lambda e, xc_t=xc_t, j=j, xb=xb, n=n, k=k: e.scalar_tensor_tensor(
                            out=xc_t[:, 0:n], in0=xr_t[:, j, xb + k:xb + k + n], scalar=cwt[:, l, k, j:j + 1], in1=xc_t[:, 0:n],
                            op0=ALU.mult, op1=ALU.add), reads=xin + ["cwt", xc_id], writes=[xc_id])
                    xcb_t, xcb_id = xcb.next()
                    S.add("pool", lambda e, xcb_t=xcb_t, xc_t=xc_t, n=n: e.tensor_copy(out=xcb_t[:, 0:n], in_=xc_t[:, 0:n]),
                          reads=[xc_id], writes=[xcb_id])
                    gates = []
                    for gi, nb_ in ((0, nbrg), (1, nbig)):
                        p, pid = pg.next()
                        S.add("pe", lambda e, p=p, xcb_t=xcb_t, gi=gi, j=j, n=n: e.matmul(
                            p[:, 0:n], lhsT=wg[:, gi, j, :], rhs=xcb_t[:, 0:n], start=True, stop=True),
                            reads=[xcb_id, ("wg", gi)], writes=[pid])
                        g_t, g_id = rn["r" if gi == 0 else "i"].next()
                        S.add("act", lambda e, g_t=g_t, p=p, nb_=nb_, j=j, n=n: e.activation(
                            out=g_t[:, 0:n], in_=p[:, 0:n], func=AF.Exp, scale=-1.0, bias=nb_[:, l, j:j + 1]),
                            reads=[pid, "nbrg", "nbig"], writes=[g_id])
                        S.add("dve", lambda e, g_t=g_t, n=n: e.tensor_scalar(out=g_t[:, 0:n], in0=g_t[:, 0:n], scalar1=1.0, scalar2=None,
                                                                           op0=ALU.add), reads=[g_id], writes=[g_id])
                        S.add("dve", lambda e, g_t=g_t, n=n: e.reciprocal(out=g_t[:, 0:n], in_=g_t[:, 0:n]), reads=[g_id], writes=[g_id])
                        gates.append((g_t, g_id))
                    (r_t, r_id), (i_t, i_id) = gates
                    a_t, a_id = rn["a"].next()
                    m_t, m_id = rn["m"].next()
                    S.add("act", lambda e, a_t=a_t, r_t=r_t, j=j, n=n: e.activation(out=a_t[:, 0:n], in_=r_t[:, 0:n], func=AF.Exp,
                                                                                 scale=cch[:, l, j:j + 1]), reads=[r_id, "cch"], writes=[a_id])
                    S.add("act", lambda e, a_t=a_t, m_t=m_t, n=n: e.activation(out=m_t[:, 0:n], in_=a_t[:, 0:n], func=AF.Square),
                          reads=[a_id], writes=[m_id])
                    S.add("act", lambda e, m_t=m_t, n=n: e.activation(out=m_t[:, 0:n], in_=m_t[:, 0:n], func=AF.Ln, scale=-1.0, bias=1.0),
                          reads=[m_id], writes=[m_id])
                    S.add("act", lambda e, m_t=m_t, n=n: e.activation(out=m_t[:, 0:n], in_=m_t[:, 0:n], func=AF.Exp, scale=0.5),
                          reads=[m_id], writes=[m_id])
                    S.add("dve", lambda e, m_t=m_t, i_t=i_t, n=n: e.tensor_tensor(out=m_t[:, 0:n], in0=m_t[:, 0:n], in1=i_t[:, 0:n], op=ALU.mult),
                          reads=[m_id, i_id], writes=[m_id])
                    S.add("dve", lambda e, m_t=m_t, xc_t=xc_t, n=n: e.tensor_tensor(out=m_t[:, 0:n], in0=m_t[:, 0:n], in1=xc_t[:, 0:n], op=ALU.mult),
                          reads=[m_id, xc_id], writes=[m_id])
                    hs_t, hs_id = rn["hs"].next()
                    S.add("dve", lambda e, hs_t=hs_t, a_t=a_t, m_t=m_t, j=j, n=n, hc=hc: e.tensor_tensor_scan(
                        out=hs_t[:, 0:n], data0=a_t[:, 0:n], data1=m_t[:, 0:n], initial=hc[:, j:j + 1], op0=ALU.mult, op1=ALU.add),
                        reads=[a_id, m_id, ("hcar", ck)], writes=[hs_id])
                    S.add("pool", lambda e, hs_t=hs_t, j=j, n=n, hc=hc: e.tensor_copy(out=hc[:, j:j + 1], in_=hs_t[:, n - 1:n]),
                          reads=[hs_id], writes=[("hcar", ck)])
                    S.add("dve", lambda e, hs_t=hs_t, j=j, n=n, col0=col0: e.tensor_tensor(
                        out=orT_t[:, j, col0:col0 + n], in0=hs_t[:, 0:n], in1=szr_t[:, j, col0:col0 + n], op=ALU.mult),
                        reads=[hs_id, (szr_id, j)], writes=[(orT_id, j)])
                def rnn_post():
                    xall = [(xr_id, j) for j in range(8)] + [(xr_id, "halo", xb)]
                    if kind in ("m", "p") and not sg["last"]:
                        S.add("pool", lambda e, xb=xb, n=n: e.tensor_copy(out=xhalo[:], in_=xr_t[:, :, xb + n:xb + n + 3]),
                              reads=xall, writes=["xhalo"])
                    if sg["last"]:
                        if kind == "p":
                            cdst, rdst = nc_p[l], nr_p[l]
                        else:
                            s = int(kind[1])
                            cdst, rdst = nc_s[l, s], nr_s[l, s]
                        for k3 in range(3):
                            S.add("pool", lambda e, cdst=cdst, xb=xb, n=n, k3=k3: e.dma_start(
                                out=cdst[k3].rearrange("(j p) -> p j", p=128), in_=xr_t[:, :, xb + n + k3], allow_slow_non_contiguous=True),
                                reads=xall, writes=[("out_c", l, kind, k3)], dma=(xr_id, "o"))
                        S.add("pool", lambda e, rdst=rdst, hc=hc: e.dma_start(
                            out=rdst.rearrange("(j p) -> p j", p=128), in_=hc[:], allow_slow_non_contiguous=True),
                            reads=[("hcar", ck)], writes=[("out_r", l, kind)], dma=("hcar", ck, "o"))

            for h in range(NH):
                attn_head(h)
                rnn_chunk(h)
            attn_post()
            rnn_post()
            oaT_ids = [(oaT_id, bb["col"]) for bb in blocks]
            orT_ids = [(orT_id, j) for j in range(8)]

            if DBG: print('MARK', segs[0]['kind'], segs[0]['t0'], '# ---- step 6:', len(S.ops))
            mT_t, mT_id = mT.next()
            for hf in range(2):
                wr_t, wr_id = load_w(wsc_p[l, 0][:, hf * 512:(hf + 1) * 512], ("wscp", l, 0))
                wa_t, wa_id = load_w(wsc_p[l, 1][:, hf * 512:(hf + 1) * 512], ("wscp", l, 1))
                for c4 in range(4):
                    j = hf * 4 + c4
                    pr_, prid = pg.next()
                    for kc in range(8):
                        S.add("pe", lambda e, pr_=pr_, wr_t=wr_t, kc=kc, c4=c4: e.matmul(
                            pr_[:, 0:ncols], lhsT=wr_t[:, kc, c4 * 128:(c4 + 1) * 128], rhs=orT_t[:, kc, 0:ncols],
                            start=(kc == 0), stop=(kc == 7)), reads=orT_ids + [wr_id], writes=[prid])
                    pa_, paid = pg.next()
                    for kc in range(8):
                        S.add("pe", lambda e, pa_=pa_, wa_t=wa_t, kc=kc, c4=c4: e.matmul(
                            pa_[:, 0:ncols], lhsT=wa_t[:, kc, c4 * 128:(c4 + 1) * 128], rhs=oaT_t[:, kc, 0:ncols],
                            start=(kc == 0), stop=(kc == 7)), reads=oaT_ids + [wa_id], writes=[paid])
                    mt_t, mt_id = mtmp.next()
                    S.add("dve", lambda e, mt_t=mt_t, pr_=pr_, j=j: e.tensor_tensor(out=mt_t[:, 0, 0:ncols], in0=pr_[:, 0:ncols],
                                                                                 in1=sgr_t[:, j, 0:ncols], op=ALU.mult),
                          reads=[prid, (sgr_id, j)], writes=[(mt_id, 0)])
                    S.add("dve", lambda e, mt_t=mt_t, pa_=pa_, j=j: e.tensor_tensor(out=mt_t[:, 1, 0:ncols], in0=pa_[:, 0:ncols],
                                                                                 in1=sga_t[:, j, 0:ncols], op=ALU.mult),
                          reads=[paid, (sga_id, j)], writes=[(mt_id, 1)])
                    S.add("pool", lambda e, mt_t=mt_t, j=j: e.tensor_tensor(out=mT_t[:, j, 0:ncols], in0=mt_t[:, 0, 0:ncols],
                                                                          in1=mt_t[:, 1, 0:ncols], op=ALU.add),
                          reads=[(mt_id, 0), (mt_id, 1)], writes=[(mT_id, j)])
            mT_ids = [(mT_id, j) for j in range(8)]
            for hf in range(2):
                wo_t, wo_id = load_w(wsc_p[l, 2][:, hf * 512:(hf + 1) * 512], ("wscp", l, 2))
                for b in blocks:
                    bn = b["bn"]
                    p, pid = pg.next()
                    for kc in range(8):
                        S.add("pe", lambda e, p=p, wo_t=wo_t, kc=kc, b=b, bn=bn: e.matmul(
                            p[0:bn, :], lhsT=mT_t[:, kc, b["col"]:b["col"] + bn], rhs=wo_t[:, kc, :],
                            start=(kc == 0), stop=(kc == 7)), reads=mT_ids + [wo_id], writes=[pid])
                    S.add("dve", lambda e, p=p, b=b, bn=bn, hf=hf: e.tensor_tensor(
                        out=b["h"][0:bn, hf * 512:(hf + 1) * 512], in0=b["h"][0:bn, hf * 512:(hf + 1) * 512], in1=p[0:bn, :], op=ALU.add),
                        reads=[pid, b["hid"]], writes=[b["hid"]])
            for b in blocks:
                sg, bn = b["sg"], b["bn"]
                kind = sg["kind"]
                r0 = sg["t0"] + b["bo"]
                if not last_layer:
                    dst = hb_p[r0:r0 + bn, :] if kind in ("m", "p") else hb_s[int(kind[1]), r0:r0 + bn, :]
                    S.add("pool", lambda e, dst=dst, b=b, bn=bn: e.dma_start(out=dst, in_=b["h"][0:bn, :]),
                          reads=[b["hid"]], writes=[b["hsrc"]], dma=(b["hid"], "o"))
                elif kind != "m":
                    jk_t, jk_id = xnb.next()
                    st, sid = rmsnorm_stats(b["h"][0:bn, :], [b["hid"]], bn, float(D), jk_t[0:bn, :], jk_id)
                    y_t, y_id = yst.next()
                    S.add("dve", lambda e, y_t=y_t, b=b, st=st, bn=bn: e.scalar_tensor_tensor(
                        out=y_t[0:bn, :], in0=b["h"][0:bn, :], scalar=st[0:bn, 1:2], in1=fnwt[0:bn, :], op0=ALU.mult, op1=ALU.mult),
                        reads=[b["hid"], sid, "fnwt"], writes=[y_id])
                    dst = y_p[r0 - NMETA:r0 - NMETA + bn, :] if kind == "p" else y_s[int(kind[1]), r0:r0 + bn, :]
                    S.add("pool", lambda e, dst=dst, y_t=y_t, bn=bn: e.dma_start(out=dst, in_=y_t[0:bn, :]),
                          reads=[y_id], writes=[("out_y", kind, r0)], dma=y_id)

        for l in range(depth):
            if l + 1 < depth:
                convert_weights(l + 1)
            load_gate_w(l)
            if not NOCACHE:
                convert_cache(l)
            do_tile(l, [dict(kind="s0", col0=0, n=64, t0=0, last=True)])
            do_tile(l, [dict(kind="s1", col0=0, n=64, t0=0, last=True)])
            do_tile(l, [dict(kind="m", col0=0, n=16, t0=0, last=False)])
            for ti in range(ntile):
                do_tile(l, [dict(kind="p", col0=0, n=TT, t0=NMETA + ti * TT, last=(ti == ntile - 1))])

        S.finalize()
        print("ops", len(S.ops), "sems", S.n_sems)
        with nc.allow_low_precision(reason="bf16 matmul operands by design"), nc.Block() as block:
            S.emit(block)
    return nc


_ROPE = None


def _rope_table():
    global _ROPE
    if _ROPE is None:
        half = 32
        inv = 1.0 / (10000.0 ** (np.arange(half, dtype=np.float32) / np.float32(half)))
        pos = np.concatenate([np.arange(TP), NMETA + PAST + np.arange(DSEQ)]).astype(np.float32)
        ang = pos[:, None] * inv[None, :].astype(np.float32)
        _ROPE = np.ascontiguousarray(np.stack([np.cos(ang), np.sin(ang)], axis=1).astype(np.float32))
    return _ROPE


def kernel(x_prompt, x_sample, cache_k, cache_v, state_conv, state_rnn, meta_tokens,
           norm_w, w_in, conv_w, conv_b, w_rg, b_rg, w_ig, b_ig, lru_lambda,
           lambda_q1, lambda_k1, lambda_q2, lambda_k2, subln_w,
           w_proj_rnn, w_proj_att, w_out, final_norm_w):
    A = lambda a: np.ascontiguousarray(np.asarray(a, dtype=np.float32))
    nc = build_program()
    shared = dict(meta=A(meta_tokens), norm_w=A(norm_w), w_in=A(w_in), conv_w=A(conv_w), conv_b=A(conv_b),
                  w_rg=A(w_rg), b_rg=A(b_rg), w_ig=A(w_ig), b_ig=A(b_ig), lru_l=A(lru_lambda),
                  lq1=A(lambda_q1), lk1=A(lambda_k1), lq2=A(lambda_q2), lk2=A(lambda_k2), subln=A(subln_w),
                  w_pr=A(w_proj_rnn), w_pa=A(w_proj_att), w_out=A(w_out), fnw=A(final_norm_w), rope=_rope_table())
    x_prompt, x_sample = np.asarray(x_prompt), np.asarray(x_sample)
    cache_k, cache_v = np.asarray(cache_k), np.asarray(cache_v)
    state_conv, state_rnn = np.asarray(state_conv), np.asarray(state_rnn)
    in_maps = []
    for c in range(8):
        m = dict(shared)
        m["x_p"] = A(x_prompt[c])
        m["x_s"] = A(x_sample[2 * c:2 * c + 2])
        m["cache_k"] = A(cache_k[:, 2 * c:2 * c + 2].reshape(DEPTH, 2, PAST, D))
        m["cache_v"] = A(cache_v[:, 2 * c:2 * c + 2].reshape(DEPTH, 2, PAST, D))
        m["st_conv"] = A(state_conv[:, 2 * c:2 * c + 2])
        m["st_rnn"] = A(state_rnn[:, 2 * c:2 * c + 2])
        in_maps.append(m)
    res = run_bass_kernel_spmd(nc, in_maps, core_ids=list(range(8)))
    R = res.results
    y_p = np.stack([R[c]["y_p"] for c in range(8)])
    y_s = np.concatenate([R[c]["y_s"] for c in range(8)], axis=0)
    nk_p = np.stack([R[c]["nk_p"] for c in range(8)], axis=1).reshape(DEPTH, 8, TP, NH, 2, 64)
    nv_p = np.stack([R[c]["nv_p"] for c in range(8)], axis=1).reshape(DEPTH, 8, TP, NH, 128)
    nc_p = np.stack([R[c]["nc_p"] for c in range(8)], axis=1)
    nr_p = np.stack([R[c]["nr_p"] for c in range(8)], axis=1)
    nk_s = np.concatenate([R[c]["nk_s"] for c in range(8)], axis=1).reshape(DEPTH, 16, DSEQ, NH, 2, 64)
    nv_s = np.concatenate([R[c]["nv_s"] for c in range(8)], axis=1).reshape(DEPTH, 16, DSEQ, NH, 128)
    nc_s = np.concatenate([R[c]["nc_s"] for c in range(8)], axis=1)
    nr_s = np.concatenate([R[c]["nr_s"] for c in range(8)], axis=1)
    return (y_p, y_s, nk_p, nv_p, nc_p, nr_p, nk_s, nv_s, nc_s, nr_s)
```

```python
import contextlib
import math
import numpy as np
import concourse.bass as bass
import concourse.mybir as mybir
from concourse.bass_utils import run_bass_kernel_spmd

F32 = mybir.dt.float32
BF16 = mybir.dt.bfloat16
AF = mybir.ActivationFunctionType
ALU = mybir.AluOpType
AX = mybir.AxisListType

D = 1024
DEPTH = 4
SEQ = 4096
NMETA = 16
TP = SEQ + NMETA
DSEQ = 64
PAST = 2048
NH = 8
EPS = 1e-6
TT = 256
NW = 3
NTILE = SEQ // TT
KCOLS_P = 33 * 128
KCOLS_S = 17 * 128
SEM_ROT = 30000
DBG = False
NOCACHE = False


class _Chan:
    def __init__(self, nc, name):
        self.nc, self.name, self.sems, self.cur = nc, name, [], 0

    def bump(self, units):
        if not self.sems or self.cur + units > SEM_ROT:
            self.sems.append(self.nc.alloc_semaphore(f"{self.name}_{len(self.sems)}"))
            self.cur = 0
        self.cur += units
        return (self.sems[-1], self.cur)


class Sched:
    ENG = ("pe", "act", "dve", "pool", "sp")

    def __init__(self, nc):
        self.nc, self.ops, self.state = nc, [], {}

    limit = None
    bases = set()

    def _split(self, b):
        if isinstance(b, tuple) and isinstance(b[0], str) and b[0] in self.bases:
            return b[0], b
        return b, None

    def add(self, eng, fn, reads=(), writes=(), dma=None, grp=False):
        if self.limit is not None and len(self.ops) >= self.limit:
            return -1
        deps, raw = set(), set()
        st = self.state
        for b in reads:
            base, part = self._split(b)
            e = st.setdefault(base, [None, [], {}])
            ws = [e[0]]
            if part is None:
                ws += [pv[0] for pv in e[2].values()]
            elif part in e[2]:
                ws.append(e[2][part][0])
            for w in ws:
                if w is not None:
                    deps.add(w)
                    raw.add(w)
        for b in writes:
            base, part = self._split(b)
            e = st.setdefault(base, [None, [], {}])
            if e[0] is not None:
                deps.add(e[0])
            deps.update(e[1])
            if part is None:
                for pv in e[2].values():
                    deps.add(pv[0])
                    deps.update(pv[1])
            elif part in e[2]:
                deps.add(e[2][part][0])
                deps.update(e[2][part][1])
        i = len(self.ops)
        deps.discard(None)
        if dma is not None:
            dma = (eng, dma)
        self.ops.append(dict(eng=eng, fn=fn, deps=deps, raw=raw, dma=dma, mark=False, grp=grp))
        for b in reads:
            base, part = self._split(b)
            e = st[base]
            if part is None:
                e[1].append(i)
            else:
                e[2].setdefault(part, [None, []])[1].append(i)
        for b in writes:
            base, part = self._split(b)
            e = st[base]
            if part is None:
                e[0], e[1], e[2] = i, [], {}
            else:
                e[2][part] = [i, []]
        return i

    def finalize(self):
        nc, ops = self.nc, self.ops
        for op in ops:
            need = []
            for d in op["deps"]:
                p = ops[d]
                if p["dma"] is not None or p["eng"] != op["eng"]:
                    need.append(d)
                elif p["eng"] != "pe" and d in op["raw"]:
                    need.append(d)
            op["need"] = need
            for d in need:
                ops[d]["mark"] = True
        chans = {e: _Chan(nc, "c_" + e) for e in self.ENG}
        dchan = {}
        for op in ops:
            if op["dma"] is not None:
                key = op["dma"]
                if key not in dchan:
                    dchan[key] = _Chan(nc, f"dma{len(dchan)}")
                op["inc"] = dchan[key].bump(16)
            elif op["mark"]:
                op["inc"] = chans[op["eng"]].bump(1)
            else:
                op["inc"] = None
        gfinal = {}
        for op in ops:
            if op["grp"]:
                gfinal[op["dma"]] = op["inc"]
        seen = {e: {} for e in self.ENG}
        for op in ops:
            w = {}
            for d in op["need"]:
                sem, val = gfinal[ops[d]["dma"]] if ops[d]["grp"] else ops[d]["inc"]
                k = id(sem)
                if seen[op["eng"]].get(k, 0) >= val:
                    continue
                if k not in w or w[k][1] < val:
                    w[k] = (sem, val)
            for k, sv in w.items():
                seen[op["eng"]][k] = sv[1]
            op["waits"] = list(w.values())
        self.dchan = dchan
        self.n_sems = sum(len(c.sems) for c in chans.values()) + sum(len(c.sems) for c in dchan.values())

    def emit(self, block):
        ops = self.ops

        def run(engname, e):
            for op in ops:
                if op["eng"] != engname:
                    continue
                for sem, val in op["waits"]:
                    e.wait_ge(sem, val)
                ins = op["fn"](e)
                if op["inc"] is not None:
                    ins.then_inc(op["inc"][0], 16 if op["dma"] is not None else 1)

        @block.tensor
        def _(e):
            run("pe", e)

        @block.scalar
        def _(e):
            run("act", e)

        @block.vector
        def _(e):
            run("dve", e)

        @block.gpsimd
        def _(e):
            run("pool", e)

        @block.sync
        def _(e):
            run("sp", e)
            for c in self.dchan.values():
                e.wait_ge(c.sems[-1], c.cur)


class Rot:
    def __init__(self, es, nc, name, shape, dt, n, psum=False):
        mk = nc.psum_tensor if psum else nc.sbuf_tensor
        self.t = [es.enter_context(mk(f"{name}{i}", shape, dt)) for i in range(n)]
        self.ids = [f"{name}{i}" for i in range(n)]
        Sched.bases.update(self.ids)
        self.k = 0

    def next(self):
        i = self.k % len(self.t)
        self.k += 1
        return self.t[i], self.ids[i]


def build_program(depth=DEPTH, ntile=NTILE, small=True):
    nc = bass.Bass("TRN2", target_bir_lowering=False)

    def din(name, shape):
        return nc.dram_tensor(name, shape, F32, kind="ExternalInput").ap()

    def dout(name, shape):
        return nc.dram_tensor(name, shape, F32, kind="ExternalOutput").ap()

    def dscr(name, shape, dt):
        return nc.dram_tensor(name, shape, dt).ap()

    x_p = din("x_p", [SEQ, D])
    x_s = din("x_s", [2, DSEQ, D])
    cache_k = din("cache_k", [DEPTH, 2, PAST, D])
    cache_v = din("cache_v", [DEPTH, 2, PAST, D])
    st_conv = din("st_conv", [DEPTH, 2, 3, D])
    st_rnn = din("st_rnn", [DEPTH, 2, D])
    meta = din("meta", [NMETA, D])
    norm_w = din("norm_w", [DEPTH, D])
    w_in = din("w_in", [DEPTH, D, 8 * D])
    conv_w = din("conv_w", [DEPTH, 4, D])
    conv_b = din("conv_b", [DEPTH, D])
    w_rg = din("w_rg", [DEPTH, 8, 128, 128])
    b_rg = din("b_rg", [DEPTH, D])
    w_ig = din("w_ig", [DEPTH, 8, 128, 128])
    b_ig = din("b_ig", [DEPTH, D])
    lru_l = din("lru_l", [DEPTH, D])
    lam_in = [din(n, [DEPTH, 64]) for n in ("lq1", "lk1", "lq2", "lk2")]
    subln = din("subln", [DEPTH, 128])
    w_pr = din("w_pr", [DEPTH, D, D])
    w_pa = din("w_pa", [DEPTH, D, D])
    w_out = din("w_out", [DEPTH, D, D])
    fnw = din("fnw", [D])
    rope = din("rope", [TP + DSEQ, 2, 32])

    y_p = dout("y_p", [SEQ, D])
    y_s = dout("y_s", [2, DSEQ, D])
    nk_p = dout("nk_p", [DEPTH, TP, D])
    nv_p = dout("nv_p", [DEPTH, TP, D])
    nc_p = dout("nc_p", [DEPTH, 3, D])
    nr_p = dout("nr_p", [DEPTH, D])
    nk_s = dout("nk_s", [DEPTH, 2, DSEQ, D])
    nv_s = dout("nv_s", [DEPTH, 2, DSEQ, D])
    nc_s = dout("nc_s", [DEPTH, 2, 3, D])
    nr_s = dout("nr_s", [DEPTH, 2, D])

    wsc_in = dscr("wsc_in", [DEPTH, D, 8 * D], BF16)
    wsc_p = dscr("wsc_p", [DEPTH, 3, D, D], BF16)
    hb_p = dscr("hb_p", [TP, D], F32)
    hb_s = dscr("hb_s", [2, DSEQ, D], F32)
    kT_p = dscr("kT_p", [NH, 128, KCOLS_P], BF16)
    v_p = dscr("v_p", [NH, 128, 33, 130], BF16)
    kT_s = dscr("kT_s", [2, NH, 128, KCOLS_S], BF16)
    v_s = dscr("v_s", [2, NH, 128, 17, 130], BF16)

    S = Sched(nc)
    TW = TT
    with contextlib.ExitStack() as es:
        def T(name, shape, dt):
            return es.enter_context(nc.sbuf_tensor(name, shape, dt))

        identf = T("identf", [128, 128], F32)
        ident = T("ident", [128, 128], BF16)
        zeros = T("zeros", [128, 1040], BF16)
        nwt = T("nwt", [128, DEPTH, 8], F32)
        cwt = T("cwt", [128, DEPTH, 4, 8], F32)
        cbt = T("cbt", [128, DEPTH, 8], F32)
        nbrg = T("nbrg", [128, DEPTH, 8], F32)
        nbig = T("nbig", [128, DEPTH, 8], F32)
        cch = T("cch", [128, DEPTH, 8], F32)
        wg = T("wg", [128, 2, 8, 128], BF16)
        subw = T("subw", [128, DEPTH, 128], F32)
        sub_t = T("sub_t", [128, 128], F32)
        lamt = T("lamt", [128, 4, 64], F32)
        lams = T("lams", [128, 2], F32)
        neglam = T("neglam", [128, DEPTH], F32)
        fnwt = T("fnwt", [128, D], F32)
        hcar = {k: T("hcar_" + k, [128, 8], F32) for k in ("p", "s0", "s1")}
        xhalo = T("xhalo_p", [128, 8, 3], F32)

        S.add("pool", lambda e: e.memset(identf[:], 1.0), writes=["identf"])
        S.add("pool", lambda e: e.affine_select(out=identf[:], in_=identf[:], pattern=[[-1, 128]],
                                                compare_op=ALU.is_equal, fill=0.0, base=0, channel_multiplier=1),
              reads=["identf"], writes=["identf"])
        S.add("dve", lambda e: e.tensor_copy(out=ident[:], in_=identf[:]), reads=["identf"], writes=["ident"])
        S.add("pool", lambda e: e.memset(zeros[:], 0.0), writes=["zeros"])

        def sdma(out, in_, w, r=()):
            S.add("sp", lambda e: e.dma_start(out=out, in_=in_, allow_slow_non_contiguous=True),
                  reads=list(r), writes=[w], dma=w)

        sdma(nwt[:], norm_w.rearrange("l (j p) -> p l j", p=128), "nwt")
        sdma(cwt[:], conv_w.rearrange("l k (j p) -> p l k j", p=128), "cwt")
        sdma(cbt[:], conv_b.rearrange("l (j p) -> p l j", p=128), "cbt")
        sdma(nbrg[:], b_rg.rearrange("l (j p) -> p l j", p=128), "nbrg")
        sdma(nbig[:], b_ig.rearrange("l (j p) -> p l j", p=128), "nbig")
        sdma(cch[:], lru_l.rearrange("l (j p) -> p l j", p=128), "cch")
        sdma(fnwt[:], fnw.partition_broadcast(128), "fnwt")
        S.add("dve", lambda e: e.tensor_scalar(out=nbrg[:], in0=nbrg[:], scalar1=-1.0, scalar2=None, op0=ALU.mult),
              reads=["nbrg"], writes=["nbrg"])
        S.add("dve", lambda e: e.tensor_scalar(out=nbig[:], in0=nbig[:], scalar1=-1.0, scalar2=None, op0=ALU.mult),
              reads=["nbig"], writes=["nbig"])
        S.add("act", lambda e: e.activation(out=cch[:], in_=cch[:], func=AF.Exp, scale=-1.0), reads=["cch"], writes=["cch"])
        S.add("act", lambda e: e.activation(out=cch[:], in_=cch[:], func=AF.Ln, bias=1.0), reads=["cch"], writes=["cch"])
        S.add("dve", lambda e: e.tensor_scalar(out=cch[:], in0=cch[:], scalar1=-8.0, scalar2=None, op0=ALU.mult),
              reads=["cch"], writes=["cch"])
        for l in range(DEPTH):
            lam_init = 0.8 - 0.6 * math.exp(-0.3 * l)
            sdma(sub_t[:], subln[l].partition_broadcast(128), "sub_t")
            S.add("dve", lambda e, l=l, li=lam_init: e.tensor_scalar(
                out=subw[:, l, :], in0=sub_t[:],
                scalar1=1.0 - li, scalar2=None, op0=ALU.mult), reads=["sub_t"], writes=[("subw", l)])
            for i4 in range(4):
                sdma(lamt[:, i4, :], lam_in[i4][l].partition_broadcast(128), ("lamt", i4))
            for pr in range(2):
                S.add("dve", lambda e, pr=pr: e.tensor_tensor(out=lamt[:, 2 * pr, :], in0=lamt[:, 2 * pr, :],
                                                            in1=lamt[:, 2 * pr + 1, :], op=ALU.mult),
                      reads=[("lamt", 2 * pr), ("lamt", 2 * pr + 1)], writes=[("lamt", 2 * pr)])
                S.add("dve", lambda e, pr=pr: e.tensor_reduce(out=lams[:, pr:pr + 1], in_=lamt[:, 2 * pr, :],
                                                            axis=AX.X, op=ALU.add),
                      reads=[("lamt", 2 * pr)], writes=[("lams", pr)])
            S.add("act", lambda e: e.activation(out=lams[:], in_=lams[:], func=AF.Exp),
                  reads=[("lams", 0), ("lams", 1)], writes=[("lams", 0), ("lams", 1)])
            S.add("dve", lambda e, l=l: e.tensor_tensor(out=neglam[:, l:l + 1], in0=lams[:, 1:2], in1=lams[:, 0:1],
                                                      op=ALU.subtract),
                  reads=[("lams", 0), ("lams", 1)], writes=[("neglam", l)])
            S.add("dve", lambda e, l=l, li=lam_init: e.tensor_scalar(out=neglam[:, l:l + 1], in0=neglam[:, l:l + 1],
                                                                   scalar1=-li, scalar2=None, op0=ALU.add),
                  reads=[("neglam", l)], writes=[("neglam", l)])

        def load_gate_w(l):
            for gi, wsrc in enumerate((w_rg, w_ig)):
                S.add("pool", lambda e, l=l, gi=gi, wsrc=wsrc: e.dma_start(
                    out=wg[:, gi, :, :], in_=wsrc[l].rearrange("n i j -> i n j")),
                    writes=[("wg", gi)], dma=("wg", gi))

        def convert_weights(l):
            for q in range(8):
                S.add("pool", lambda e, l=l, q=q: e.dma_start(out=wsc_in[l][:, q * D:(q + 1) * D],
                                                            in_=w_in[l][:, q * D:(q + 1) * D]),
                      writes=[("wsc", l, q)], dma=("wsc", l), grp=True)
            for i3, wsrc in enumerate((w_pr, w_pa, w_out)):
                S.add("pool", lambda e, l=l, i3=i3, wsrc=wsrc: e.dma_start(out=wsc_p[l, i3], in_=wsrc[l]),
                      writes=[("wscp", l, i3)], dma=("wsc", l), grp=True)

        convert_weights(0)
        S.add("sp", lambda e: e.dma_start(out=kT_p.rearrange("h p t -> p h t")[:, :, 0:128],
                                          in_=zeros[:, 0:1024].rearrange("p (h x) -> p h x", h=8)),
              reads=["zeros"], writes=["kTp"], dma="z0")
        S.add("sp", lambda e: e.dma_start(out=v_p.rearrange("h p k x -> p h k x")[:, :, 0, :],
                                          in_=zeros[:, 0:1040].rearrange("p (h x) -> p h x", h=8)),
              reads=["zeros"], writes=["vp"], dma="z1")
        for s_ in range(2):
            S.add("sp", lambda e, s_=s_: e.dma_start(out=kT_s[s_].rearrange("h p t -> p h t")[:, :, 2048:2176],
                                                   in_=zeros[:, 0:1024].rearrange("p (h x) -> p h x", h=8)),
                  reads=["zeros"], writes=[("kTs", s_)], dma=("z2", s_))
            S.add("sp", lambda e, s_=s_: e.dma_start(out=v_s[s_].rearrange("h p k x -> p h k x")[:, :, 16, :],
                                                   in_=zeros[:, 0:1040].rearrange("p (h x) -> p h x", h=8)),
                  reads=["zeros"], writes=[("vs", s_)], dma=("z3", s_))

        wpool = Rot(es, nc, "wch", [128, 8, 512], BF16, NW)
        hres = Rot(es, nc, "hres", [128, D], F32, 3)
        xnb = Rot(es, nc, "xnb", [128, D], BF16, 2)
        st1 = Rot(es, nc, "st1", [128, 2], F32, 4)
        xnT = Rot(es, nc, "xnT", [128, 8, TW], BF16, 1)
        rot = Rot(es, nc, "rot", [128, D], F32, 2)
        rotb = Rot(es, nc, "rotb", [128, 2048], BF16, 2)
        rtmp = Rot(es, nc, "rtmp", [128, 4, 256], F32, 2)
        ropet = Rot(es, nc, "ropet", [128, 2, 32], F32, 3)
        qT = Rot(es, nc, "qT", [128, 8, TW], BF16, 1)
        kTst = Rot(es, nc, "kTst", [128, 8, 256], BF16, 2)
        vst = Rot(es, nc, "vst", [128, 8, 2, 130], BF16, 2)
        vf = Rot(es, nc, "vf", [128, D], F32, 3)
        gatea = Rot(es, nc, "gatea", [128, D], BF16, 2)
        etm = Rot(es, nc, "etm", [128, 512], F32, 2)
        XW = TW + 9
        xr = Rot(es, nc, "xr", [128, 8, XW], F32, 1)
        szr = Rot(es, nc, "szr", [128, 8, TW], BF16, 1)
        sgr = Rot(es, nc, "sgr", [128, 8, TW], BF16, 1)
        sga = Rot(es, nc, "sga", [128, 8, TW], BF16, 1)
        rn = {k: Rot(es, nc, "rn_" + k, [128, TW], F32, 2) for k in ("xc", "r", "i", "a", "m", "hs")}
        xcb = Rot(es, nc, "xcb", [128, TW], BF16, 2)
        orT = Rot(es, nc, "orT", [128, 8, TW], BF16, 1)
        oaT = Rot(es, nc, "oaT", [128, 8, TW], BF16, 1)
        mT = Rot(es, nc, "mT", [128, 8, TW], BF16, 1)
        kbuf = Rot(es, nc, "kbuf", [128, 17 * 128], BF16, 2)
        vbuf = Rot(es, nc, "vbuf", [128, 17, 130], BF16, 2)
        ptb = Rot(es, nc, "ptb", [128, 2, TW], BF16, 3)
        obuf = Rot(es, nc, "obuf", [128, 8, 128], F32, 2)
        oab = Rot(es, nc, "oab", [128, D], BF16, 1)
        arec = Rot(es, nc, "arec", [128, 4], F32, 4)
        atmp = Rot(es, nc, "atmp", [128, 128], F32, 2)
        st8 = Rot(es, nc, "st8", [128, 8], F32, 2)
        mtmp = Rot(es, nc, "mtmp", [128, 2, TW], F32, 2)
        yst = vf
        cst = vf
        cstb = xnb
        psc = Rot(es, nc, "psc", [128, 2, 512], F32, 2, psum=True)
        pacc_t = es.enter_context(nc.psum_tensor("pacc", [128, 2, 512], F32))
        pg = Rot(es, nc, "pg", [128, 512], F32, 2, psum=True)
        print("sbuf bytes remaining", nc.sbuf_bytes_remaining)

        for v_t, v_id in zip(vst.t, vst.ids):
            S.add("pool", lambda e, v_t=v_t: e.memset(v_t[:, :, :, 128:129], 1.0), writes=[v_id])
            S.add("pool", lambda e, v_t=v_t: e.memset(v_t[:, :, :, 129:130], 0.0), writes=[v_id])

        def load_w(src, src_id):
            wt, wid = wpool.next()
            S.add("sp", lambda e: e.dma_start(out=wt[:], in_=src.rearrange("(kc p) n -> p kc n", p=128)),
                  reads=[src_id], writes=[wid], dma=wid)
            return wt, wid

        def pgT(p):
            return p[:].bitcast(BF16).rearrange("p (j x) -> p j x", j=8)

        def rmsnorm_stats(src_ap, src_ids, n, scale_div, width_ap_out, junk_id):
            st, sid = st1.next()
            S.add("act", lambda e: e.activation(out=width_ap_out, in_=src_ap, func=AF.Square, accum_out=st[0:n, 0:1]),
                  reads=src_ids, writes=[junk_id, sid])
            S.add("act", lambda e: e.activation(out=st[0:n, 1:2], in_=st[0:n, 0:1], func=AF.Ln, scale=1.0 / scale_div, bias=EPS),
                  reads=[sid], writes=[sid])
            S.add("act", lambda e: e.activation(out=st[0:n, 1:2], in_=st[0:n, 1:2], func=AF.Exp, scale=-0.5),
                  reads=[sid], writes=[sid])
            return st, sid

        def sigmoid_from_psum(p, pid, n_part, ncol, out_ap, out_id, mul_by_x=False, extra=None, extra_id=None):
            if not mul_by_x:
                S.add("act", lambda e: e.activation(out=out_ap, in_=p, func=AF.Sigmoid), reads=[pid], writes=[out_id])
                return
            if extra is None:
                S.add("act", lambda e: e.activation(out=out_ap, in_=p, func=AF.Silu), reads=[pid], writes=[out_id])
                return
            et, eid = etm.next()
            S.add("act", lambda e: e.activation(out=et[0:n_part, 0:ncol], in_=p, func=AF.Silu), reads=[pid], writes=[eid])
            S.add("pool", lambda e: e.tensor_tensor(
                out=out_ap.rearrange("p (h x) -> p h x", x=128),
                in0=et[0:n_part, 0:ncol].rearrange("p (h x) -> p h x", x=128),
                in1=extra.unsqueeze(1).to_broadcast([n_part, ncol // 128, 128]), op=ALU.mult),
                reads=[eid, extra_id], writes=[out_id])

        def convert_cache(l):
            for s in range(2):
                for g2 in range(8):
                    ks_t, ks_id = kTst.next()
                    vs_t, vs_id = vst.next()
                    for kk in range(2):
                        kb = 2 * g2 + kk
                        c_t, c_id = cst.next()
                        S.add("sp", lambda e, c_t=c_t, kb=kb, s=s: e.dma_start(out=c_t[:], in_=cache_k[l, s, kb * 128:(kb + 1) * 128, :]),
                              writes=[c_id], dma=c_id)
                        cb_t, cb_id = cstb.next()
                        S.add("dve", lambda e, c_t=c_t, cb_t=cb_t: e.tensor_copy(out=cb_t[:], in_=c_t[:]),
                              reads=[c_id], writes=[cb_id])
                        p, pid = pg.next()
                        for h in range(NH):
                            S.add("pe", lambda e, p=p, cb_t=cb_t, h=h: e.transpose(out=pgT(p)[:, h, :], in_=cb_t[:, h * 128:(h + 1) * 128],
                                                                                 identity=ident[:]),
                                  reads=[cb_id, "ident"], writes=[pid])
                        S.add("act", lambda e, p=p, ks_t=ks_t, kk=kk: e.copy(out=ks_t[:, :, kk * 128:(kk + 1) * 128], in_=pgT(p)),
                              reads=[pid], writes=[ks_id])
                        c2_t, c2_id = cst.next()
                        S.add("sp", lambda e, c2_t=c2_t, kb=kb, s=s: e.dma_start(out=c2_t[:], in_=cache_v[l, s, kb * 128:(kb + 1) * 128, :]),
                              writes=[c2_id], dma=c2_id)
                        S.add("pool", lambda e, c2_t=c2_t, vs_t=vs_t, kk=kk: e.tensor_copy(
                            out=vs_t[:, :, kk, 0:128], in_=c2_t[:].rearrange("p (h x) -> p h x", h=8)),
                            reads=[c2_id], writes=[vs_id])
                    S.add("pool", lambda e, ks_t=ks_t, g2=g2, s=s: e.dma_start(
                        out=kT_s[s].rearrange("h p t -> p h t")[:, :, g2 * 256:(g2 + 1) * 256], in_=ks_t[:]),
                        reads=[ks_id], writes=[("kTs", s)], dma=ks_id)
                    S.add("pool", lambda e, vs_t=vs_t, g2=g2, s=s: e.dma_start(
                        out=v_s[s].rearrange("h p k x -> p h k x")[:, :, 2 * g2:2 * g2 + 2, :], in_=vs_t[:]),
                        reads=[vs_id], writes=[("vs", s)], dma=vs_id)

        def do_tile(l, segs):
            last_layer = (l == depth - 1)
            ncols = sum(sg["n"] for sg in segs)
            blocks = []
            for si, sg in enumerate(segs):
                for bo in range(0, sg["n"], 128):
                    bn = min(128, sg["n"] - bo)
                    blocks.append(dict(si=si, sg=sg, bo=bo, bn=bn, col=sg["col0"] + bo))
            xt_t, xt_id = xnT.next()
            if DBG: print('MARK', segs[0]['kind'], segs[0]['t0'], '# ---- step 1:', len(S.ops))
            for b in blocks:
                sg, bn = b["sg"], b["bn"]
                h_t, h_id = hres.next()
                b["h"], b["hid"] = h_t, h_id
                r0 = sg["t0"] + b["bo"]
                if sg["kind"] in ("m", "p"):
                    hid_src = ("hp", r0 // 128 if sg["kind"] == "p" else "m")
                    if l == 0:
                        src = meta[0:bn, :] if sg["kind"] == "m" else x_p[r0 - NMETA:r0 - NMETA + bn, :]
                    else:
                        src = hb_p[r0:r0 + bn, :]
                else:
                    s = int(sg["kind"][1])
                    hid_src = ("hs", s)
                    src = x_s[s, r0:r0 + bn, :] if l == 0 else hb_s[s, r0:r0 + bn, :]
                b["hsrc"] = hid_src
                S.add("sp", lambda e, h_t=h_t, src=src, bn=bn: e.dma_start(out=h_t[0:bn, :], in_=src),
                      reads=[hid_src], writes=[h_id], dma=h_id)
                xb_t, xb_id = xnb.next()
                st, sid = rmsnorm_stats(h_t[0:bn, :], [h_id], bn, float(D), xb_t[0:bn, :], xb_id)
                S.add("dve", lambda e, xb_t=xb_t, h_t=h_t, st=st, bn=bn: e.tensor_scalar(
                    out=xb_t[0:bn, :], in0=h_t[0:bn, :], scalar1=st[0:bn, 1:2], scalar2=None, op0=ALU.mult),
                    reads=[h_id, sid], writes=[xb_id])
                p, pid = pg.next()
                for j in range(8):
                    S.add("pe", lambda e, p=p, xb_t=xb_t, j=j, bn=bn: e.transpose(
                        out=pgT(p)[:, j, 0:bn], in_=xb_t[0:bn, j * 128:(j + 1) * 128], identity=ident[0:bn, 0:bn]),
                        reads=[xb_id, "ident"], writes=[pid])
                S.add("dve", lambda e, p=p, b=b, bn=bn: e.tensor_tensor(
                    out=xt_t[:, :, b["col"]:b["col"] + bn], in0=pgT(p)[:, :, 0:bn],
                    in1=nwt[:, l, :].unsqueeze(2).to_broadcast([128, 8, bn]), op=ALU.mult),
                    reads=[pid, "nwt"], writes=[(xt_id, b["col"])])
            xt_ids = [(xt_id, b["col"]) for b in blocks]

            if DBG: print('MARK', segs[0]['kind'], segs[0]['t0'], '# ---- step 2:', len(S.ops))
            qT_t, qT_id = qT.next()
            for b in blocks:
                sg, bn = b["sg"], b["bn"]
                b["vf"], b["vfid"] = vf.next()
                b["ga"], b["gaid"] = gatea.next()
                b["ro"], b["roid"] = rot.next()
                b["rb"], b["rbid"] = rotb.next()
                r0 = sg["t0"] + b["bo"]
                rrow = r0 if sg["kind"] in ("m", "p") else TP + r0
                b["rp"], b["rpid"] = ropet.next()
                S.add("sp", lambda e, rp_t=b["rp"], rrow=rrow, bn=bn: e.dma_start(out=rp_t[0:bn], in_=rope[rrow:rrow + bn]),
                      writes=[b["rpid"]], dma=b["rpid"])
            for sg in segs:
                sg["vst"], sg["vstid"] = vst.next()
                sg["kst"], sg["kstid"] = kTst.next()
            for g in range(8):
                wt, wid = load_w(wsc_in[l][:, 2 * D + g * 512: 2 * D + (g + 1) * 512], ("wsc", l, 2 + g // 2))
                for b in blocks:
                    bn = b["bn"]
                    p, pid = pg.next()
                    for kc in range(8):
                        S.add("pe", lambda e, p=p, wt=wt, kc=kc, b=b, bn=bn: e.matmul(
                            p[0:bn, :], lhsT=xt_t[:, kc, b["col"]:b["col"] + bn], rhs=wt[:, kc, :],
                            start=(kc == 0), stop=(kc == 7)), reads=[(xt_id, b["col"]), wid], writes=[pid])
                    if g < 4:
                        rp_t = b["rp"]
                        src = p[0:bn, :].rearrange("p (a c x) -> p a c x", a=8, c=2)
                        x1, x2 = src[:, :, 0, :], src[:, :, 1, :]
                        cosb = rp_t[0:bn, 0:1, :].to_broadcast([bn, 8, 32])
                        sinb = rp_t[0:bn, 1:2, :].to_broadcast([bn, 8, 32])
                        tm_t, tm_id = rtmp.next()
                        tt = [tm_t[0:bn, i4, :].rearrange("p (a x) -> p a x", a=8) for i4 in range(4)]
                        for i4, (xa, tb) in enumerate(((x1, cosb), (x2, sinb), (x2, cosb), (x1, sinb))):
                            S.add("dve", lambda e, o=tt[i4], xa=xa, tb=tb: e.tensor_tensor(out=o, in0=xa, in1=tb, op=ALU.mult),
                                  reads=[pid, b["rpid"]], writes=[(tm_id, i4)])
                        if g < 2:
                            dst = b["rb"][0:bn, g * 512:(g + 1) * 512].rearrange("p (a c x) -> p a c x", a=8, c=2)
                            did = (b["rbid"], g)
                        else:
                            dst = b["ro"][0:bn, (g - 2) * 512:(g - 1) * 512].rearrange("p (a c x) -> p a c x", a=8, c=2)
                            did = (b["roid"], g)
                        S.add("pool", lambda e, dst=dst, tt=tt: e.tensor_tensor(out=dst[:, :, 0, :], in0=tt[0], in1=tt[1], op=ALU.subtract),
                              reads=[(tm_id, 0), (tm_id, 1)], writes=[did])
                        S.add("pool", lambda e, dst=dst, tt=tt: e.tensor_tensor(out=dst[:, :, 1, :], in0=tt[2], in1=tt[3], op=ALU.add),
                              reads=[(tm_id, 2), (tm_id, 3)], writes=[did])
                        if g >= 2:
                            S.add("pool", lambda e, b=b, bn=bn, g=g: e.tensor_copy(
                                out=b["rb"][0:bn, 1024 + (g - 2) * 512:1024 + (g - 1) * 512], in_=b["ro"][0:bn, (g - 2) * 512:(g - 1) * 512]),
                                reads=[did], writes=[(b["rbid"], g)])
                    elif g < 6:
                        S.add("act", lambda e, p=p, b=b, bn=bn, g=g: e.copy(out=b["vf"][0:bn, (g - 4) * 512:(g - 3) * 512], in_=p[0:bn, :]),
                              reads=[pid], writes=[(b["vfid"], g)])
                        sg = b["sg"]
                        kk = b["bo"] // 128
                        S.add("pool", lambda e, b=b, sg=sg, kk=kk, bn=bn, g=g: e.tensor_copy(
                            out=sg["vst"][0:bn, 4 * (g - 4):4 * (g - 3), kk, 0:128],
                            in_=b["vf"][0:bn, (g - 4) * 512:(g - 3) * 512].rearrange("p (h x) -> p h x", h=4)),
                            reads=[(b["vfid"], g)], writes=[sg["vstid"]])
                    else:
                        c0 = (g - 6) * 512
                        sigmoid_from_psum(p[0:bn, :], pid, bn, 512, b["ga"][0:bn, c0:c0 + 512], (b["gaid"], g),
                                          mul_by_x=True, extra=subw[0:bn, l, :], extra_id=("subw", l))
            for b in blocks:
                sg, bn = b["sg"], b["bn"]
                kind = sg["kind"]
                r0 = sg["t0"] + b["bo"]
                rb_t, rb_id = b["rb"], b["rbid"]
                rbids = [(rb_id, g) for g in range(4)]
                if kind in ("m", "p"):
                    kdst, vdst = nk_p[l, r0:r0 + bn, :], nv_p[l, r0:r0 + bn, :]
                else:
                    s = int(kind[1])
                    kdst, vdst = nk_s[l, s, r0:r0 + bn, :], nv_s[l, s, r0:r0 + bn, :]
                S.add("pool", lambda e, kdst=kdst, b=b, bn=bn: e.dma_start(out=kdst, in_=b["ro"][0:bn, :]),
                      reads=[(b["roid"], 2), (b["roid"], 3)], writes=[("out_k", l, kind, r0)], dma=b["roid"])
                S.add("pool", lambda e, vdst=vdst, b=b, bn=bn: e.dma_start(out=vdst, in_=b["vf"][0:bn, :]),
                      reads=[(b["vfid"], 4), (b["vfid"], 5)], writes=[("out_v", l, kind, r0)], dma=b["vfid"])
                for which in range(2):
                    p, pid = pg.next()
                    for h in range(NH):
                        S.add("pe", lambda e, p=p, rb_t=rb_t, h=h, which=which, bn=bn: e.transpose(
                            out=pgT(p)[:, h, 0:bn], in_=rb_t[0:bn, which * 1024 + h * 128: which * 1024 + (h + 1) * 128],
                            identity=ident[0:bn, 0:bn]), reads=rbids + ["ident"], writes=[pid])
                    if which == 0:
                        S.add("act", lambda e, p=p, b=b, bn=bn: e.copy(out=qT_t[:, :, b["col"]:b["col"] + bn], in_=pgT(p)[:, :, 0:bn]),
                              reads=[pid], writes=[(qT_id, b["col"])])
                    else:
                        S.add("act", lambda e, p=p, sg=sg, b=b, bn=bn: e.copy(out=sg["kst"][:, :, b["bo"]:b["bo"] + bn], in_=pgT(p)[:, :, 0:bn]),
                              reads=[pid], writes=[sg["kstid"]])
            if DBG: print('MARK', segs[0]['kind'], segs[0]['t0'], '# store K^T / ', len(S.ops))
            for sg in segs:
                kind, n = sg["kind"], sg["n"]
                if kind == "m":
                    kcol, kb0, kTd, vd, kid, vid_ = 0, 0, kT_p, v_p, "kTp", "vp"
                elif kind == "p":
                    kb0 = 1 + (sg["t0"] - NMETA) // 128
                    kcol, kTd, vd, kid, vid_ = kb0 * 128, kT_p, v_p, "kTp", "vp"
                else:
                    s = int(kind[1])
                    kb0, kcol, kTd, vd, kid, vid_ = 16, 2048, kT_s[s], v_s[s], ("kTs", s), ("vs", s)
                sg["kb0"] = kb0
                nkb = (n + 127) // 128
                pn = min(n, 128)
                S.add("pool", lambda e, sg=sg, kTd=kTd, kcol=kcol, n=n: e.dma_start(
                    out=kTd.rearrange("h p t -> p h t")[:, :, kcol:kcol + n], in_=sg["kst"][:, :, 0:n]),
                    reads=[sg["kstid"]], writes=[kid], dma=sg["kstid"])
                S.add("pool", lambda e, sg=sg, vd=vd, kb0=kb0, nkb=nkb, pn=pn: e.dma_start(
                    out=vd.rearrange("h p k x -> p h k x")[0:pn, :, kb0:kb0 + nkb, :], in_=sg["vst"][0:pn, :, 0:nkb, :]),
                    reads=[sg["vstid"]], writes=[vid_], dma=sg["vstid"])

            if DBG: print('MARK', segs[0]['kind'], segs[0]['t0'], '# ---- step 3:', len(S.ops))
            xr_t, xr_id = xr.next()
            szr_t, szr_id = szr.next()
            sgr_t, sgr_id = sgr.next()
            sga_t, sga_id = sga.next()
            for si, sg in enumerate(segs):
                sg["xb"] = sg["col0"] + 3 * si
            for gg in range(8):
                colbase = gg * 512 if gg < 4 else 6 * D + (gg - 4) * 512
                wt, wid = load_w(wsc_in[l][:, colbase:colbase + 512], ("wsc", l, colbase // D))
                for c4 in range(4):
                    j = (gg % 2) * 4 + c4
                    p, pid = pg.next()
                    for kc in range(8):
                        S.add("pe", lambda e, p=p, wt=wt, kc=kc, c4=c4: e.matmul(
                            p[:, 0:ncols], lhsT=wt[:, kc, c4 * 128:(c4 + 1) * 128], rhs=xt_t[:, kc, 0:ncols],
                            start=(kc == 0), stop=(kc == 7)), reads=xt_ids + [wid], writes=[pid])
                    if gg < 2:
                        for sg in segs:
                            S.add("act", lambda e, p=p, sg=sg, j=j: e.copy(
                                out=xr_t[:, j, sg["xb"] + 3:sg["xb"] + 3 + sg["n"]], in_=p[:, sg["col0"]:sg["col0"] + sg["n"]]),
                                reads=[pid], writes=[(xr_id, j)])
                    elif gg < 4:
                        sigmoid_from_psum(p[:, 0:ncols], pid, 128, ncols, szr_t[:, j, 0:ncols], (szr_id, j), mul_by_x=True)
                    elif gg < 6:
                        sigmoid_from_psum(p[:, 0:ncols], pid, 128, ncols, sgr_t[:, j, 0:ncols], (sgr_id, j))
                    else:
                        sigmoid_from_psum(p[:, 0:ncols], pid, 128, ncols, sga_t[:, j, 0:ncols], (sga_id, j))

            if DBG: print('MARK', segs[0]['kind'], segs[0]['t0'], '# ---- step 5 ', len(S.ops))
            oaT_t, oaT_id = oaT.next()
            assert len(segs) == 1
            sg = segs[0]
            if True:
                kind, n, col0 = sg["kind"], sg["n"], sg["col0"]
                if kind == "m":
                    kbs = [(0, 16, 0, False)]
                    kTd, vd, kid, vid_ = kT_p, v_p, "kTp", "vp"
                elif kind == "p":
                    f0 = sg["t0"] - NMETA
                    kbs = [(0, 16, 0, False)] + [(kb, 128, 0, False) for kb in range(1, 1 + f0 // 128)]
                    for m in range(n // 128):
                        kbs.append((1 + f0 // 128 + m, 128, 128 * m, True))
                    kTd, vd, kid, vid_ = kT_p, v_p, "kTp", "vp"
                else:
                    s = int(kind[1])
                    kbs = [(kb, 128, 0, False) for kb in range(16)] + [(16, 64, 0, False)]
                    kTd, vd, kid, vid_ = kT_s[s], v_s[s], ("kTs", s), ("vs", s)
                nkb_all = kbs[-1][0] + 1
                nqb = (n + 127) // 128
                qbn = [min(128, n - 128 * q) for q in range(nqb)]
                ob = []
                for q in range(nqb):
                    o_t, o_id = obuf.next()
                    ob.append((o_t, o_id))
                def attn_head(h):
                    for q in range(nqb):
                        S.add("pe", lambda e, q=q, qn=qbn[q]: e.matmul(pacc_t[0:qn, q, :], lhsT=zeros[:, 0:qn], rhs=zeros[:, 0:512],
                                                                     start=True, stop=True), reads=["zeros"], writes=[("pacc", q)])
                    items = []
                    for lo in (0, 17):
                        part = [x for x in kbs if lo <= x[0] < lo + 17]
                        if not part:
                            continue
                        npk = part[-1][0] - lo + 1
                        kb_t, kb_id = kbuf.next()
                        vb_t, vb_id = vbuf.next()
                        S.add("sp", lambda e, kb_t=kb_t, kTd=kTd, h=h, lo=lo, npk=npk: e.dma_start(
                            out=kb_t[:, 0:npk * 128], in_=kTd[h, :, lo * 128:(lo + npk) * 128]), reads=[kid], writes=[kb_id], dma=kb_id)
                        S.add("sp", lambda e, vb_t=vb_t, vd=vd, h=h, lo=lo, npk=npk: e.dma_start(
                            out=vb_t[:, 0:npk, :], in_=vd[h, :, lo:lo + npk, :]), reads=[vid_], writes=[vb_id], dma=vb_id)
                        for (kb, kk, qlo, diag) in part:
                            items.append((kb_t, kb_id, vb_t, vb_id, kb - lo, kk, qlo, diag))
                    qids = [(qT_id, bb["col"]) for bb in blocks if bb["sg"] is sg]

                    def score(it):
                        kb_t, kb_id, vb_t, vb_id, kl, kk, qlo, diag = it
                        nqc = n - qlo
                        ps_t, ps_id = psc.next()
                        for c in range(2):
                            S.add("pe", lambda e, ps_t=ps_t, kb_t=kb_t, c=c, kl=kl, kk=kk, qlo=qlo, nqc=nqc: e.matmul(
                                ps_t[0:kk, c, 0:nqc], lhsT=kb_t[64 * c:64 * c + 64, kl * 128:kl * 128 + kk],
                                rhs=qT_t[64 * c:64 * c + 64, h, col0 + qlo:col0 + n], start=True, stop=True),
                                reads=[kb_id] + qids, writes=[ps_id])
                        return ps_t, ps_id

                    nxt = score(items[0])
                    for ii, it in enumerate(items):
                        kb_t, kb_id, vb_t, vb_id, kl, kk, qlo, diag = it
                        nqc = n - qlo
                        ps_t, ps_id = nxt
                        if ii + 1 < len(items):
                            nxt = score(items[ii + 1])
                        pt_t, pt_id = ptb.next()
                        S.add("act", lambda e, pt_t=pt_t, ps_t=ps_t, kk=kk, nqc=nqc: e.activation(
                            out=pt_t[0:kk, :, 0:nqc], in_=ps_t[0:kk, :, 0:nqc], func=AF.Exp, scale=0.125),
                            reads=[ps_id], writes=[pt_id])
                        if diag:
                            S.add("pool", lambda e, pt_t=pt_t: e.memset(pt_t[64:128, :, 0:64], 0.0),
                                  reads=[pt_id], writes=[pt_id])
                        for q in range(qlo // 128, nqb):
                            for c in range(2):
                                S.add("pe", lambda e, pt_t=pt_t, vb_t=vb_t, q=q, c=c, kk=kk, kl=kl, qlo=qlo, qn=qbn[q]: e.matmul(
                                    pacc_t[0:qn, q, c * 129:(c + 1) * 129], lhsT=pt_t[0:kk, c, 128 * q - qlo:128 * q - qlo + qn],
                                    rhs=vb_t[0:kk, kl, 0:129], start=False, stop=True, skip_group_check=True),
                                    reads=[pt_id, vb_id], writes=[("pacc", q)])
                    for q in range(nqb):
                        nq = qbn[q]
                        o_t, o_id = ob[q]
                        ar_t, ar_id = arec.next()
                        acc = pacc_t[0:nq, q, 0:258].rearrange("p (c x) -> p c x", c=2)
                        S.add("dve", lambda e, ar_t=ar_t, acc=acc, nq=nq: e.reciprocal(out=ar_t[0:nq, 0:2], in_=acc[:, :, 128]),
                              reads=[("pacc", q)], writes=[ar_id])
                        S.add("dve", lambda e, ar_t=ar_t, nq=nq: e.tensor_tensor(out=ar_t[0:nq, 2:3], in0=ar_t[0:nq, 1:2],
                                                                              in1=neglam[0:nq, l:l + 1], op=ALU.mult),
                              reads=[ar_id, ("neglam", l)], writes=[ar_id])
                        at_t, at_id = atmp.next()
                        S.add("dve", lambda e, at_t=at_t, acc=acc, ar_t=ar_t, nq=nq: e.tensor_scalar(
                            out=at_t[0:nq, :], in0=acc[:, 1, 0:128], scalar1=ar_t[0:nq, 2:3], scalar2=None, op0=ALU.mult),
                            reads=[("pacc", q), ar_id], writes=[at_id])
                        S.add("dve", lambda e, o_t=o_t, at_t=at_t, acc=acc, ar_t=ar_t, nq=nq, h=h: e.scalar_tensor_tensor(
                            out=o_t[0:nq, h, :], in0=acc[:, 0, 0:128], scalar=ar_t[0:nq, 0:1], in1=at_t[0:nq, :],
                            op0=ALU.mult, op1=ALU.add), reads=[("pacc", q), ar_id, at_id], writes=[(o_id, h)])
                def attn_post():
                    sblocks = [bb for bb in blocks if bb["sg"] is sg]
                    for q in range(nqb):
                        nq = qbn[q]
                        o_t, o_id = ob[q]
                        bb = sblocks[q]
                        oids = [(o_id, h) for h in range(NH)]
                        s8_t, s8_id = st8.next()
                        osq_t, osq_id = rot.next()
                        osq = osq_t[:].rearrange("p (h x) -> p h x", h=8)
                        S.add("pool", lambda e, o_t=o_t, nq=nq, osq=osq: e.tensor_tensor(out=osq[0:nq], in0=o_t[0:nq], in1=o_t[0:nq], op=ALU.mult),
                              reads=oids, writes=[osq_id])
                        S.add("dve", lambda e, s8_t=s8_t, nq=nq, osq=osq: e.tensor_reduce(out=s8_t[0:nq, :], in_=osq[0:nq], axis=AX.X, op=ALU.add),
                              reads=[osq_id], writes=[s8_id])
                        S.add("act", lambda e, s8_t=s8_t, nq=nq: e.activation(out=s8_t[0:nq, :], in_=s8_t[0:nq, :], func=AF.Ln,
                                                                            scale=1.0 / 128, bias=EPS), reads=[s8_id], writes=[s8_id])
                        S.add("act", lambda e, s8_t=s8_t, nq=nq: e.activation(out=s8_t[0:nq, :], in_=s8_t[0:nq, :], func=AF.Exp, scale=-0.5),
                              reads=[s8_id], writes=[s8_id])
                        S.add("dve", lambda e, o_t=o_t, s8_t=s8_t, nq=nq: e.tensor_tensor(
                            out=o_t[0:nq], in0=o_t[0:nq], in1=s8_t[0:nq, :].unsqueeze(2).to_broadcast([nq, 8, 128]), op=ALU.mult),
                            reads=oids + [s8_id], writes=oids)
                        oa_t, oa_id = oab.next()
                        S.add("pool", lambda e, oa_t=oa_t, o_t=o_t, bb=bb, nq=nq: e.tensor_tensor(
                            out=oa_t[0:nq, :], in0=o_t[0:nq].rearrange("p h x -> p (h x)"), in1=bb["ga"][0:nq, :], op=ALU.mult),
                            reads=oids + [(bb["gaid"], 6), (bb["gaid"], 7)], writes=[oa_id])
                        p, pid = pg.next()
                        for j in range(8):
                            S.add("pe", lambda e, p=p, oa_t=oa_t, j=j, nq=nq: e.transpose(
                                out=pgT(p)[:, j, 0:nq], in_=oa_t[0:nq, j * 128:(j + 1) * 128], identity=ident[0:nq, 0:nq]),
                                reads=[oa_id, "ident"], writes=[pid])
                        S.add("act", lambda e, p=p, bb=bb, nq=nq: e.copy(out=oaT_t[:, :, bb["col"]:bb["col"] + nq], in_=pgT(p)[:, :, 0:nq]),
                              reads=[pid], writes=[(oaT_id, bb["col"])])

            orT_t, orT_id = orT.next()
            if True:
                kind, n, col0, xb = sg["kind"], sg["n"], sg["col0"], sg["xb"]
                ck = "p" if kind in ("m", "p") else kind
                hc = hcar[ck]
                if kind == "m":
                    S.add("pool", lambda e, xb=xb: e.memset(xr_t[:, :, xb:xb + 3], 0.0), writes=[(xr_id, "halo", xb)])
                    S.add("pool", lambda e, hc=hc: e.memset(hc[:], 0.0), writes=[("hcar", ck)])
                elif kind == "p":
                    S.add("pool", lambda e, xb=xb: e.tensor_copy(out=xr_t[:, :, xb:xb + 3], in_=xhalo[:]),
                          reads=["xhalo"], writes=[(xr_id, "halo", xb)])
                else:
                    s = int(kind[1])
                    for k3 in range(3):
                        S.add("sp", lambda e, xb=xb, s=s, k3=k3: e.dma_start(out=xr_t[:, :, xb + k3],
                                                                           in_=st_conv[l, s, k3].rearrange("(j p) -> p j", p=128),
                                                                           allow_slow_non_contiguous=True),
                              writes=[(xr_id, "halo", xb)], dma=(xr_id, "halo", xb))
                    S.add("sp", lambda e, s=s, hc=hc: e.dma_start(out=hc[:], in_=st_rnn[l, s].rearrange("(j p) -> p j", p=128),
                                                                allow_slow_non_contiguous=True),
                          writes=[("hcar", ck)], dma=("hcar", ck))
                rnn_st = {}

                def rnn_conv(j):
                    xc_t, xc_id = rn["xc"].next()
                    xin = [(xr_id, j), (xr_id, "halo", xb)]
                    S.add("dve", lambda e, xc_t=xc_t, j=j, xb=xb, n=n: e.tensor_scalar(
                        out=xc_t[:, 0:n], in0=xr_t[:, j, xb:xb + n], scalar1=cwt[:, l, 0, j:j + 1], scalar2=cbt[:, l, j:j + 1],
                        op0=ALU.mult, op1=ALU.add), reads=xin + ["cwt", "cbt"], writes=[xc_id])
                    for k in range(1, 4):
                        S.add("dve", lambda e, xc_t=xc_t, j=j, xb=xb, n=n, k=k: e.scalar_tensor_tensor(
                            out=xc_t[:, 0:n], in0=xr_t[:, j, xb + k:xb + k + n], scalar=cwt[:, l, k, j:j + 1], in1=xc_t[:, 0:n],
                            op0=ALU.mult, op1=ALU.add), reads=xin + ["cwt", xc_id], writes=[xc_id])
                    xcb_t, xcb_id = xcb.next()
                    S.add("pool", lambda e, xcb_t=xcb_t, xc_t=xc_t, n=n: e.tensor_copy(out=xcb_t[:, 0:n], in_=xc_t[:, 0:n]),
                          reads=[xc_id], writes=[xcb_id])
                    rnn_st[j] = (xc_t, xc_id, xcb_t, xcb_id)

                def rnn_chunk(j):
                    xc_t, xc_id, xcb_t, xcb_id = rnn_st[j]
                    gates = []
                    for gi, nb_ in ((0, nbrg), (1, nbig)):
                        p, pid = pg.next()
                        S.add("pe", lambda e, p=p, xcb_t=xcb_t, gi=gi, j=j, n=n: e.matmul(
                            p[:, 0:n], lhsT=wg[:, gi, j, :], rhs=xcb_t[:, 0:n], start=True, stop=True),
                            reads=[xcb_id, ("wg", gi)], writes=[pid])
                        g_t, g_id = rn["r" if gi == 0 else "i"].next()
                        S.add("act", lambda e, g_t=g_t, p=p, nb_=nb_, j=j, n=n: e.activation(
                            out=g_t[:, 0:n], in_=p[:, 0:n], func=AF.Exp, scale=-1.0, bias=nb_[:, l, j:j + 1]),
                            reads=[pid, "nbrg", "nbig"], writes=[g_id])
                        S.add("dve", lambda e, g_t=g_t, n=n: e.tensor_scalar(out=g_t[:, 0:n], in0=g_t[:, 0:n], scalar1=1.0, scalar2=None,
                                                                           op0=ALU.add), reads=[g_id], writes=[g_id])
                        S.add("dve", lambda e, g_t=g_t, n=n: e.reciprocal(out=g_t[:, 0:n], in_=g_t[:, 0:n]), reads=[g_id], writes=[g_id])
                        gates.append((g_t, g_id))
                    (r_t, r_id), (i_t, i_id) = gates
                    a_t, a_id = rn["a"].next()
                    m_t, m_id = rn["m"].next()
                    S.add("act", lambda e, a_t=a_t, r_t=r_t, j=j, n=n: e.activation(out=a_t[:, 0:n], in_=r_t[:, 0:n], func=AF.Exp,
                                                                                 scale=cch[:, l, j:j + 1]), reads=[r_id, "cch"], writes=[a_id])
                    S.add("act", lambda e, a_t=a_t, m_t=m_t, n=n: e.activation(out=m_t[:, 0:n], in_=a_t[:, 0:n], func=AF.Square),
                          reads=[a_id], writes=[m_id])
                    S.add("act", lambda e, m_t=m_t, n=n: e.activation(out=m_t[:, 0:n], in_=m_t[:, 0:n], func=AF.Ln, scale=-1.0, bias=1.0),
                          reads=[m_id], writes=[m_id])
                    S.add("act", lambda e, m_t=m_t, n=n: e.activation(out=m_t[:, 0:n], in_=m_t[:, 0:n], func=AF.Exp, scale=0.5),
                          reads=[m_id], writes=[m_id])
                    S.add("dve", lambda e, m_t=m_t, i_t=i_t, n=n: e.tensor_tensor(out=m_t[:, 0:n], in0=m_t[:, 0:n], in1=i_t[:, 0:n], op=ALU.mult),
                          reads=[m_id, i_id], writes=[m_id])
                    S.add("dve", lambda e, m_t=m_t, xc_t=xc_t, n=n: e.tensor_tensor(out=m_t[:, 0:n], in0=m_t[:, 0:n], in1=xc_t[:, 0:n], op=ALU.mult),
                          reads=[m_id, xc_id], writes=[m_id])
                    hs_t, hs_id = rn["hs"].next()
                    S.add("dve", lambda e, hs_t=hs_t, a_t=a_t, m_t=m_t, j=j, n=n, hc=hc: e.tensor_tensor_scan(
                        out=hs_t[:, 0:n], data0=a_t[:, 0:n], data1=m_t[:, 0:n], initial=hc[:, j:j + 1], op0=ALU.mult, op1=ALU.add),
                        reads=[a_id, m_id, ("hcar", ck)], writes=[hs_id])
                    S.add("pool", lambda e, hs_t=hs_t, j=j, n=n, hc=hc: e.tensor_copy(out=hc[:, j:j + 1], in_=hs_t[:, n - 1:n]),
                          reads=[hs_id], writes=[("hcar", ck)])
                    S.add("dve", lambda e, hs_t=hs_t, j=j, n=n, col0=col0: e.tensor_tensor(
                        out=orT_t[:, j, col0:col0 + n], in0=hs_t[:, 0:n], in1=szr_t[:, j, col0:col0 + n], op=ALU.mult),
                        reads=[hs_id, (szr_id, j)], writes=[(orT_id, j)])
                def rnn_post():
                    xall = [(xr_id, j) for j in range(8)] + [(xr_id, "halo", xb)]
                    if kind in ("m", "p") and not sg["last"]:
                        S.add("pool", lambda e, xb=xb, n=n: e.tensor_copy(out=xhalo[:], in_=xr_t[:, :, xb + n:xb + n + 3]),
                              reads=xall, writes=["xhalo"])
                    if sg["last"]:
                        if kind == "p":
                            cdst, rdst = nc_p[l], nr_p[l]
                        else:
                            s = int(kind[1])
                            cdst, rdst = nc_s[l, s], nr_s[l, s]
                        for k3 in range(3):
                            S.add("pool", lambda e, cdst=cdst, xb=xb, n=n, k3=k3: e.dma_start(
                                out=cdst[k3].rearrange("(j p) -> p j", p=128), in_=xr_t[:, :, xb + n + k3], allow_slow_non_contiguous=True),
                                reads=xall, writes=[("out_c", l, kind, k3)], dma=(xr_id, "o"))
                        S.add("pool", lambda e, rdst=rdst, hc=hc: e.dma_start(
                            out=rdst.rearrange("(j p) -> p j", p=128), in_=hc[:], allow_slow_non_contiguous=True),
                            reads=[("hcar", ck)], writes=[("out_r", l, kind)], dma=("hcar", ck, "o"))

            for h in range(NH):
                rnn_conv(h)
                attn_head(h)
                rnn_chunk(h)
            attn_post()
            rnn_post()
            oaT_ids = [(oaT_id, bb["col"]) for bb in blocks]
            orT_ids = [(orT_id, j) for j in range(8)]

            if DBG: print('MARK', segs[0]['kind'], segs[0]['t0'], '# ---- step 6:', len(S.ops))
            mT_t, mT_id = mT.next()
            for hf in range(2):
                wr_t, wr_id = load_w(wsc_p[l, 0][:, hf * 512:(hf + 1) * 512], ("wscp", l, 0))
                wa_t, wa_id = load_w(wsc_p[l, 1][:, hf * 512:(hf + 1) * 512], ("wscp", l, 1))
                for c4 in range(4):
                    j = hf * 4 + c4
                    pr_, prid = pg.next()
                    for kc in range(8):
                        S.add("pe", lambda e, pr_=pr_, wr_t=wr_t, kc=kc, c4=c4: e.matmul(
                            pr_[:, 0:ncols], lhsT=wr_t[:, kc, c4 * 128:(c4 + 1) * 128], rhs=orT_t[:, kc, 0:ncols],
                            start=(kc == 0), stop=(kc == 7)), reads=orT_ids + [wr_id], writes=[prid])
                    pa_, paid = pg.next()
                    for kc in range(8):
                        S.add("pe", lambda e, pa_=pa_, wa_t=wa_t, kc=kc, c4=c4: e.matmul(
                            pa_[:, 0:ncols], lhsT=wa_t[:, kc, c4 * 128:(c4 + 1) * 128], rhs=oaT_t[:, kc, 0:ncols],
                            start=(kc == 0), stop=(kc == 7)), reads=oaT_ids + [wa_id], writes=[paid])
                    mt_t, mt_id = mtmp.next()
                    S.add("dve", lambda e, mt_t=mt_t, pr_=pr_, j=j: e.tensor_tensor(out=mt_t[:, 0, 0:ncols], in0=pr_[:, 0:ncols],
                                                                                 in1=sgr_t[:, j, 0:ncols], op=ALU.mult),
                          reads=[prid, (sgr_id, j)], writes=[(mt_id, 0)])
                    S.add("dve", lambda e, mt_t=mt_t, pa_=pa_, j=j: e.tensor_tensor(out=mt_t[:, 1, 0:ncols], in0=pa_[:, 0:ncols],
                                                                                 in1=sga_t[:, j, 0:ncols], op=ALU.mult),
                          reads=[paid, (sga_id, j)], writes=[(mt_id, 1)])
                    S.add("pool", lambda e, mt_t=mt_t, j=j: e.tensor_tensor(out=mT_t[:, j, 0:ncols], in0=mt_t[:, 0, 0:ncols],
                                                                          in1=mt_t[:, 1, 0:ncols], op=ALU.add),
                          reads=[(mt_id, 0), (mt_id, 1)], writes=[(mT_id, j)])
            mT_ids = [(mT_id, j) for j in range(8)]
            for hf in range(2):
                wo_t, wo_id = load_w(wsc_p[l, 2][:, hf * 512:(hf + 1) * 512], ("wscp", l, 2))
                for b in blocks:
                    bn = b["bn"]
                    p, pid = pg.next()
                    for kc in range(8):
                        S.add("pe", lambda e, p=p, wo_t=wo_t, kc=kc, b=b, bn=bn: e.matmul(
                            p[0:bn, :], lhsT=mT_t[:, kc, b["col"]:b["col"] + bn], rhs=wo_t[:, kc, :],
                            start=(kc == 0), stop=(kc == 7)), reads=mT_ids + [wo_id], writes=[pid])
                    S.add("dve", lambda e, p=p, b=b, bn=bn, hf=hf: e.tensor_tensor(
                        out=b["h"][0:bn, hf * 512:(hf + 1) * 512], in0=b["h"][0:bn, hf * 512:(hf + 1) * 512], in1=p[0:bn, :], op=ALU.add),
                        reads=[pid, b["hid"]], writes=[b["hid"]])
            for b in blocks:
                sg, bn = b["sg"], b["bn"]
                kind = sg["kind"]
                r0 = sg["t0"] + b["bo"]
                if not last_layer:
                    dst = hb_p[r0:r0 + bn, :] if kind in ("m", "p") else hb_s[int(kind[1]), r0:r0 + bn, :]
                    S.add("pool", lambda e, dst=dst, b=b, bn=bn: e.dma_start(out=dst, in_=b["h"][0:bn, :]),
                          reads=[b["hid"]], writes=[b["hsrc"]], dma=(b["hid"], "o"))
                elif kind != "m":
                    jk_t, jk_id = xnb.next()
                    st, sid = rmsnorm_stats(b["h"][0:bn, :], [b["hid"]], bn, float(D), jk_t[0:bn, :], jk_id)
                    y_t, y_id = yst.next()
                    S.add("dve", lambda e, y_t=y_t, b=b, st=st, bn=bn: e.scalar_tensor_tensor(
                        out=y_t[0:bn, :], in0=b["h"][0:bn, :], scalar=st[0:bn, 1:2], in1=fnwt[0:bn, :], op0=ALU.mult, op1=ALU.mult),
                        reads=[b["hid"], sid, "fnwt"], writes=[y_id])
                    dst = y_p[r0 - NMETA:r0 - NMETA + bn, :] if kind == "p" else y_s[int(kind[1]), r0:r0 + bn, :]
                    S.add("pool", lambda e, dst=dst, y_t=y_t, bn=bn: e.dma_start(out=dst, in_=y_t[0:bn, :]),
                          reads=[y_id], writes=[("out_y", kind, r0)], dma=y_id)

        for l in range(depth):
            if l + 1 < depth:
                convert_weights(l + 1)
            load_gate_w(l)
            if not NOCACHE:
                convert_cache(l)
            do_tile(l, [dict(kind="s0", col0=0, n=64, t0=0, last=True)])
            do_tile(l, [dict(kind="s1", col0=0, n=64, t0=0, last=True)])
            do_tile(l, [dict(kind="m", col0=0, n=16, t0=0, last=False)])
            for ti in range(ntile):
                do_tile(l, [dict(kind="p", col0=0, n=TT, t0=NMETA + ti * TT, last=(ti == ntile - 1))])

        S.finalize()
        print("ops", len(S.ops), "sems", S.n_sems)
        with nc.allow_low_precision(reason="bf16 matmul operands by design"), nc.Block() as block:
            S.emit(block)
    return nc


_ROPE = None


def _rope_table():
    global _ROPE
    if _ROPE is None:
        half = 32
        inv = 1.0 / (10000.0 ** (np.arange(half, dtype=np.float32) / np.float32(half)))
        pos = np.concatenate([np.arange(TP), NMETA + PAST + np.arange(DSEQ)]).astype(np.float32)
        ang = pos[:, None] * inv[None, :].astype(np.float32)
        _ROPE = np.ascontiguousarray(np.stack([np.cos(ang), np.sin(ang)], axis=1).astype(np.float32))
    return _ROPE


def kernel(x_prompt, x_sample, cache_k, cache_v, state_conv, state_rnn, meta_tokens,
           norm_w, w_in, conv_w, conv_b, w_rg, b_rg, w_ig, b_ig, lru_lambda,
           lambda_q1, lambda_k1, lambda_q2, lambda_k2, subln_w,
           w_proj_rnn, w_proj_att, w_out, final_norm_w):
    A = lambda a: np.ascontiguousarray(np.asarray(a, dtype=np.float32))
    nc = build_program()
    shared = dict(meta=A(meta_tokens), norm_w=A(norm_w), w_in=A(w_in), conv_w=A(conv_w), conv_b=A(conv_b),
                  w_rg=A(w_rg), b_rg=A(b_rg), w_ig=A(w_ig), b_ig=A(b_ig), lru_l=A(lru_lambda),
                  lq1=A(lambda_q1), lk1=A(lambda_k1), lq2=A(lambda_q2), lk2=A(lambda_k2), subln=A(subln_w),
                  w_pr=A(w_proj_rnn), w_pa=A(w_proj_att), w_out=A(w_out), fnw=A(final_norm_w), rope=_rope_table())
    x_prompt, x_sample = np.asarray(x_prompt), np.asarray(x_sample)
    cache_k, cache_v = np.asarray(cache_k), np.asarray(cache_v)
    state_conv, state_rnn = np.asarray(state_conv), np.asarray(state_rnn)
    in_maps = []
    for c in range(8):
        m = dict(shared)
        m["x_p"] = A(x_prompt[c])
        m["x_s"] = A(x_sample[2 * c:2 * c + 2])
        m["cache_k"] = A(cache_k[:, 2 * c:2 * c + 2].reshape(DEPTH, 2, PAST, D))
        m["cache_v"] = A(cache_v[:, 2 * c:2 * c + 2].reshape(DEPTH, 2, PAST, D))
        m["st_conv"] = A(state_conv[:, 2 * c:2 * c + 2])
        m["st_rnn"] = A(state_rnn[:, 2 * c:2 * c + 2])
        in_maps.append(m)
    res = run_bass_kernel_spmd(nc, in_maps, core_ids=list(range(8)))
    R = res.results
    y_p = np.stack([R[c]["y_p"] for c in range(8)])
    y_s = np.concatenate([R[c]["y_s"] for c in range(8)], axis=0)
    nk_p = np.stack([R[c]["nk_p"] for c in range(8)], axis=1).reshape(DEPTH, 8, TP, NH, 2, 64)
    nv_p = np.stack([R[c]["nv_p"] for c in range(8)], axis=1).reshape(DEPTH, 8, TP, NH, 128)
    nc_p = np.stack([R[c]["nc_p"] for c in range(8)], axis=1)
    nr_p = np.stack([R[c]["nr_p"] for c in range(8)], axis=1)
    nk_s = np.concatenate([R[c]["nk_s"] for c in range(8)], axis=1).reshape(DEPTH, 16, DSEQ, NH, 2, 64)
    nv_s = np.concatenate([R[c]["nv_s"] for c in range(8)], axis=1).reshape(DEPTH, 16, DSEQ, NH, 128)
    nc_s = np.concatenate([R[c]["nc_s"] for c in range(8)], axis=1)
    nr_s = np.concatenate([R[c]["nr_s"] for c in range(8)], axis=1)
    return (y_p, y_s, nk_p, nv_p, nc_p, nr_p, nk_s, nv_s, nc_s, nr_s)
```

```python
import contextlib
import math
import numpy as np
import concourse.bass as bass
import concourse.mybir as mybir
from concourse.bass_utils import run_bass_kernel_spmd

F32 = mybir.dt.float32
BF16 = mybir.dt.bfloat16
AF = mybir.ActivationFunctionType
ALU = mybir.AluOpType
AX = mybir.AxisListType

D = 1024
DEPTH = 4
SEQ = 4096
NMETA = 16
TP = SEQ + NMETA
DSEQ = 64
PAST = 2048
NH = 8
EPS = 1e-6
TT = 256
NW = 3
NTILE = SEQ // TT
KCOLS_P = 33 * 128
KCOLS_S = 17 * 128
SEM_ROT = 30000
DBG = False
NOCACHE = False


class _Chan:
    def __init__(self, nc, name):
        self.nc, self.name, self.sems, self.cur = nc, name, [], 0

    def bump(self, units):
        if not self.sems or self.cur + units > SEM_ROT:
            self.sems.append(self.nc.alloc_semaphore(f"{self.name}_{len(self.sems)}"))
            self.cur = 0
        self.cur += units
        return (self.sems[-1], self.cur)


class Sched:
    ENG = ("pe", "act", "dve", "pool", "sp")

    def __init__(self, nc):
        self.nc, self.ops, self.state = nc, [], {}

    limit = None
    bases = set()

    def _split(self, b):
        if isinstance(b, tuple) and isinstance(b[0], str) and b[0] in self.bases:
            return b[0], b
        return b, None

    def add(self, eng, fn, reads=(), writes=(), dma=None, grp=False):
        if self.limit is not None and len(self.ops) >= self.limit:
            return -1
        deps, raw = set(), set()
        st = self.state
        for b in reads:
            base, part = self._split(b)
            e = st.setdefault(base, [None, [], {}])
            ws = [e[0]]
            if part is None:
                ws += [pv[0] for pv in e[2].values()]
            elif part in e[2]:
                ws.append(e[2][part][0])
            for w in ws:
                if w is not None:
                    deps.add(w)
                    raw.add(w)
        for b in writes:
            base, part = self._split(b)
            e = st.setdefault(base, [None, [], {}])
            if e[0] is not None:
                deps.add(e[0])
            deps.update(e[1])
            if part is None:
                for pv in e[2].values():
                    deps.add(pv[0])
                    deps.update(pv[1])
            elif part in e[2]:
                deps.add(e[2][part][0])
                deps.update(e[2][part][1])
        i = len(self.ops)
        deps.discard(None)
        if dma is not None:
            dma = (eng, dma)
        self.ops.append(dict(eng=eng, fn=fn, deps=deps, raw=raw, dma=dma, mark=False, grp=grp))
        for b in reads:
            base, part = self._split(b)
            e = st[base]
            if part is None:
                e[1].append(i)
            else:
                e[2].setdefault(part, [None, []])[1].append(i)
        for b in writes:
            base, part = self._split(b)
            e = st[base]
            if part is None:
                e[0], e[1], e[2] = i, [], {}
            else:
                e[2][part] = [i, []]
        return i

    def finalize(self):
        nc, ops = self.nc, self.ops
        for op in ops:
            need = []
            for d in op["deps"]:
                p = ops[d]
                if p["dma"] is not None or p["eng"] != op["eng"]:
                    need.append(d)
                elif p["eng"] != "pe" and d in op["raw"]:
                    need.append(d)
            op["need"] = need
            for d in need:
                ops[d]["mark"] = True
        chans = {e: _Chan(nc, "c_" + e) for e in self.ENG}
        dchan = {}
        for op in ops:
            if op["dma"] is not None:
                key = op["dma"]
                if key not in dchan:
                    dchan[key] = _Chan(nc, f"dma{len(dchan)}")
                op["inc"] = dchan[key].bump(16)
            elif op["mark"]:
                op["inc"] = chans[op["eng"]].bump(1)
            else:
                op["inc"] = None
        gfinal = {}
        for op in ops:
            if op["grp"]:
                gfinal[op["dma"]] = op["inc"]
        seen = {e: {} for e in self.ENG}
        for op in ops:
            w = {}
            for d in op["need"]:
                sem, val = gfinal[ops[d]["dma"]] if ops[d]["grp"] else ops[d]["inc"]
                k = id(sem)
                if seen[op["eng"]].get(k, 0) >= val:
                    continue
                if k not in w or w[k][1] < val:
                    w[k] = (sem, val)
            for k, sv in w.items():
                seen[op["eng"]][k] = sv[1]
            op["waits"] = list(w.values())
        self.dchan = dchan
        self.n_sems = sum(len(c.sems) for c in chans.values()) + sum(len(c.sems) for c in dchan.values())

    def emit(self, block):
        ops = self.ops

        def run(engname, e):
            for op in ops:
                if op["eng"] != engname:
                    continue
                for sem, val in op["waits"]:
                    e.wait_ge(sem, val)
                ins = op["fn"](e)
                if op["inc"] is not None:
                    ins.then_inc(op["inc"][0], 16 if op["dma"] is not None else 1)

        @block.tensor
        def _(e):
            run("pe", e)

        @block.scalar
        def _(e):
            run("act", e)

        @block.vector
        def _(e):
            run("dve", e)

        @block.gpsimd
        def _(e):
            run("pool", e)

        @block.sync
        def _(e):
            run("sp", e)
            for c in self.dchan.values():
                e.wait_ge(c.sems[-1], c.cur)


class Rot:
    def __init__(self, es, nc, name, shape, dt, n, psum=False):
        mk = nc.psum_tensor if psum else nc.sbuf_tensor
        self.t = [es.enter_context(mk(f"{name}{i}", shape, dt)) for i in range(n)]
        self.ids = [f"{name}{i}" for i in range(n)]
        Sched.bases.update(self.ids)
        self.k = 0

    def next(self):
        i = self.k % len(self.t)
        self.k += 1
        return self.t[i], self.ids[i]


def build_program(depth=DEPTH, ntile=NTILE, small=True):
    nc = bass.Bass("TRN2", target_bir_lowering=False)

    def din(name, shape):
        return nc.dram_tensor(name, shape, F32, kind="ExternalInput").ap()

    def dout(name, shape):
        return nc.dram_tensor(name, shape, F32, kind="ExternalOutput").ap()

    def dscr(name, shape, dt):
        return nc.dram_tensor(name, shape, dt).ap()

    x_p = din("x_p", [SEQ, D])
    x_s = din("x_s", [2, DSEQ, D])
    cache_k = din("cache_k", [DEPTH, 2, PAST, D])
    cache_v = din("cache_v", [DEPTH, 2, PAST, D])
    st_conv = din("st_conv", [DEPTH, 2, 3, D])
    st_rnn = din("st_rnn", [DEPTH, 2, D])
    meta = din("meta", [NMETA, D])
    norm_w = din("norm_w", [DEPTH, D])
    w_in = din("w_in", [DEPTH, D, 8 * D])
    conv_w = din("conv_w", [DEPTH, 4, D])
    conv_b = din("conv_b", [DEPTH, D])
    w_rg = din("w_rg", [DEPTH, 8, 128, 128])
    b_rg = din("b_rg", [DEPTH, D])
    w_ig = din("w_ig", [DEPTH, 8, 128, 128])
    b_ig = din("b_ig", [DEPTH, D])
    lru_l = din("lru_l", [DEPTH, D])
    lam_in = [din(n, [DEPTH, 64]) for n in ("lq1", "lk1", "lq2", "lk2")]
    subln = din("subln", [DEPTH, 128])
    w_pr = din("w_pr", [DEPTH, D, D])
    w_pa = din("w_pa", [DEPTH, D, D])
    w_out = din("w_out", [DEPTH, D, D])
    fnw = din("fnw", [D])
    rope = din("rope", [TP + DSEQ, 2, 32])

    y_p = dout("y_p", [SEQ, D])
    y_s = dout("y_s", [2, DSEQ, D])
    nk_p = dout("nk_p", [DEPTH, TP, D])
    nv_p = dout("nv_p", [DEPTH, TP, D])
    nc_p = dout("nc_p", [DEPTH, 3, D])
    nr_p = dout("nr_p", [DEPTH, D])
    nk_s = dout("nk_s", [DEPTH, 2, DSEQ, D])
    nv_s = dout("nv_s", [DEPTH, 2, DSEQ, D])
    nc_s = dout("nc_s", [DEPTH, 2, 3, D])
    nr_s = dout("nr_s", [DEPTH, 2, D])

    wsc_in = dscr("wsc_in", [DEPTH, 16, 128, 8, 512], BF16)
    wsc_p = dscr("wsc_p", [DEPTH, 3, 2, 128, 8, 512], BF16)
    hb_p = dscr("hb_p", [TP, D], F32)
    hb_s = dscr("hb_s", [2, DSEQ, D], F32)
    kT_p = dscr("kT_p", [NH, 128, KCOLS_P], BF16)
    v_p = dscr("v_p", [NH, 128, 33, 130], BF16)
    kT_s = dscr("kT_s", [2, NH, 128, KCOLS_S], BF16)
    v_s = dscr("v_s", [2, NH, 128, 17, 130], BF16)

    S = Sched(nc)
    TW = TT
    with contextlib.ExitStack() as es:
        def T(name, shape, dt):
            return es.enter_context(nc.sbuf_tensor(name, shape, dt))

        identf = T("identf", [128, 128], F32)
        ident = T("ident", [128, 128], BF16)
        zeros = T("zeros", [128, 1040], BF16)
        nwt = T("nwt", [128, DEPTH, 8], F32)
        cwt = T("cwt", [128, DEPTH, 4, 8], F32)
        cbt = T("cbt", [128, DEPTH, 8], F32)
        nbrg = T("nbrg", [128, DEPTH, 8], F32)
        nbig = T("nbig", [128, DEPTH, 8], F32)
        cch = T("cch", [128, DEPTH, 8], F32)
        wg = T("wg", [128, 2, 8, 128], BF16)
        subw = T("subw", [128, DEPTH, 128], F32)
        sub_t = T("sub_t", [128, 128], F32)
        lamt = T("lamt", [128, 4, 64], F32)
        lams = T("lams", [128, 2], F32)
        neglam = T("neglam", [128, DEPTH], F32)
        fnwt = T("fnwt", [128, D], F32)
        hcar = {k: T("hcar_" + k, [128, 8], F32) for k in ("p", "s0", "s1")}
        xhalo = T("xhalo_p", [128, 8, 3], F32)

        S.add("pool", lambda e: e.memset(identf[:], 1.0), writes=["identf"])
        S.add("pool", lambda e: e.affine_select(out=identf[:], in_=identf[:], pattern=[[-1, 128]],
                                                compare_op=ALU.is_equal, fill=0.0, base=0, channel_multiplier=1),
              reads=["identf"], writes=["identf"])
        S.add("dve", lambda e: e.tensor_copy(out=ident[:], in_=identf[:]), reads=["identf"], writes=["ident"])
        S.add("pool", lambda e: e.memset(zeros[:], 0.0), writes=["zeros"])

        def sdma(out, in_, w, r=()):
            S.add("sp", lambda e: e.dma_start(out=out, in_=in_, allow_slow_non_contiguous=True),
                  reads=list(r), writes=[w], dma=w)

        sdma(nwt[:], norm_w.rearrange("l (j p) -> p l j", p=128), "nwt")
        sdma(cwt[:], conv_w.rearrange("l k (j p) -> p l k j", p=128), "cwt")
        sdma(cbt[:], conv_b.rearrange("l (j p) -> p l j", p=128), "cbt")
        sdma(nbrg[:], b_rg.rearrange("l (j p) -> p l j", p=128), "nbrg")
        sdma(nbig[:], b_ig.rearrange("l (j p) -> p l j", p=128), "nbig")
        sdma(cch[:], lru_l.rearrange("l (j p) -> p l j", p=128), "cch")
        sdma(fnwt[:], fnw.partition_broadcast(128), "fnwt")
        S.add("dve", lambda e: e.tensor_scalar(out=nbrg[:], in0=nbrg[:], scalar1=-1.0, scalar2=None, op0=ALU.mult),
              reads=["nbrg"], writes=["nbrg"])
        S.add("dve", lambda e: e.tensor_scalar(out=nbig[:], in0=nbig[:], scalar1=-1.0, scalar2=None, op0=ALU.mult),
              reads=["nbig"], writes=["nbig"])
        S.add("act", lambda e: e.activation(out=cch[:], in_=cch[:], func=AF.Exp, scale=-1.0), reads=["cch"], writes=["cch"])
        S.add("act", lambda e: e.activation(out=cch[:], in_=cch[:], func=AF.Ln, bias=1.0), reads=["cch"], writes=["cch"])
        S.add("dve", lambda e: e.tensor_scalar(out=cch[:], in0=cch[:], scalar1=-8.0, scalar2=None, op0=ALU.mult),
              reads=["cch"], writes=["cch"])
        for l in range(DEPTH):
            lam_init = 0.8 - 0.6 * math.exp(-0.3 * l)
            sdma(sub_t[:], subln[l].partition_broadcast(128), "sub_t")
            S.add("dve", lambda e, l=l, li=lam_init: e.tensor_scalar(
                out=subw[:, l, :], in0=sub_t[:],
                scalar1=1.0 - li, scalar2=None, op0=ALU.mult), reads=["sub_t"], writes=[("subw", l)])
            for i4 in range(4):
                sdma(lamt[:, i4, :], lam_in[i4][l].partition_broadcast(128), ("lamt", i4))
            for pr in range(2):
                S.add("dve", lambda e, pr=pr: e.tensor_tensor(out=lamt[:, 2 * pr, :], in0=lamt[:, 2 * pr, :],
                                                            in1=lamt[:, 2 * pr + 1, :], op=ALU.mult),
                      reads=[("lamt", 2 * pr), ("lamt", 2 * pr + 1)], writes=[("lamt", 2 * pr)])
                S.add("dve", lambda e, pr=pr: e.tensor_reduce(out=lams[:, pr:pr + 1], in_=lamt[:, 2 * pr, :],
                                                            axis=AX.X, op=ALU.add),
                      reads=[("lamt", 2 * pr)], writes=[("lams", pr)])
            S.add("act", lambda e: e.activation(out=lams[:], in_=lams[:], func=AF.Exp),
                  reads=[("lams", 0), ("lams", 1)], writes=[("lams", 0), ("lams", 1)])
            S.add("dve", lambda e, l=l: e.tensor_tensor(out=neglam[:, l:l + 1], in0=lams[:, 1:2], in1=lams[:, 0:1],
                                                      op=ALU.subtract),
                  reads=[("lams", 0), ("lams", 1)], writes=[("neglam", l)])
            S.add("dve", lambda e, l=l, li=lam_init: e.tensor_scalar(out=neglam[:, l:l + 1], in0=neglam[:, l:l + 1],
                                                                   scalar1=-li, scalar2=None, op0=ALU.add),
                  reads=[("neglam", l)], writes=[("neglam", l)])

        def load_gate_w(l):
            for gi, wsrc in enumerate((w_rg, w_ig)):
                S.add("pool", lambda e, l=l, gi=gi, wsrc=wsrc: e.dma_start(
                    out=wg[:, gi, :, :], in_=wsrc[l].rearrange("n i j -> i n j")),
                    writes=[("wg", gi)], dma=("wg", gi))

        def convert_weights(l):
            for ch in range(16):
                S.add("pool", lambda e, l=l, ch=ch: e.dma_start(
                    out=wsc_in[l, ch], in_=w_in[l][:, ch * 512:(ch + 1) * 512].rearrange("(kc p) n -> p kc n", p=128)),
                    writes=[("wsc", l, ch)], dma=("wsc", l), grp=True)
            for i3, wsrc in enumerate((w_pr, w_pa, w_out)):
                for hf in range(2):
                    S.add("pool", lambda e, l=l, i3=i3, hf=hf, wsrc=wsrc: e.dma_start(
                        out=wsc_p[l, i3, hf], in_=wsrc[l][:, hf * 512:(hf + 1) * 512].rearrange("(kc p) n -> p kc n", p=128)),
                        writes=[("wscp", l, i3, hf)], dma=("wsc", l), grp=True)

        convert_weights(0)
        S.add("sp", lambda e: e.dma_start(out=kT_p.rearrange("h p t -> p h t")[:, :, 0:128],
                                          in_=zeros[:, 0:1024].rearrange("p (h x) -> p h x", h=8)),
              reads=["zeros"], writes=["kTp"], dma="z0")
        S.add("sp", lambda e: e.dma_start(out=v_p.rearrange("h p k x -> p h k x")[:, :, 0, :],
                                          in_=zeros[:, 0:1040].rearrange("p (h x) -> p h x", h=8)),
              reads=["zeros"], writes=["vp"], dma="z1")
        for s_ in range(2):
            S.add("sp", lambda e, s_=s_: e.dma_start(out=kT_s[s_].rearrange("h p t -> p h t")[:, :, 2048:2176],
                                                   in_=zeros[:, 0:1024].rearrange("p (h x) -> p h x", h=8)),
                  reads=["zeros"], writes=[("kTs", s_)], dma=("z2", s_))
            S.add("sp", lambda e, s_=s_: e.dma_start(out=v_s[s_].rearrange("h p k x -> p h k x")[:, :, 16, :],
                                                   in_=zeros[:, 0:1040].rearrange("p (h x) -> p h x", h=8)),
                  reads=["zeros"], writes=[("vs", s_)], dma=("z3", s_))

        wpool = Rot(es, nc, "wch", [128, 8, 512], BF16, NW)
        hres = Rot(es, nc, "hres", [128, D], F32, 3)
        xnb = Rot(es, nc, "xnb", [128, D], BF16, 2)
        st1 = Rot(es, nc, "st1", [128, 2], F32, 4)
        xnT = Rot(es, nc, "xnT", [128, 8, TW], BF16, 1)
        rot = Rot(es, nc, "rot", [128, D], F32, 2)
        rotb = Rot(es, nc, "rotb", [128, 2048], BF16, 2)
        rtmp = Rot(es, nc, "rtmp", [128, 4, 256], F32, 2)
        ropet = Rot(es, nc, "ropet", [128, 2, 32], F32, 3)
        qT = Rot(es, nc, "qT", [128, 8, TW], BF16, 1)
        kTst = Rot(es, nc, "kTst", [128, 8, 256], BF16, 2)
        vst = Rot(es, nc, "vst", [128, 8, 2, 130], BF16, 2)
        vf = Rot(es, nc, "vf", [128, D], F32, 3)
        gatea = Rot(es, nc, "gatea", [128, D], BF16, 2)
        etm = Rot(es, nc, "etm", [128, 512], F32, 2)
        XW = TW + 9
        xr = Rot(es, nc, "xr", [128, 8, XW], F32, 1)
        szr = Rot(es, nc, "szr", [128, 8, TW], BF16, 1)
        sgr = Rot(es, nc, "sgr", [128, 8, TW], BF16, 1)
        sga = Rot(es, nc, "sga", [128, 8, TW], BF16, 1)
        rn = {k: Rot(es, nc, "rn_" + k, [128, TW], F32, 2) for k in ("xc", "r", "i", "a", "m", "hs")}
        xcb = Rot(es, nc, "xcb", [128, TW], BF16, 2)
        orT = Rot(es, nc, "orT", [128, 8, TW], BF16, 1)
        oaT = Rot(es, nc, "oaT", [128, 8, TW], BF16, 1)
        mT = Rot(es, nc, "mT", [128, 8, TW], BF16, 1)
        kbuf = Rot(es, nc, "kbuf", [128, 17 * 128], BF16, 2)
        vbuf = Rot(es, nc, "vbuf", [128, 17, 130], BF16, 2)
        ptb = Rot(es, nc, "ptb", [128, 2, TW], BF16, 3)
        obuf = Rot(es, nc, "obuf", [128, 8, 128], F32, 2)
        oab = Rot(es, nc, "oab", [128, D], BF16, 1)
        arec = Rot(es, nc, "arec", [128, 4], F32, 4)
        atmp = Rot(es, nc, "atmp", [128, 128], F32, 2)
        st8 = Rot(es, nc, "st8", [128, 8], F32, 2)
        mtmp = Rot(es, nc, "mtmp", [128, 2, TW], F32, 2)
        yst = vf
        cst = vf
        cstb = xnb
        psc = Rot(es, nc, "psc", [128, 2, 512], F32, 2, psum=True)
        pacc_t = es.enter_context(nc.psum_tensor("pacc", [128, 2, 512], F32))
        pg = Rot(es, nc, "pg", [128, 512], F32, 2, psum=True)
        print("sbuf bytes remaining", nc.sbuf_bytes_remaining)

        for v_t, v_id in zip(vst.t, vst.ids):
            S.add("pool", lambda e, v_t=v_t: e.memset(v_t[:, :, :, 128:129], 1.0), writes=[v_id])
            S.add("pool", lambda e, v_t=v_t: e.memset(v_t[:, :, :, 129:130], 0.0), writes=[v_id])

        def load_w(src, src_id):
            wt, wid = wpool.next()
            S.add("sp", lambda e: e.dma_start(out=wt[:], in_=src),
                  reads=[src_id], writes=[wid], dma=wid)
            return wt, wid

        def pgT(p):
            return p[:].bitcast(BF16).rearrange("p (j x) -> p j x", j=8)

        def rmsnorm_stats(src_ap, src_ids, n, scale_div, width_ap_out, junk_id):
            st, sid = st1.next()
            S.add("act", lambda e: e.activation(out=width_ap_out, in_=src_ap, func=AF.Square, accum_out=st[0:n, 0:1]),
                  reads=src_ids, writes=[junk_id, sid])
            S.add("act", lambda e: e.activation(out=st[0:n, 1:2], in_=st[0:n, 0:1], func=AF.Ln, scale=1.0 / scale_div, bias=EPS),
                  reads=[sid], writes=[sid])
            S.add("act", lambda e: e.activation(out=st[0:n, 1:2], in_=st[0:n, 1:2], func=AF.Exp, scale=-0.5),
                  reads=[sid], writes=[sid])
            return st, sid

        def sigmoid_from_psum(p, pid, n_part, ncol, out_ap, out_id, mul_by_x=False, extra=None, extra_id=None):
            if not mul_by_x:
                S.add("act", lambda e: e.activation(out=out_ap, in_=p, func=AF.Sigmoid), reads=[pid], writes=[out_id])
                return
            if extra is None:
                S.add("act", lambda e: e.activation(out=out_ap, in_=p, func=AF.Silu), reads=[pid], writes=[out_id])
                return
            et, eid = etm.next()
            S.add("act", lambda e: e.activation(out=et[0:n_part, 0:ncol], in_=p, func=AF.Silu), reads=[pid], writes=[eid])
            S.add("pool", lambda e: e.tensor_tensor(
                out=out_ap.rearrange("p (h x) -> p h x", x=128),
                in0=et[0:n_part, 0:ncol].rearrange("p (h x) -> p h x", x=128),
                in1=extra.unsqueeze(1).to_broadcast([n_part, ncol // 128, 128]), op=ALU.mult),
                reads=[eid, extra_id], writes=[out_id])

        def convert_cache(l):
            for s in range(2):
                for g2 in range(8):
                    ks_t, ks_id = kTst.next()
                    vs_t, vs_id = vst.next()
                    for kk in range(2):
                        kb = 2 * g2 + kk
                        c_t, c_id = cst.next()
                        S.add("sp", lambda e, c_t=c_t, kb=kb, s=s: e.dma_start(out=c_t[:], in_=cache_k[l, s, kb * 128:(kb + 1) * 128, :]),
                              writes=[c_id], dma=c_id)
                        cb_t, cb_id = cstb.next()
                        S.add("dve", lambda e, c_t=c_t, cb_t=cb_t: e.tensor_copy(out=cb_t[:], in_=c_t[:]),
                              reads=[c_id], writes=[cb_id])
                        p, pid = pg.next()
                        for h in range(NH):
                            S.add("pe", lambda e, p=p, cb_t=cb_t, h=h: e.transpose(out=pgT(p)[:, h, :], in_=cb_t[:, h * 128:(h + 1) * 128],
                                                                                 identity=ident[:]),
                                  reads=[cb_id, "ident"], writes=[pid])
                        S.add("act", lambda e, p=p, ks_t=ks_t, kk=kk: e.copy(out=ks_t[:, :, kk * 128:(kk + 1) * 128], in_=pgT(p)),
                              reads=[pid], writes=[ks_id])
                        c2_t, c2_id = cst.next()
                        S.add("sp", lambda e, c2_t=c2_t, kb=kb, s=s: e.dma_start(out=c2_t[:], in_=cache_v[l, s, kb * 128:(kb + 1) * 128, :]),
                              writes=[c2_id], dma=c2_id)
                        S.add("pool", lambda e, c2_t=c2_t, vs_t=vs_t, kk=kk: e.tensor_copy(
                            out=vs_t[:, :, kk, 0:128], in_=c2_t[:].rearrange("p (h x) -> p h x", h=8)),
                            reads=[c2_id], writes=[vs_id])
                    S.add("pool", lambda e, ks_t=ks_t, g2=g2, s=s: e.dma_start(
                        out=kT_s[s].rearrange("h p t -> p h t")[:, :, g2 * 256:(g2 + 1) * 256], in_=ks_t[:]),
                        reads=[ks_id], writes=[("kTs", s)], dma=ks_id)
                    S.add("pool", lambda e, vs_t=vs_t, g2=g2, s=s: e.dma_start(
                        out=v_s[s].rearrange("h p k x -> p h k x")[:, :, 2 * g2:2 * g2 + 2, :], in_=vs_t[:]),
                        reads=[vs_id], writes=[("vs", s)], dma=vs_id)

        def do_tile(l, segs):
            last_layer = (l == depth - 1)
            ncols = sum(sg["n"] for sg in segs)
            blocks = []
            for si, sg in enumerate(segs):
                for bo in range(0, sg["n"], 128):
                    bn = min(128, sg["n"] - bo)
                    blocks.append(dict(si=si, sg=sg, bo=bo, bn=bn, col=sg["col0"] + bo))
            xt_t, xt_id = xnT.next()
            if DBG: print('MARK', segs[0]['kind'], segs[0]['t0'], '# ---- step 1:', len(S.ops))
            for b in blocks:
                sg, bn = b["sg"], b["bn"]
                h_t, h_id = hres.next()
                b["h"], b["hid"] = h_t, h_id
                r0 = sg["t0"] + b["bo"]
                if sg["kind"] in ("m", "p"):
                    hid_src = ("hp", r0 // 128 if sg["kind"] == "p" else "m")
                    if l == 0:
                        src = meta[0:bn, :] if sg["kind"] == "m" else x_p[r0 - NMETA:r0 - NMETA + bn, :]
                    else:
                        src = hb_p[r0:r0 + bn, :]
                else:
                    s = int(sg["kind"][1])
                    hid_src = ("hs", s)
                    src = x_s[s, r0:r0 + bn, :] if l == 0 else hb_s[s, r0:r0 + bn, :]
                b["hsrc"] = hid_src
                S.add("sp", lambda e, h_t=h_t, src=src, bn=bn: e.dma_start(out=h_t[0:bn, :], in_=src),
                      reads=[hid_src], writes=[h_id], dma=h_id)
                xb_t, xb_id = xnb.next()
                st, sid = rmsnorm_stats(h_t[0:bn, :], [h_id], bn, float(D), xb_t[0:bn, :], xb_id)
                S.add("dve", lambda e, xb_t=xb_t, h_t=h_t, st=st, bn=bn: e.tensor_scalar(
                    out=xb_t[0:bn, :], in0=h_t[0:bn, :], scalar1=st[0:bn, 1:2], scalar2=None, op0=ALU.mult),
                    reads=[h_id, sid], writes=[xb_id])
                p, pid = pg.next()
                for j in range(8):
                    S.add("pe", lambda e, p=p, xb_t=xb_t, j=j, bn=bn: e.transpose(
                        out=pgT(p)[:, j, 0:bn], in_=xb_t[0:bn, j * 128:(j + 1) * 128], identity=ident[0:bn, 0:bn]),
                        reads=[xb_id, "ident"], writes=[pid])
                S.add("dve", lambda e, p=p, b=b, bn=bn: e.tensor_tensor(
                    out=xt_t[:, :, b["col"]:b["col"] + bn], in0=pgT(p)[:, :, 0:bn],
                    in1=nwt[:, l, :].unsqueeze(2).to_broadcast([128, 8, bn]), op=ALU.mult),
                    reads=[pid, "nwt"], writes=[(xt_id, b["col"])])
            xt_ids = [(xt_id, b["col"]) for b in blocks]

            if DBG: print('MARK', segs[0]['kind'], segs[0]['t0'], '# ---- step 2:', len(S.ops))
            qT_t, qT_id = qT.next()
            for b in blocks:
                sg, bn = b["sg"], b["bn"]
                b["vf"], b["vfid"] = vf.next()
                b["ga"], b["gaid"] = gatea.next()
                b["ro"], b["roid"] = rot.next()
                b["rb"], b["rbid"] = rotb.next()
                r0 = sg["t0"] + b["bo"]
                rrow = r0 if sg["kind"] in ("m", "p") else TP + r0
                b["rp"], b["rpid"] = ropet.next()
                S.add("sp", lambda e, rp_t=b["rp"], rrow=rrow, bn=bn: e.dma_start(out=rp_t[0:bn], in_=rope[rrow:rrow + bn]),
                      writes=[b["rpid"]], dma=b["rpid"])
            for sg in segs:
                sg["vst"], sg["vstid"] = vst.next()
                sg["kst"], sg["kstid"] = kTst.next()
            for g in range(8):
                wt, wid = load_w(wsc_in[l, 4 + g], ("wsc", l, 4 + g))
                for b in blocks:
                    bn = b["bn"]
                    p, pid = pg.next()
                    for kc in range(8):
                        S.add("pe", lambda e, p=p, wt=wt, kc=kc, b=b, bn=bn: e.matmul(
                            p[0:bn, :], lhsT=xt_t[:, kc, b["col"]:b["col"] + bn], rhs=wt[:, kc, :],
                            start=(kc == 0), stop=(kc == 7)), reads=[(xt_id, b["col"]), wid], writes=[pid])
                    if g < 4:
                        rp_t = b["rp"]
                        src = p[0:bn, :].rearrange("p (a c x) -> p a c x", a=8, c=2)
                        x1, x2 = src[:, :, 0, :], src[:, :, 1, :]
                        cosb = rp_t[0:bn, 0:1, :].to_broadcast([bn, 8, 32])
                        sinb = rp_t[0:bn, 1:2, :].to_broadcast([bn, 8, 32])
                        tm_t, tm_id = rtmp.next()
                        tt = [tm_t[0:bn, i4, :].rearrange("p (a x) -> p a x", a=8) for i4 in range(4)]
                        for i4, (xa, tb) in enumerate(((x1, cosb), (x2, sinb), (x2, cosb), (x1, sinb))):
                            S.add("dve", lambda e, o=tt[i4], xa=xa, tb=tb: e.tensor_tensor(out=o, in0=xa, in1=tb, op=ALU.mult),
                                  reads=[pid, b["rpid"]], writes=[(tm_id, i4)])
                        if g < 2:
                            dst = b["rb"][0:bn, g * 512:(g + 1) * 512].rearrange("p (a c x) -> p a c x", a=8, c=2)
                            did = (b["rbid"], g)
                        else:
                            dst = b["ro"][0:bn, (g - 2) * 512:(g - 1) * 512].rearrange("p (a c x) -> p a c x", a=8, c=2)
                            did = (b["roid"], g)
                        S.add("pool", lambda e, dst=dst, tt=tt: e.tensor_tensor(out=dst[:, :, 0, :], in0=tt[0], in1=tt[1], op=ALU.subtract),
                              reads=[(tm_id, 0), (tm_id, 1)], writes=[did])
                        S.add("pool", lambda e, dst=dst, tt=tt: e.tensor_tensor(out=dst[:, :, 1, :], in0=tt[2], in1=tt[3], op=ALU.add),
                              reads=[(tm_id, 2), (tm_id, 3)], writes=[did])
                        if g >= 2:
                            S.add("pool", lambda e, b=b, bn=bn, g=g: e.tensor_copy(
                                out=b["rb"][0:bn, 1024 + (g - 2) * 512:1024 + (g - 1) * 512], in_=b["ro"][0:bn, (g - 2) * 512:(g - 1) * 512]),
                                reads=[did], writes=[(b["rbid"], g)])
                    elif g < 6:
                        S.add("act", lambda e, p=p, b=b, bn=bn, g=g: e.copy(out=b["vf"][0:bn, (g - 4) * 512:(g - 3) * 512], in_=p[0:bn, :]),
                              reads=[pid], writes=[(b["vfid"], g)])
                        sg = b["sg"]
                        kk = b["bo"] // 128
                        S.add("pool", lambda e, b=b, sg=sg, kk=kk, bn=bn, g=g: e.tensor_copy(
                            out=sg["vst"][0:bn, 4 * (g - 4):4 * (g - 3), kk, 0:128],
                            in_=b["vf"][0:bn, (g - 4) * 512:(g - 3) * 512].rearrange("p (h x) -> p h x", h=4)),
                            reads=[(b["vfid"], g)], writes=[sg["vstid"]])
                    else:
                        c0 = (g - 6) * 512
                        sigmoid_from_psum(p[0:bn, :], pid, bn, 512, b["ga"][0:bn, c0:c0 + 512], (b["gaid"], g),
                                          mul_by_x=True, extra=subw[0:bn, l, :], extra_id=("subw", l))
            for b in blocks:
                sg, bn = b["sg"], b["bn"]
                kind = sg["kind"]
                r0 = sg["t0"] + b["bo"]
                rb_t, rb_id = b["rb"], b["rbid"]
                rbids = [(rb_id, g) for g in range(4)]
                if kind in ("m", "p"):
                    kdst, vdst = nk_p[l, r0:r0 + bn, :], nv_p[l, r0:r0 + bn, :]
                else:
                    s = int(kind[1])
                    kdst, vdst = nk_s[l, s, r0:r0 + bn, :], nv_s[l, s, r0:r0 + bn, :]
                S.add("pool", lambda e, kdst=kdst, b=b, bn=bn: e.dma_start(out=kdst, in_=b["ro"][0:bn, :]),
                      reads=[(b["roid"], 2), (b["roid"], 3)], writes=[("out_k", l, kind, r0)], dma=b["roid"])
                S.add("pool", lambda e, vdst=vdst, b=b, bn=bn: e.dma_start(out=vdst, in_=b["vf"][0:bn, :]),
                      reads=[(b["vfid"], 4), (b["vfid"], 5)], writes=[("out_v", l, kind, r0)], dma=b["vfid"])
                for which in range(2):
                    p, pid = pg.next()
                    for h in range(NH):
                        S.add("pe", lambda e, p=p, rb_t=rb_t, h=h, which=which, bn=bn: e.transpose(
                            out=pgT(p)[:, h, 0:bn], in_=rb_t[0:bn, which * 1024 + h * 128: which * 1024 + (h + 1) * 128],
                            identity=ident[0:bn, 0:bn]), reads=rbids + ["ident"], writes=[pid])
                    if which == 0:
                        S.add("act", lambda e, p=p, b=b, bn=bn: e.copy(out=qT_t[:, :, b["col"]:b["col"] + bn], in_=pgT(p)[:, :, 0:bn]),
                              reads=[pid], writes=[(qT_id, b["col"])])
                    else:
                        S.add("act", lambda e, p=p, sg=sg, b=b, bn=bn: e.copy(out=sg["kst"][:, :, b["bo"]:b["bo"] + bn], in_=pgT(p)[:, :, 0:bn]),
                              reads=[pid], writes=[sg["kstid"]])
            if DBG: print('MARK', segs[0]['kind'], segs[0]['t0'], '# store K^T / ', len(S.ops))
            for sg in segs:
                kind, n = sg["kind"], sg["n"]
                if kind == "m":
                    kcol, kb0, kTd, vd, kid, vid_ = 0, 0, kT_p, v_p, "kTp", "vp"
                elif kind == "p":
                    kb0 = 1 + (sg["t0"] - NMETA) // 128
                    kcol, kTd, vd, kid, vid_ = kb0 * 128, kT_p, v_p, "kTp", "vp"
                else:
                    s = int(kind[1])
                    kb0, kcol, kTd, vd, kid, vid_ = 16, 2048, kT_s[s], v_s[s], ("kTs", s), ("vs", s)
                sg["kb0"] = kb0
                nkb = (n + 127) // 128
                pn = min(n, 128)
                S.add("pool", lambda e, sg=sg, kTd=kTd, kcol=kcol, n=n: e.dma_start(
                    out=kTd.rearrange("h p t -> p h t")[:, :, kcol:kcol + n], in_=sg["kst"][:, :, 0:n]),
                    reads=[sg["kstid"]], writes=[kid], dma=sg["kstid"])
                S.add("pool", lambda e, sg=sg, vd=vd, kb0=kb0, nkb=nkb, pn=pn: e.dma_start(
                    out=vd.rearrange("h p k x -> p h k x")[0:pn, :, kb0:kb0 + nkb, :], in_=sg["vst"][0:pn, :, 0:nkb, :]),
                    reads=[sg["vstid"]], writes=[vid_], dma=sg["vstid"])

            if DBG: print('MARK', segs[0]['kind'], segs[0]['t0'], '# ---- step 3:', len(S.ops))
            xr_t, xr_id = xr.next()
            szr_t, szr_id = szr.next()
            sgr_t, sgr_id = sgr.next()
            sga_t, sga_id = sga.next()
            for si, sg in enumerate(segs):
                sg["xb"] = sg["col0"] + 3 * si
            for gg in range(8):
                colbase = gg * 512 if gg < 4 else 6 * D + (gg - 4) * 512
                wt, wid = load_w(wsc_in[l, colbase // 512], ("wsc", l, colbase // 512))
                for c4 in range(4):
                    j = (gg % 2) * 4 + c4
                    p, pid = pg.next()
                    for kc in range(8):
                        S.add("pe", lambda e, p=p, wt=wt, kc=kc, c4=c4: e.matmul(
                            p[:, 0:ncols], lhsT=wt[:, kc, c4 * 128:(c4 + 1) * 128], rhs=xt_t[:, kc, 0:ncols],
                            start=(kc == 0), stop=(kc == 7)), reads=xt_ids + [wid], writes=[pid])
                    if gg < 2:
                        for sg in segs:
                            S.add("act", lambda e, p=p, sg=sg, j=j: e.copy(
                                out=xr_t[:, j, sg["xb"] + 3:sg["xb"] + 3 + sg["n"]], in_=p[:, sg["col0"]:sg["col0"] + sg["n"]]),
                                reads=[pid], writes=[(xr_id, j)])
                    elif gg < 4:
                        sigmoid_from_psum(p[:, 0:ncols], pid, 128, ncols, szr_t[:, j, 0:ncols], (szr_id, j), mul_by_x=True)
                    elif gg < 6:
                        sigmoid_from_psum(p[:, 0:ncols], pid, 128, ncols, sgr_t[:, j, 0:ncols], (sgr_id, j))
                    else:
                        sigmoid_from_psum(p[:, 0:ncols], pid, 128, ncols, sga_t[:, j, 0:ncols], (sga_id, j))

            if DBG: print('MARK', segs[0]['kind'], segs[0]['t0'], '# ---- step 5 ', len(S.ops))
            oaT_t, oaT_id = oaT.next()
            assert len(segs) == 1
            sg = segs[0]
            if True:
                kind, n, col0 = sg["kind"], sg["n"], sg["col0"]
                if kind == "m":
                    kbs = [(0, 16, 0, False)]
                    kTd, vd, kid, vid_ = kT_p, v_p, "kTp", "vp"
                elif kind == "p":
                    f0 = sg["t0"] - NMETA
                    kbs = [(0, 16, 0, False)] + [(kb, 128, 0, False) for kb in range(1, 1 + f0 // 128)]
                    for m in range(n // 128):
                        kbs.append((1 + f0 // 128 + m, 128, 128 * m, True))
                    kTd, vd, kid, vid_ = kT_p, v_p, "kTp", "vp"
                else:
                    s = int(kind[1])
                    kbs = [(kb, 128, 0, False) for kb in range(16)] + [(16, 64, 0, False)]
                    kTd, vd, kid, vid_ = kT_s[s], v_s[s], ("kTs", s), ("vs", s)
                nkb_all = kbs[-1][0] + 1
                nqb = (n + 127) // 128
                qbn = [min(128, n - 128 * q) for q in range(nqb)]
                ob = []
                for q in range(nqb):
                    o_t, o_id = obuf.next()
                    ob.append((o_t, o_id))
                def attn_head(h):
                    for q in range(nqb):
                        S.add("pe", lambda e, q=q, qn=qbn[q]: e.matmul(pacc_t[0:qn, q, :], lhsT=zeros[:, 0:qn], rhs=zeros[:, 0:512],
                                                                     start=True, stop=True), reads=["zeros"], writes=[("pacc", q)])
                    items = []
                    for lo in (0, 17):
                        part = [x for x in kbs if lo <= x[0] < lo + 17]
                        if not part:
                            continue
                        npk = part[-1][0] - lo + 1
                        kb_t, kb_id = kbuf.next()
                        vb_t, vb_id = vbuf.next()
                        S.add("sp", lambda e, kb_t=kb_t, kTd=kTd, h=h, lo=lo, npk=npk: e.dma_start(
                            out=kb_t[:, 0:npk * 128], in_=kTd[h, :, lo * 128:(lo + npk) * 128]), reads=[kid], writes=[kb_id], dma=kb_id)
                        S.add("sp", lambda e, vb_t=vb_t, vd=vd, h=h, lo=lo, npk=npk: e.dma_start(
                            out=vb_t[:, 0:npk, :], in_=vd[h, :, lo:lo + npk, :]), reads=[vid_], writes=[vb_id], dma=vb_id)
                        for (kb, kk, qlo, diag) in part:
                            items.append((kb_t, kb_id, vb_t, vb_id, kb - lo, kk, qlo, diag))
                    qids = [(qT_id, bb["col"]) for bb in blocks if bb["sg"] is sg]

                    def score(it):
                        kb_t, kb_id, vb_t, vb_id, kl, kk, qlo, diag = it
                        nqc = n - qlo
                        ps_t, ps_id = psc.next()
                        for c in range(2):
                            S.add("pe", lambda e, ps_t=ps_t, kb_t=kb_t, c=c, kl=kl, kk=kk, qlo=qlo, nqc=nqc: e.matmul(
                                ps_t[0:kk, c, 0:nqc], lhsT=kb_t[64 * c:64 * c + 64, kl * 128:kl * 128 + kk],
                                rhs=qT_t[64 * c:64 * c + 64, h, col0 + qlo:col0 + n], start=True, stop=True),
                                reads=[kb_id] + qids, writes=[ps_id])
                        return ps_t, ps_id

                    nxt = score(items[0])
                    for ii, it in enumerate(items):
                        kb_t, kb_id, vb_t, vb_id, kl, kk, qlo, diag = it
                        nqc = n - qlo
                        ps_t, ps_id = nxt
                        if ii + 1 < len(items):
                            nxt = score(items[ii + 1])
                        pt_t, pt_id = ptb.next()
                        S.add("act", lambda e, pt_t=pt_t, ps_t=ps_t, kk=kk, nqc=nqc: e.activation(
                            out=pt_t[0:kk, :, 0:nqc], in_=ps_t[0:kk, :, 0:nqc], func=AF.Exp, scale=0.125),
                            reads=[ps_id], writes=[pt_id])
                        if diag:
                            S.add("pool", lambda e, pt_t=pt_t: e.memset(pt_t[64:128, :, 0:64], 0.0),
                                  reads=[pt_id], writes=[pt_id])
                        for q in range(qlo // 128, nqb):
                            for c in range(2):
                                S.add("pe", lambda e, pt_t=pt_t, vb_t=vb_t, q=q, c=c, kk=kk, kl=kl, qlo=qlo, qn=qbn[q]: e.matmul(
                                    pacc_t[0:qn, q, c * 129:(c + 1) * 129], lhsT=pt_t[0:kk, c, 128 * q - qlo:128 * q - qlo + qn],
                                    rhs=vb_t[0:kk, kl, 0:129], start=False, stop=True, skip_group_check=True),
                                    reads=[pt_id, vb_id], writes=[("pacc", q)])
                    for q in range(nqb):
                        nq = qbn[q]
                        o_t, o_id = ob[q]
                        ar_t, ar_id = arec.next()
                        acc = pacc_t[0:nq, q, 0:258].rearrange("p (c x) -> p c x", c=2)
                        S.add("dve", lambda e, ar_t=ar_t, acc=acc, nq=nq: e.reciprocal(out=ar_t[0:nq, 0:2], in_=acc[:, :, 128]),
                              reads=[("pacc", q)], writes=[ar_id])
                        S.add("dve", lambda e, ar_t=ar_t, nq=nq: e.tensor_tensor(out=ar_t[0:nq, 2:3], in0=ar_t[0:nq, 1:2],
                                                                              in1=neglam[0:nq, l:l + 1], op=ALU.mult),
                              reads=[ar_id, ("neglam", l)], writes=[ar_id])
                        at_t, at_id = atmp.next()
                        S.add("dve", lambda e, at_t=at_t, acc=acc, ar_t=ar_t, nq=nq: e.tensor_scalar(
                            out=at_t[0:nq, :], in0=acc[:, 1, 0:128], scalar1=ar_t[0:nq, 2:3], scalar2=None, op0=ALU.mult),
                            reads=[("pacc", q), ar_id], writes=[at_id])
                        S.add("dve", lambda e, o_t=o_t, at_t=at_t, acc=acc, ar_t=ar_t, nq=nq, h=h: e.scalar_tensor_tensor(
                            out=o_t[0:nq, h, :], in0=acc[:, 0, 0:128], scalar=ar_t[0:nq, 0:1], in1=at_t[0:nq, :],
                            op0=ALU.mult, op1=ALU.add), reads=[("pacc", q), ar_id, at_id], writes=[(o_id, h)])
                def attn_post():
                    sblocks = [bb for bb in blocks if bb["sg"] is sg]
                    for q in range(nqb):
                        nq = qbn[q]
                        o_t, o_id = ob[q]
                        bb = sblocks[q]
                        oids = [(o_id, h) for h in range(NH)]
                        s8_t, s8_id = st8.next()
                        osq_t, osq_id = rot.next()
                        osq = osq_t[:].rearrange("p (h x) -> p h x", h=8)
                        S.add("pool", lambda e, o_t=o_t, nq=nq, osq=osq: e.tensor_tensor(out=osq[0:nq], in0=o_t[0:nq], in1=o_t[0:nq], op=ALU.mult),
                              reads=oids, writes=[osq_id])
                        S.add("dve", lambda e, s8_t=s8_t, nq=nq, osq=osq: e.tensor_reduce(out=s8_t[0:nq, :], in_=osq[0:nq], axis=AX.X, op=ALU.add),
                              reads=[osq_id], writes=[s8_id])
                        S.add("act", lambda e, s8_t=s8_t, nq=nq: e.activation(out=s8_t[0:nq, :], in_=s8_t[0:nq, :], func=AF.Ln,
                                                                            scale=1.0 / 128, bias=EPS), reads=[s8_id], writes=[s8_id])
                        S.add("act", lambda e, s8_t=s8_t, nq=nq: e.activation(out=s8_t[0:nq, :], in_=s8_t[0:nq, :], func=AF.Exp, scale=-0.5),
                              reads=[s8_id], writes=[s8_id])
                        S.add("dve", lambda e, o_t=o_t, s8_t=s8_t, nq=nq: e.tensor_tensor(
                            out=o_t[0:nq], in0=o_t[0:nq], in1=s8_t[0:nq, :].unsqueeze(2).to_broadcast([nq, 8, 128]), op=ALU.mult),
                            reads=oids + [s8_id], writes=oids)
                        oa_t, oa_id = oab.next()
                        S.add("pool", lambda e, oa_t=oa_t, o_t=o_t, bb=bb, nq=nq: e.tensor_tensor(
                            out=oa_t[0:nq, :], in0=o_t[0:nq].rearrange("p h x -> p (h x)"), in1=bb["ga"][0:nq, :], op=ALU.mult),
                            reads=oids + [(bb["gaid"], 6), (bb["gaid"], 7)], writes=[oa_id])
                        p, pid = pg.next()
                        for j in range(8):
                            S.add("pe", lambda e, p=p, oa_t=oa_t, j=j, nq=nq: e.transpose(
                                out=pgT(p)[:, j, 0:nq], in_=oa_t[0:nq, j * 128:(j + 1) * 128], identity=ident[0:nq, 0:nq]),
                                reads=[oa_id, "ident"], writes=[pid])
                        S.add("act", lambda e, p=p, bb=bb, nq=nq: e.copy(out=oaT_t[:, :, bb["col"]:bb["col"] + nq], in_=pgT(p)[:, :, 0:nq]),
                              reads=[pid], writes=[(oaT_id, bb["col"])])

            orT_t, orT_id = orT.next()
            if True:
                kind, n, col0, xb = sg["kind"], sg["n"], sg["col0"], sg["xb"]
                ck = "p" if kind in ("m", "p") else kind
                hc = hcar[ck]
                if kind == "m":
                    S.add("pool", lambda e, xb=xb: e.memset(xr_t[:, :, xb:xb + 3], 0.0), writes=[(xr_id, "halo", xb)])
                    S.add("pool", lambda e, hc=hc: e.memset(hc[:], 0.0), writes=[("hcar", ck)])
                elif kind == "p":
                    S.add("pool", lambda e, xb=xb: e.tensor_copy(out=xr_t[:, :, xb:xb + 3], in_=xhalo[:]),
                          reads=["xhalo"], writes=[(xr_id, "halo", xb)])
                else:
                    s = int(kind[1])
                    for k3 in range(3):
                        S.add("sp", lambda e, xb=xb, s=s, k3=k3: e.dma_start(out=xr_t[:, :, xb + k3],
                                                                           in_=st_conv[l, s, k3].rearrange("(j p) -> p j", p=128),
                                                                           allow_slow_non_contiguous=True),
                              writes=[(xr_id, "halo", xb)], dma=(xr_id, "halo", xb))
                    S.add("sp", lambda e, s=s, hc=hc: e.dma_start(out=hc[:], in_=st_rnn[l, s].rearrange("(j p) -> p j", p=128),
                                                                allow_slow_non_contiguous=True),
                          writes=[("hcar", ck)], dma=("hcar", ck))
                rnn_st = {}

                def rnn_conv(j):
                    xc_t, xc_id = rn["xc"].next()
                    xin = [(xr_id, j), (xr_id, "halo", xb)]
                    S.add("dve", lambda e, xc_t=xc_t, j=j, xb=xb, n=n: e.tensor_scalar(
                        out=xc_t[:, 0:n], in0=xr_t[:, j, xb:xb + n], scalar1=cwt[:, l, 0, j:j + 1], scalar2=cbt[:, l, j:j + 1],
                        op0=ALU.mult, op1=ALU.add), reads=xin + ["cwt", "cbt"], writes=[xc_id])
                    for k in range(1, 4):
                        S.add("dve", lambda e, xc_t=xc_t, j=j, xb=xb, n=n, k=k: e.scalar_tensor_tensor(
                            out=xc_t[:, 0:n], in0=xr_t[:, j, xb + k:xb + k + n], scalar=cwt[:, l, k, j:j + 1], in1=xc_t[:, 0:n],
                            op0=ALU.mult, op1=ALU.add), reads=xin + ["cwt", xc_id], writes=[xc_id])
                    xcb_t, xcb_id = xcb.next()
                    S.add("pool", lambda e, xcb_t=xcb_t, xc_t=xc_t, n=n: e.tensor_copy(out=xcb_t[:, 0:n], in_=xc_t[:, 0:n]),
                          reads=[xc_id], writes=[xcb_id])
                    rnn_st[j] = (xc_t, xc_id, xcb_t, xcb_id)

                def rnn_chunk(j):
                    xc_t, xc_id, xcb_t, xcb_id = rnn_st[j]
                    gates = []
                    for gi, nb_ in ((0, nbrg), (1, nbig)):
                        p, pid = pg.next()
                        S.add("pe", lambda e, p=p, xcb_t=xcb_t, gi=gi, j=j, n=n: e.matmul(
                            p[:, 0:n], lhsT=wg[:, gi, j, :], rhs=xcb_t[:, 0:n], start=True, stop=True),
                            reads=[xcb_id, ("wg", gi)], writes=[pid])
                        g_t, g_id = rn["r" if gi == 0 else "i"].next()
                        S.add("act", lambda e, g_t=g_t, p=p, nb_=nb_, j=j, n=n: e.activation(
                            out=g_t[:, 0:n], in_=p[:, 0:n], func=AF.Exp, scale=-1.0, bias=nb_[:, l, j:j + 1]),
                            reads=[pid, "nbrg", "nbig"], writes=[g_id])
                        S.add("dve", lambda e, g_t=g_t, n=n: e.tensor_scalar(out=g_t[:, 0:n], in0=g_t[:, 0:n], scalar1=1.0, scalar2=None,
                                                                           op0=ALU.add), reads=[g_id], writes=[g_id])
                        S.add("dve", lambda e, g_t=g_t, n=n: e.reciprocal(out=g_t[:, 0:n], in_=g_t[:, 0:n]), reads=[g_id], writes=[g_id])
                        gates.append((g_t, g_id))
                    (r_t, r_id), (i_t, i_id) = gates
                    a_t, a_id = rn["a"].next()
                    m_t, m_id = rn["m"].next()
                    S.add("act", lambda e, a_t=a_t, r_t=r_t, j=j, n=n: e.activation(out=a_t[:, 0:n], in_=r_t[:, 0:n], func=AF.Exp,
                                                                                 scale=cch[:, l, j:j + 1]), reads=[r_id, "cch"], writes=[a_id])
                    S.add("act", lambda e, a_t=a_t, m_t=m_t, n=n: e.activation(out=m_t[:, 0:n], in_=a_t[:, 0:n], func=AF.Square),
                          reads=[a_id], writes=[m_id])
                    S.add("act", lambda e, m_t=m_t, n=n: e.activation(out=m_t[:, 0:n], in_=m_t[:, 0:n], func=AF.Ln, scale=-1.0, bias=1.0),
                          reads=[m_id], writes=[m_id])
                    S.add("act", lambda e, m_t=m_t, n=n: e.activation(out=m_t[:, 0:n], in_=m_t[:, 0:n], func=AF.Exp, scale=0.5),
                          reads=[m_id], writes=[m_id])
                    S.add("dve", lambda e, m_t=m_t, i_t=i_t, n=n: e.tensor_tensor(out=m_t[:, 0:n], in0=m_t[:, 0:n], in1=i_t[:, 0:n], op=ALU.mult),
                          reads=[m_id, i_id], writes=[m_id])
                    S.add("dve", lambda e, m_t=m_t, xc_t=xc_t, n=n: e.tensor_tensor(out=m_t[:, 0:n], in0=m_t[:, 0:n], in1=xc_t[:, 0:n], op=ALU.mult),
                          reads=[m_id, xc_id], writes=[m_id])
                    hs_t, hs_id = rn["hs"].next()
                    S.add("dve", lambda e, hs_t=hs_t, a_t=a_t, m_t=m_t, j=j, n=n, hc=hc: e.tensor_tensor_scan(
                        out=hs_t[:, 0:n], data0=a_t[:, 0:n], data1=m_t[:, 0:n], initial=hc[:, j:j + 1], op0=ALU.mult, op1=ALU.add),
                        reads=[a_id, m_id, ("hcar", ck)], writes=[hs_id])
                    S.add("pool", lambda e, hs_t=hs_t, j=j, n=n, hc=hc: e.tensor_copy(out=hc[:, j:j + 1], in_=hs_t[:, n - 1:n]),
                          reads=[hs_id], writes=[("hcar", ck)])
                    S.add("dve", lambda e, hs_t=hs_t, j=j, n=n, col0=col0: e.tensor_tensor(
                        out=orT_t[:, j, col0:col0 + n], in0=hs_t[:, 0:n], in1=szr_t[:, j, col0:col0 + n], op=ALU.mult),
                        reads=[hs_id, (szr_id, j)], writes=[(orT_id, j)])
                def rnn_post():
                    xall = [(xr_id, j) for j in range(8)] + [(xr_id, "halo", xb)]
                    if kind in ("m", "p") and not sg["last"]:
                        S.add("pool", lambda e, xb=xb, n=n: e.tensor_copy(out=xhalo[:], in_=xr_t[:, :, xb + n:xb + n + 3]),
                              reads=xall, writes=["xhalo"])
                    if sg["last"]:
                        if kind == "p":
                            cdst, rdst = nc_p[l], nr_p[l]
                        else:
                            s = int(kind[1])
                            cdst, rdst = nc_s[l, s], nr_s[l, s]
                        for k3 in range(3):
                            S.add("pool", lambda e, cdst=cdst, xb=xb, n=n, k3=k3: e.dma_start(
                                out=cdst[k3].rearrange("(j p) -> p j", p=128), in_=xr_t[:, :, xb + n + k3], allow_slow_non_contiguous=True),
                                reads=xall, writes=[("out_c", l, kind, k3)], dma=(xr_id, "o"))
                        S.add("pool", lambda e, rdst=rdst, hc=hc: e.dma_start(
                            out=rdst.rearrange("(j p) -> p j", p=128), in_=hc[:], allow_slow_non_contiguous=True),
                            reads=[("hcar", ck)], writes=[("out_r", l, kind)], dma=("hcar", ck, "o"))

            for h in range(NH):
                rnn_conv(h)
                attn_head(h)
                rnn_chunk(h)
            attn_post()
            rnn_post()
            oaT_ids = [(oaT_id, bb["col"]) for bb in blocks]
            orT_ids = [(orT_id, j) for j in range(8)]

            if DBG: print('MARK', segs[0]['kind'], segs[0]['t0'], '# ---- step 6:', len(S.ops))
            mT_t, mT_id = mT.next()
            for hf in range(2):
                wr_t, wr_id = load_w(wsc_p[l, 0, hf], ("wscp", l, 0, hf))
                wa_t, wa_id = load_w(wsc_p[l, 1, hf], ("wscp", l, 1, hf))
                for c4 in range(4):
                    j = hf * 4 + c4
                    pr_, prid = pg.next()
                    for kc in range(8):
                        S.add("pe", lambda e, pr_=pr_, wr_t=wr_t, kc=kc, c4=c4: e.matmul(
                            pr_[:, 0:ncols], lhsT=wr_t[:, kc, c4 * 128:(c4 + 1) * 128], rhs=orT_t[:, kc, 0:ncols],
                            start=(kc == 0), stop=(kc == 7)), reads=orT_ids + [wr_id], writes=[prid])
                    pa_, paid = pg.next()
                    for kc in range(8):
                        S.add("pe", lambda e, pa_=pa_, wa_t=wa_t, kc=kc, c4=c4: e.matmul(
                            pa_[:, 0:ncols], lhsT=wa_t[:, kc, c4 * 128:(c4 + 1) * 128], rhs=oaT_t[:, kc, 0:ncols],
                            start=(kc == 0), stop=(kc == 7)), reads=oaT_ids + [wa_id], writes=[paid])
                    mt_t, mt_id = mtmp.next()
                    S.add("dve", lambda e, mt_t=mt_t, pr_=pr_, j=j: e.tensor_tensor(out=mt_t[:, 0, 0:ncols], in0=pr_[:, 0:ncols],
                                                                                 in1=sgr_t[:, j, 0:ncols], op=ALU.mult),
                          reads=[prid, (sgr_id, j)], writes=[(mt_id, 0)])
                    S.add("dve", lambda e, mt_t=mt_t, pa_=pa_, j=j: e.tensor_tensor(out=mt_t[:, 1, 0:ncols], in0=pa_[:, 0:ncols],
                                                                                 in1=sga_t[:, j, 0:ncols], op=ALU.mult),
                          reads=[paid, (sga_id, j)], writes=[(mt_id, 1)])
                    S.add("pool", lambda e, mt_t=mt_t, j=j: e.tensor_tensor(out=mT_t[:, j, 0:ncols], in0=mt_t[:, 0, 0:ncols],
                                                                          in1=mt_t[:, 1, 0:ncols], op=ALU.add),
                          reads=[(mt_id, 0), (mt_id, 1)], writes=[(mT_id, j)])
            mT_ids = [(mT_id, j) for j in range(8)]
            for hf in range(2):
                wo_t, wo_id = load_w(wsc_p[l, 2, hf], ("wscp", l, 2, hf))
                for b in blocks:
                    bn = b["bn"]
                    p, pid = pg.next()
                    for kc in range(8):
                        S.add("pe", lambda e, p=p, wo_t=wo_t, kc=kc, b=b, bn=bn: e.matmul(
                            p[0:bn, :], lhsT=mT_t[:, kc, b["col"]:b["col"] + bn], rhs=wo_t[:, kc, :],
                            start=(kc == 0), stop=(kc == 7)), reads=mT_ids + [wo_id], writes=[pid])
                    S.add("dve", lambda e, p=p, b=b, bn=bn, hf=hf: e.tensor_tensor(
                        out=b["h"][0:bn, hf * 512:(hf + 1) * 512], in0=b["h"][0:bn, hf * 512:(hf + 1) * 512], in1=p[0:bn, :], op=ALU.add),
                        reads=[pid, b["hid"]], writes=[b["hid"]])
            for b in blocks:
                sg, bn = b["sg"], b["bn"]
                kind = sg["kind"]
                r0 = sg["t0"] + b["bo"]
                if not last_layer:
                    dst = hb_p[r0:r0 + bn, :] if kind in ("m", "p") else hb_s[int(kind[1]), r0:r0 + bn, :]
                    S.add("pool", lambda e, dst=dst, b=b, bn=bn: e.dma_start(out=dst, in_=b["h"][0:bn, :]),
                          reads=[b["hid"]], writes=[b["hsrc"]], dma=(b["hid"], "o"))
                elif kind != "m":
                    jk_t, jk_id = xnb.next()
                    st, sid = rmsnorm_stats(b["h"][0:bn, :], [b["hid"]], bn, float(D), jk_t[0:bn, :], jk_id)
                    y_t, y_id = yst.next()
                    S.add("dve", lambda e, y_t=y_t, b=b, st=st, bn=bn: e.scalar_tensor_tensor(
                        out=y_t[0:bn, :], in0=b["h"][0:bn, :], scalar=st[0:bn, 1:2], in1=fnwt[0:bn, :], op0=ALU.mult, op1=ALU.mult),
                        reads=[b["hid"], sid, "fnwt"], writes=[y_id])
                    dst = y_p[r0 - NMETA:r0 - NMETA + bn, :] if kind == "p" else y_s[int(kind[1]), r0:r0 + bn, :]
                    S.add("pool", lambda e, dst=dst, y_t=y_t, bn=bn: e.dma_start(out=dst, in_=y_t[0:bn, :]),
                          reads=[y_id], writes=[("out_y", kind, r0)], dma=y_id)

        for l in range(depth):
            if l + 1 < depth:
                convert_weights(l + 1)
            load_gate_w(l)
            if not NOCACHE:
                convert_cache(l)
            do_tile(l, [dict(kind="s0", col0=0, n=64, t0=0, last=True)])
            do_tile(l, [dict(kind="s1", col0=0, n=64, t0=0, last=True)])
            do_tile(l, [dict(kind="m", col0=0, n=16, t0=0, last=False)])
            for ti in range(ntile):
                do_tile(l, [dict(kind="p", col0=0, n=TT, t0=NMETA + ti * TT, last=(ti == ntile - 1))])

        S.finalize()
        print("ops", len(S.ops), "sems", S.n_sems)
        with nc.allow_low_precision(reason="bf16 matmul operands by design"), nc.Block() as block:
            S.emit(block)
    return nc


_ROPE = None


def _rope_table():
    global _ROPE
    if _ROPE is None:
        half = 32
        inv = 1.0 / (10000.0 ** (np.arange(half, dtype=np.float32) / np.float32(half)))
        pos = np.concatenate([np.arange(TP), NMETA + PAST + np.arange(DSEQ)]).astype(np.float32)
        ang = pos[:, None] * inv[None, :].astype(np.float32)
        _ROPE = np.ascontiguousarray(np.stack([np.cos(ang), np.sin(ang)], axis=1).astype(np.float32))
    return _ROPE


def kernel(x_prompt, x_sample, cache_k, cache_v, state_conv, state_rnn, meta_tokens,
           norm_w, w_in, conv_w, conv_b, w_rg, b_rg, w_ig, b_ig, lru_lambda,
           lambda_q1, lambda_k1, lambda_q2, lambda_k2, subln_w,
           w_proj_rnn, w_proj_att, w_out, final_norm_w):
    A = lambda a: np.ascontiguousarray(np.asarray(a, dtype=np.float32))
    nc = build_program()
    shared = dict(meta=A(meta_tokens), norm_w=A(norm_w), w_in=A(w_in), conv_w=A(conv_w), conv_b=A(conv_b),
                  w_rg=A(w_rg), b_rg=A(b_rg), w_ig=A(w_ig), b_ig=A(b_ig), lru_l=A(lru_lambda),
                  lq1=A(lambda_q1), lk1=A(lambda_k1), lq2=A(lambda_q2), lk2=A(lambda_k2), subln=A(subln_w),
                  w_pr=A(w_proj_rnn), w_pa=A(w_proj_att), w_out=A(w_out), fnw=A(final_norm_w), rope=_rope_table())
    x_prompt, x_sample = np.asarray(x_prompt), np.asarray(x_sample)
    cache_k, cache_v = np.asarray(cache_k), np.asarray(cache_v)
    state_conv, state_rnn = np.asarray(state_conv), np.asarray(state_rnn)
    in_maps = []
    for c in range(8):
        m = dict(shared)
        m["x_p"] = A(x_prompt[c])
        m["x_s"] = A(x_sample[2 * c:2 * c + 2])
        m["cache_k"] = A(cache_k[:, 2 * c:2 * c + 2].reshape(DEPTH, 2, PAST, D))
        m["cache_v"] = A(cache_v[:, 2 * c:2 * c + 2].reshape(DEPTH, 2, PAST, D))
        m["st_conv"] = A(state_conv[:, 2 * c:2 * c + 2])
        m["st_rnn"] = A(state_rnn[:, 2 * c:2 * c + 2])
        in_maps.append(m)
    res = run_bass_kernel_spmd(nc, in_maps, core_ids=list(range(8)))
    R = res.results
    y_p = np.stack([R[c]["y_p"] for c in range(8)])
    y_s = np.concatenate([R[c]["y_s"] for c in range(8)], axis=0)
    nk_p = np.stack([R[c]["nk_p"] for c in range(8)], axis=1).reshape(DEPTH, 8, TP, NH, 2, 64)
    nv_p = np.stack([R[c]["nv_p"] for c in range(8)], axis=1).reshape(DEPTH, 8, TP, NH, 128)
    nc_p = np.stack([R[c]["nc_p"] for c in range(8)], axis=1)
    nr_p = np.stack([R[c]["nr_p"] for c in range(8)], axis=1)
    nk_s = np.concatenate([R[c]["nk_s"] for c in range(8)], axis=1).reshape(DEPTH, 16, DSEQ, NH, 2, 64)
    nv_s = np.concatenate([R[c]["nv_s"] for c in range(8)], axis=1).reshape(DEPTH, 16, DSEQ, NH, 128)
    nc_s = np.concatenate([R[c]["nc_s"] for c in range(8)], axis=1)
    nr_s = np.concatenate([R[c]["nr_s"] for c in range(8)], axis=1)
    return (y_p, y_s, nk_p, nv_p, nc_p, nr_p, nk_s, nv_s, nc_s, nr_s)
```
